# Optimizing a Trainium2 kernel written in Bass

```python
import math
import jax, jax.numpy as jnp
from jax import lax
import numpy as np

D_MODEL = 1024
BATCH = 4
SEQ = 4096
DEPTH = 1
DEC_BATCH = 32
DEC_SEQ = 1
PAST_LEN = 16384
PAGE_SIZE = 128

MIX_WIDTH = D_MODEL
HG_WIDTH = MIX_WIDTH // 2
HG_EXPAND = 128
HG_HEADS = HG_WIDTH // HG_EXPAND
HG_VDIM = HG_WIDTH // HG_HEADS
HG_CHUNK = 64
DA_WIDTH = MIX_WIDTH - HG_WIDTH
DA_HEAD_DIM = 64
DA_HEADS = DA_WIDTH // (2 * DA_HEAD_DIM)
DA_VDIM = 2 * DA_HEAD_DIM
DA_SCALE = DA_HEAD_DIM ** -0.5
Q_BLOCK = 128
D_FF = 2816
NORM_EPS = 1e-6
NEG_INF = -1e30
IN_SPLITS = (HG_WIDTH, 2 * HG_WIDTH, 3 * HG_WIDTH, 4 * HG_WIDTH,
             4 * HG_WIDTH + DA_WIDTH, 4 * HG_WIDTH + 2 * DA_WIDTH)
IN_COLS = 4 * HG_WIDTH + 3 * DA_WIDTH

kernel_name = 'hymba_hgrn2_diffattn_macaron_step'


def rmsnorm(x, g):
    xf = x.astype(jnp.float32)
    y = xf * lax.rsqrt(jnp.mean(xf * xf, axis=-1, keepdims=True) + NORM_EPS)
    return (y * g.astype(jnp.float32)).astype(x.dtype)


def swiglu(h, w_gate, w_up, w_down):
    return (jax.nn.silu(h @ w_gate) * (h @ w_up)) @ w_down


def alibi_slopes(n_heads):
    return jnp.asarray(np.power(2.0, -8.0 * np.arange(1, n_heads + 1) / n_heads), dtype=jnp.float32)


def hgrn2_chunked(q, k, v, logf, s0):
    B, L, H, DK = q.shape
    C = HG_CHUNK if L % HG_CHUNK == 0 else L
    n = L // C

    def to_chunks(a):
        return a.reshape(B, n, C, H, a.shape[-1]).transpose(1, 0, 3, 2, 4)

    qc, kc, vc, gc = to_chunks(q), to_chunks(k), to_chunks(v), to_chunks(logf)
    causal = jnp.tril(jnp.ones((C, C), dtype=bool))

    def step(S, inp):
        qb, kb, vb, gb = inp
        b = jnp.cumsum(gb, axis=2)
        diff = b[:, :, :, None, :] - b[:, :, None, :, :]
        decay = jnp.where(causal[:, :, None], jnp.exp(jnp.minimum(diff, 0.0)), 0.0)
        A = jnp.einsum('bhtd,bhsd,bhtsd->bhts', qb, kb, decay)
        o = (jnp.einsum('bhts,bhsv->bhtv', A, vb)
             + jnp.einsum('bhtd,bhdv->bhtv', qb * jnp.exp(b), S))
        b_last = b[:, :, -1:, :]
        S_new = (jnp.exp(b_last[:, :, 0, :])[..., None] * S
                 + jnp.einsum('bhsd,bhsv->bhdv', kb * jnp.exp(b_last - b), vb))
        return S_new, o

    S_fin, oc = lax.scan(step, s0, (qc, kc, vc, gc))
    o = oc.transpose(1, 0, 3, 2, 4).reshape(B, L, H, v.shape[-1])
    return o, S_fin


def hgrn2_mixer(xq, xf, xi, xg, lb, s0, g_out):
    B, L, _ = xq.shape

    def heads(a):
        return a.astype(jnp.float32).reshape(B, L, HG_HEADS, -1)

    q = jax.nn.silu(heads(xq)) * (HG_EXPAND ** -0.5)
    f = lb + (1.0 - lb) * jax.nn.sigmoid(heads(xf))
    i = heads(xi)
    o, s_new = hgrn2_chunked(q, 1.0 - f, i, jnp.log(f), s0.astype(jnp.float32))
    o = rmsnorm(o, g_out) * jax.nn.silu(heads(xg))
    return o, s_new


def diff_attention(q, q_pos, segments, lam):
    B, Lq = q.shape[0], q.shape[1]
    slopes = jnp.repeat(alibi_slopes(DA_HEADS), 2)[:, None, None]
    qf = q.astype(jnp.float32)
    scores = []
    lens = []
    for k, _, k_pos in segments:
        s = jnp.einsum('bqnd,bknd->bnqk', qf, k.astype(jnp.float32)) * DA_SCALE
        rel = (q_pos[:, None] - k_pos[None, :]).astype(jnp.float32)
        scores.append(jnp.where(rel >= 0, s - slopes * rel, NEG_INF))
        lens.append(k.shape[1])
    p = jax.nn.softmax(jnp.concatenate(scores, axis=-1), axis=-1)
    p = p.reshape(B, DA_HEADS, 2, Lq, p.shape[-1])
    a = p[:, :, 0] - lam * p[:, :, 1]
    outs = []
    start = 0
    for (_, v, _), n in zip(segments, lens):
        outs.append(jnp.einsum('bhqk,bkhv->bqhv', a[..., start:start + n], v.astype(jnp.float32)))
        start += n
    out = outs[0]
    for extra in outs[1:]:
        out = out + extra
    return out


def prompt_attention(q, k, v, lam):
    B, L = q.shape[0], q.shape[1]
    blk = Q_BLOCK if L % Q_BLOCK == 0 else L
    nb = L // blk
    k_pos = jnp.arange(L, dtype=jnp.int32)
    qb = q.reshape(B, nb, blk, 2 * DA_HEADS, DA_HEAD_DIM).transpose(1, 0, 2, 3, 4)

    def one_block(args):
        q_blk, b_idx = args
        q_pos = b_idx * blk + jnp.arange(blk, dtype=jnp.int32)
        return diff_attention(q_blk, q_pos, ((k, v, k_pos),), lam)

    o = lax.map(one_block, (qb, jnp.arange(nb, dtype=jnp.int32)))
    return o.transpose(1, 0, 2, 3, 4).reshape(B, L, DA_HEADS, DA_VDIM)


def sample_attention(q, k, v, past_k, past_v, lam):
    L = q.shape[1]
    past_len = past_k.shape[1]
    past_pos = jnp.arange(past_len, dtype=jnp.int32)
    new_pos = past_len + jnp.arange(L, dtype=jnp.int32)
    return diff_attention(q, new_pos, ((past_k, past_v, past_pos), (k, v, new_pos)), lam)


def decoder_layer(x, s0, past_k, past_v, lb, lam, lam_init, p):
    B, L, _ = x.shape
    dt = x.dtype
    h = x + 0.5 * swiglu(rmsnorm(x, p['ffn1_norm']), p['ffn1_w_gate'], p['ffn1_w_up'], p['ffn1_w_down'])
    z = rmsnorm(h, p['mix_norm']) @ p['w_in']
    xq, xf, xi, xg, dq, dk, dv = jnp.split(z, IN_SPLITS, axis=-1)
    o_hg, s_new = hgrn2_mixer(xq, xf, xi, xg, lb, s0, p['hg_out_norm'])
    q = rmsnorm(dq.reshape(B, L, 2 * DA_HEADS, DA_HEAD_DIM), p['da_q_norm'])
    k = rmsnorm(dk.reshape(B, L, 2 * DA_HEADS, DA_HEAD_DIM), p['da_k_norm'])
    v = dv.reshape(B, L, DA_HEADS, DA_VDIM)
    if past_k is None:
        o_da = prompt_attention(q, k, v, lam)
    else:
        o_da = sample_attention(q, k, v, past_k, past_v, lam)
    o_da = rmsnorm(o_da, p['da_subln']) * (1.0 - lam_init)
    o = jnp.concatenate([o_hg.reshape(B, L, HG_WIDTH), o_da.reshape(B, L, DA_WIDTH)], axis=-1).astype(dt)
    h = h + o @ p['w_out']
    y = h + 0.5 * swiglu(rmsnorm(h, p['ffn2_norm']), p['ffn2_w_gate'], p['ffn2_w_up'], p['ffn2_w_down'])
    return y, k, v, s_new


def setup_inputs(seed: int = 0) -> dict:
    key = jax.random.key(seed)
    ks = jax.random.split(key, 32)
    f32 = jnp.float32
    n_pages = PAST_LEN // PAGE_SIZE
    n_used = DEC_BATCH * n_pages
    n_pool = n_used + max(n_used // 4, 1)

    def nrm(k, shape, scale):
        return jax.random.normal(k, shape, f32) * scale

    def gain(k, shape):
        return 1.0 + 0.05 * jax.random.normal(k, shape, f32)

    page_table = jax.random.permutation(ks[5], n_pool)[:n_used].reshape(DEC_BATCH, n_pages).astype(jnp.int32)
    return {
        'x_prompt': nrm(ks[0], (BATCH, SEQ, D_MODEL), 1.0),
        'x_sample': nrm(ks[1], (DEC_BATCH, DEC_SEQ, D_MODEL), 1.0),
        'cache_k': nrm(ks[2], (DEPTH, n_pool, PAGE_SIZE, 2 * DA_HEADS, DA_HEAD_DIM), 1.0),
        'cache_v': nrm(ks[3], (DEPTH, n_pool, PAGE_SIZE, DA_HEADS, DA_VDIM), 1.0),
        'state_hgrn': nrm(ks[4], (DEPTH, DEC_BATCH, HG_HEADS, HG_EXPAND, HG_VDIM), 0.5),
        'page_table': page_table,
        'ffn1_norm': gain(ks[6], (DEPTH, D_MODEL)),
        'ffn1_w_gate': nrm(ks[7], (DEPTH, D_MODEL, D_FF), D_MODEL ** -0.5),
        'ffn1_w_up': nrm(ks[8], (DEPTH, D_MODEL, D_FF), D_MODEL ** -0.5),
        'ffn1_w_down': nrm(ks[9], (DEPTH, D_FF, D_MODEL), D_FF ** -0.5),
        'mix_norm': gain(ks[10], (DEPTH, D_MODEL)),
        'w_in': nrm(ks[11], (DEPTH, D_MODEL, IN_COLS), D_MODEL ** -0.5),
        'hg_lb_logits': nrm(ks[12], (DEPTH + 1, HG_WIDTH), 0.5),
        'hg_out_norm': gain(ks[13], (DEPTH, HG_VDIM)),
        'da_q_norm': gain(ks[14], (DEPTH, DA_HEAD_DIM)),
        'da_k_norm': gain(ks[15], (DEPTH, DA_HEAD_DIM)),
        'da_lambda_q1': nrm(ks[16], (DEPTH, DA_HEAD_DIM), 0.1),
        'da_lambda_k1': nrm(ks[17], (DEPTH, DA_HEAD_DIM), 0.1),
        'da_lambda_q2': nrm(ks[18], (DEPTH, DA_HEAD_DIM), 0.1),
        'da_lambda_k2': nrm(ks[19], (DEPTH, DA_HEAD_DIM), 0.1),
        'da_subln': gain(ks[20], (DEPTH, DA_VDIM)),
        'w_out': nrm(ks[21], (DEPTH, MIX_WIDTH, D_MODEL), MIX_WIDTH ** -0.5),
        'ffn2_norm': gain(ks[22], (DEPTH, D_MODEL)),
        'ffn2_w_gate': nrm(ks[23], (DEPTH, D_MODEL, D_FF), D_MODEL ** -0.5),
        'ffn2_w_up': nrm(ks[24], (DEPTH, D_MODEL, D_FF), D_MODEL ** -0.5),
        'ffn2_w_down': nrm(ks[25], (DEPTH, D_FF, D_MODEL), D_FF ** -0.5),
    }


def reference(x_prompt, x_sample, cache_k, cache_v, state_hgrn, page_table,
              ffn1_norm, ffn1_w_gate, ffn1_w_up, ffn1_w_down, mix_norm, w_in,
              hg_lb_logits, hg_out_norm, da_q_norm, da_k_norm,
              da_lambda_q1, da_lambda_k1, da_lambda_q2, da_lambda_k2, da_subln, w_out,
              ffn2_norm, ffn2_w_gate, ffn2_w_up, ffn2_w_down):
    n_batch = x_prompt.shape[0]
    dec_batch = x_sample.shape[0]
    lb_all = jnp.cumsum(jax.nn.softmax(hg_lb_logits.astype(jnp.float32), axis=0), axis=0)
    yp, ys = x_prompt, x_sample
    kp_l, vp_l, sp_l, ks_l, vs_l, ss_l = [], [], [], [], [], []
    for l in range(DEPTH):
        p = {
            'ffn1_norm': ffn1_norm[l], 'ffn1_w_gate': ffn1_w_gate[l], 'ffn1_w_up': ffn1_w_up[l],
            'ffn1_w_down': ffn1_w_down[l], 'mix_norm': mix_norm[l], 'w_in': w_in[l],
            'hg_out_norm': hg_out_norm[l], 'da_q_norm': da_q_norm[l], 'da_k_norm': da_k_norm[l],
            'da_subln': da_subln[l], 'w_out': w_out[l], 'ffn2_norm': ffn2_norm[l],
            'ffn2_w_gate': ffn2_w_gate[l], 'ffn2_w_up': ffn2_w_up[l], 'ffn2_w_down': ffn2_w_down[l],
        }
        lam_init = 0.8 - 0.6 * math.exp(-0.3 * l)
        lam = (jnp.exp(jnp.sum(da_lambda_q1[l].astype(jnp.float32) * da_lambda_k1[l].astype(jnp.float32)))
               - jnp.exp(jnp.sum(da_lambda_q2[l].astype(jnp.float32) * da_lambda_k2[l].astype(jnp.float32)))
               + lam_init)
        lb = lb_all[l].reshape(HG_HEADS, HG_EXPAND)
        s0 = jnp.zeros((n_batch, HG_HEADS, HG_EXPAND, HG_VDIM), jnp.float32)
        yp, kp, vp, sp = decoder_layer(yp, s0, None, None, lb, lam, lam_init, p)
        past_k = cache_k[l][page_table].reshape(dec_batch, -1, 2 * DA_HEADS, DA_HEAD_DIM)
        past_v = cache_v[l][page_table].reshape(dec_batch, -1, DA_HEADS, DA_VDIM)
        ys, ksn, vsn, ssn = decoder_layer(ys, state_hgrn[l], past_k, past_v, lb, lam, lam_init, p)
        kp_l.append(kp); vp_l.append(vp); sp_l.append(sp)
        ks_l.append(ksn); vs_l.append(vsn); ss_l.append(ssn)
    k_prompt = jnp.stack(kp_l, axis=0)
    v_prompt = jnp.stack(vp_l, axis=0)
    state_prompt = jnp.stack(sp_l, axis=0)
    k_sample = jnp.stack(ks_l, axis=0)
    v_sample = jnp.stack(vs_l, axis=0)
    state_sample = jnp.stack(ss_l, axis=0)
    return (yp, ys, k_prompt, v_prompt, state_prompt, k_sample, v_sample, state_sample)
```

```python
import math
from contextlib import ExitStack
import numpy as np
import concourse.bass as bass
import concourse.mybir as mybir
from concourse.bass_utils import run_bass_kernel_spmd

F32 = mybir.dt.float32
BF16 = mybir.dt.bfloat16
I32 = mybir.dt.int32
AF = mybir.ActivationFunctionType
ALU = mybir.AluOpType
AX = mybir.AxisListType

D = 1024
DFF = 2816
NFC = DFF // 128
INC = 3584
EPS = 1e-6
DA_SCALE = 64 ** -0.5
NEG = -30000.0
SLOPES = [2.0 ** (-8.0 * (h + 1) / 4) for h in range(4)]
LAM_INIT = 0.8 - 0.6 * math.exp(0.0)
STAGE = 9
TB = 4
SKIP_A = False
SKIP_P1 = False
DA_SUB = 9


class Emitter:
    ENG = ('pe', 'act', 'dve', 'pool', 'sp')
    SAME_SYNC = ('act', 'dve', 'pool')
    W = 30000

    def __init__(self, nc, es, n_dma_slots=10, max_ops=200000):
        self.nc = nc
        self.es = es
        self.ops = {e: [] for e in self.ENG}
        self.cnt = {e: 0 for e in self.ENG}
        self.sems = {e: [] for e in self.ENG}
        self.dma_q = ('sp', 'act', 'pool')
        self.nslots = n_dma_slots
        self.dsem = {}
        self.dcnt = {}
        self.dnext = {q: 0 for q in self.dma_q}
        for q in self.dma_q:
            for i in range(n_dma_slots):
                self.dsem[(q, i)] = es.enter_context(nc.semaphore('ds_%s_%d' % (q, i)))
                self.dcnt[(q, i)] = 0
        self.waited = {e: {} for e in self.ENG}
        self.bufs = {}
        self.names = {}

    def _esem(self, e, k):
        i = (k - 1) // self.W
        while len(self.sems[e]) <= i:
            self.sems[e].append(self.es.enter_context(self.nc.semaphore('s_%s_%d' % (e, len(self.sems[e])))))
        return self.sems[e][i], (k - 1) % self.W + 1

    def _semval(self, src, val):
        if isinstance(src, str):
            return self._esem(src, val)
        return self.dsem[src], val

    def _deps(self, e, reads, writes):
        deps = []
        for b in reads:
            st = self.bufs.get(b)
            if st and st['w']:
                deps.append(st['w'])
        for b in writes:
            st = self.bufs.get(b)
            if st:
                if st['w']:
                    deps.append(st['w'])
                deps.extend(st['r'])
        need = {}
        for (src, val) in deps:
            if src == e and e not in self.SAME_SYNC:
                continue
            if self.waited[e].get(src, 0) >= val:
                continue
            need[src] = max(need.get(src, 0), val)
        for src, val in need.items():
            self.waited[e][src] = val
        return list(need.items())

    def _update(self, tok, reads, writes):
        for b in reads:
            st = self.bufs.setdefault(b, {'w': None, 'r': []})
            st['r'].append(tok)
            if len(st['r']) > 48:
                mx = {}
                for (s, v) in st['r']:
                    mx[s] = max(mx.get(s, 0), v)
                st['r'] = list(mx.items())
        for b in writes:
            self.bufs[b] = {'w': tok, 'r': []}

    def op(self, e, fn, reads=(), writes=()):
        waits = self._deps(e, reads, writes)
        self.cnt[e] += 1
        tok = (e, self.cnt[e])
        sem, _ = self._esem(e, self.cnt[e])
        wl = [self._semval(s, v) for s, v in waits]

        desc = (e, tuple(reads), tuple(writes))

        def emit(eng, fn=fn, wl=wl, sem=sem, desc=desc):
            for (s, v) in wl:
                eng.wait_ge(s, v)
            ins = fn(eng)
            try:
                self.names[ins.ins.name] = desc
            except Exception:
                pass
            ins.then_inc(sem, 1)
        self.ops[e].append(emit)
        self._update(tok, reads, writes)
        return tok

    def dma(self, q, fn, reads=(), writes=()):
        waits = self._deps(q, reads, writes)
        i = self.dnext[q]
        self.dnext[q] = (i + 1) % self.nslots
        key = (q, i)
        prev = self.dcnt[key]
        if prev > 0 and self.waited[q].get(key, 0) < prev:
            waits.append((key, prev))
            self.waited[q][key] = prev
        self.dcnt[key] = prev + 16
        tok = (key, prev + 16)
        sem = self.dsem[key]
        wl = [self._semval(s, v) for s, v in waits]

        def emit(eng, fn=fn, wl=wl, sem=sem):
            for (s, v) in wl:
                eng.wait_ge(s, v)
            fn(eng).then_inc(sem, 16)
        self.ops[q].append(emit)
        self._update(tok, reads, writes)
        return tok

    def barrier(self):
        targets = []
        for e in self.ENG:
            if self.cnt[e] > 0:
                targets.append((e, self.cnt[e]))
        for key, v in self.dcnt.items():
            if v > 0:
                targets.append((key, v))
        for e in self.ENG:
            wl = []
            for (src, val) in targets:
                if src == e and e not in self.SAME_SYNC:
                    continue
                if self.waited[e].get(src, 0) >= val:
                    continue
                self.waited[e][src] = val
                wl.append(self._semval(src, val))

            def emit(eng, wl=wl):
                for (s, v) in wl:
                    eng.wait_ge(s, v)
            self.ops[e].append(emit)
        self.bufs = {}

    def finish(self, block):
        self.barrier()
        ops = self.ops

        @block.tensor
        def _(eng):
            for f in ops['pe']:
                f(eng)

        @block.scalar
        def _(eng):
            for f in ops['act']:
                f(eng)

        @block.vector
        def _(eng):
            for f in ops['dve']:
                f(eng)

        @block.gpsimd
        def _(eng):
            for f in ops['pool']:
                f(eng)

        @block.sync
        def _(eng):
            for f in ops['sp']:
                f(eng)


def V3(ap, h):
    return ap.rearrange("p (h d) -> p h d", h=h)


def build_program(NBO, NBP, NPG, NPOOL, debug=False):
    NG = NPG // 16
    NT = NPG + 1
    NKB = NBP + NBO
    nc = bass.Bass("TRN2", target_bir_lowering=False)

    def din(name, shape, dt=F32):
        return nc.dram_tensor(name, shape, dt, kind="ExternalInput").ap()

    def dout(name, shape, dt=F32):
        return nc.dram_tensor(name, shape, dt, kind="ExternalOutput").ap()

    def dscr(name, shape, dt=F32):
        return nc.dram_tensor(name, shape, dt, kind=("ExternalOutput" if debug else "Internal")).ap()

    x_own = din("x_own", [NBO * 128, D]); x_pre = din("x_pre", [NBP * 128, D]); x_s = din("x_s", [128, D])
    cache_k = din("cache_k", [NPOOL * 128, 512]); cache_v = din("cache_v", [NPOOL * 128, 512])
    state_in = din("state_in", [16, 128, 128])
    ptrep = din("ptrep", [128, 4 * NG], I32)
    flags = din("flags", [128, 2])
    w_g1 = din("w_g1", [D, DFF]); w_u1 = din("w_u1", [D, DFF]); w_d1 = din("w_d1", [DFF, D])
    w_g2 = din("w_g2", [D, DFF]); w_u2 = din("w_u2", [D, DFF]); w_d2 = din("w_d2", [DFF, D])
    w_in = din("w_in", [D, INC]); w_out = din("w_out", [D, D])
    n_f1 = din("n_f1", [1, D]); n_mix = din("n_mix", [1, D]); n_f2 = din("n_f2", [1, D])
    lb_log = din("lb_log", [2, 512])
    g_hg = din("g_hg", [1, 128]); g_q = din("g_q", [1, 64]); g_k = din("g_k", [1, 64]); g_sub = din("g_sub", [1, 128])
    lam_p = din("lam_p", [4, 64])
    c_ident = din("c_ident", [128, 128]); c_maskT = din("c_maskT", [128, 512]); c_up = din("c_up", [128, 128])
    c_wm = din("c_wm", [128, 128]); c_sel = din("c_sel", [128, 4])
    c_eboff = din("c_eboff", [128, 512]); c_ebdiag = din("c_ebdiag", [128, 512])
    c_cb = din("c_cb", [128, 4 * (NKB + 1)])
    c_biasS = din("c_biasS", [128, NG * 16 * 8]); c_selfb = din("c_selfb", [128, 4])
    c_coff = din("c_coff", [128, 1], I32)
    c_rowsel = din("c_rowsel", [128, 4])
    c_csel = din("c_csel", [128, 2])

    y_own = dout("y_own", [NBO * 128, D]); y_s = dout("y_s", [128, D])
    k_own = dout("k_own", [NBO * 128, 512]); v_own = dout("v_own", [NBO * 128, 512])
    k_s = dout("k_s", [128, 512]); v_s = dout("v_s", [128, 512])
    state_p = dout("state_p", [4, 128, 128]); state_s = dout("state_s", [16, 128, 128])
    h_own = dscr("h_own", [NBO * 128, D]); h_pre = dscr("h_pre", [NBP * 128, D]); h_s = dscr("h_s", [128, D])
    h2_own = dscr("h2_own", [NBO * 128, D]); h2_s = dscr("h2_s", [128, D])

    with ExitStack() as es:
        em = Emitter(nc, es)

        def sb(st, name, shape, dt):
            return st.enter_context(nc.sbuf_tensor(name, shape, dt))

        def dma(q, out, in_, r, w):
            em.dma(q, lambda e: e.dma_start(out=out, in_=in_), r, w)

        def act(out, in_, func, r, w, **kw):
            em.op('act', lambda e: e.activation(out=out, in_=in_, func=func, **kw), r, w)

        def tt(eng, out, in0, in1, op, r, w):
            em.op(eng, lambda e: e.tensor_tensor(out=out, in0=in0, in1=in1, op=op), r, w)

        def ts(eng, out, in0, s1, s2, op0, op1, r, w):
            if s2 is None:
                em.op(eng, lambda e: e.tensor_scalar(out=out, in0=in0, scalar1=s1, scalar2=None, op0=op0), r, w)
            else:
                em.op(eng, lambda e: e.tensor_scalar(out=out, in0=in0, scalar1=s1, scalar2=s2, op0=op0, op1=op1), r, w)

        def stt(eng, out, in0, scalar, in1, op0, op1, r, w):
            em.op(eng, lambda e: e.scalar_tensor_tensor(out=out, in0=in0, scalar=scalar, in1=in1, op0=op0, op1=op1), r, w)

        def cp(eng, out, in_, r, w):
            em.op(eng, lambda e: e.tensor_copy(out=out, in_=in_), r, w)

        def mm(out, lhsT, rhs, start, stop, r, w):
            em.op('pe', lambda e: e.matmul(out, lhsT=lhsT, rhs=rhs, start=start, stop=stop), r, w)

        def tr(out, in_, idn, r, w):
            em.op('pe', lambda e: e.transpose(out=out, in_=in_, identity=idn), r, w)

        def memset(eng, ap, val, w):
            em.op(eng, lambda e: e.memset(ap, val), [], w)

        def red(out, in_, r, w):
            em.op('dve', lambda e: e.tensor_reduce(out=out, in_=in_, axis=AX.X, op=ALU.add), r, w)

        def recip(out, in_, r, w):
            em.op('dve', lambda e: e.reciprocal(out=out, in_=in_), r, w)

        PB = [es.enter_context(nc.psum_tensor("pb%d" % i, [128, 512], F32)) for i in range(8)]
        PBh = [p[:].bitcast(BF16) for p in PB]

        G = es
        identf = sb(G, "identf", [128, 128], F32); ident = sb(G, "ident", [128, 128], BF16)
        onesf = sb(G, "onesf", [128, 128], F32)
        flg = sb(G, "flg", [128, 2], F32)
        lam_t = sb(G, "lam_t", [128, 4], F32)
        dma('sp', identf[:], c_ident, [], ['identf'])
        cp('dve', ident[:], identf[:], ['identf'], ['ident'])
        memset('pool', onesf[:], 1.0, ['onesf'])
        dma('sp', flg[:], flags, [], ['flg'])
        with ExitStack() as t0:
            lp = sb(t0, "lp", [128, 4, 64], F32); lpr = sb(t0, "lpr", [128, 2, 64], F32); ls = sb(t0, "ls", [128, 2], F32)
            for i in range(4):
                dma('sp', lp[:, i, :], lam_p[i:i + 1, :].partition_broadcast(128), [], ['lp%d' % i])
            tt('dve', lpr[:, 0, :], lp[:, 0, :], lp[:, 1, :], ALU.mult, ['lp0', 'lp1'], ['lpr0'])
            tt('dve', lpr[:, 1, :], lp[:, 2, :], lp[:, 3, :], ALU.mult, ['lp2', 'lp3'], ['lpr1'])
            red(ls[:, 0:2], lpr[:], ['lpr0', 'lpr1'], ['ls'])
            act(ls[:], ls[:], AF.Exp, ['ls'], ['ls'])
            stt('dve', lam_t[:, 0:1], ls[:, 0:1], LAM_INIT, ls[:, 1:2], ALU.add, ALU.subtract, ['ls'], ['lam_t'])
            ts('dve', lam_t[:, 1:2], lam_t[:, 0:1], -1.0, None, ALU.mult, None, ['lam_t'], ['lam_t'])
            em.barrier()

        def rms_rstd(ss_ap, n, out_ap, key):
            act(out_ap, ss_ap, AF.Sqrt, [key], [key], scale=1.0 / n, bias=epsc[:, 0:1])
            recip(out_ap, out_ap, [key], [key])

        epsc = sb(G, "epsc", [128, 1], F32)
        memset('pool', epsc[:], EPS, ['epsc'])

        def ffn_phase(tag, tiles, wg_d, wu_d, wd_d, nvec_d):
            with ExitStack() as ph:
                Wg = sb(ph, "Wg" + tag, [128, 8, DFF], BF16)
                Wu = sb(ph, "Wu" + tag, [128, 8, DFF], BF16)
                Wd = sb(ph, "Wd" + tag, [128, NFC, D], BF16)
                gbc = sb(ph, "gbc" + tag, [128, D], F32)
                xt = [sb(ph, "xt%d" % i + tag, [128, TB, D], F32) for i in range(2)]
                xs = [sb(ph, "xs%d" % i + tag, [128, D], BF16) for i in range(2)]
                xnT = sb(ph, "xnT" + tag, [128, 8, TB * 128], BF16)
                sg = [sb(ph, "sg%d" % i + tag, [128, TB * 128], F32) for i in range(2)]
                aT = sb(ph, "aT" + tag, [128, NFC, TB * 128], BF16)
                stat = sb(ph, "stat" + tag, [128, 2 * TB], F32)
                dma('sp', gbc[:], nvec_d.partition_broadcast(128), [], ['gbc'])
                for kc in range(8):
                    dma('pool', Wg[:, kc, :], wg_d[kc * 128:(kc + 1) * 128, :], [], ['Wg%d' % kc])
                    dma('pool', Wu[:, kc, :], wu_d[kc * 128:(kc + 1) * 128, :], [], ['Wu%d' % kc])
                for fc in range(NFC):
                    dma('pool', Wd[:, fc, :], wd_d[fc * 128:(fc + 1) * 128, :], [], ['Wd%d' % fc])
                for ti, (src, dst, nb) in enumerate(tiles):
                    N = nb * 128
                    X = xt[ti % 2]
                    xk = 'xt%d' % (ti % 2)
                    dma('sp', X[:, 0:nb, :], src.rearrange("(b p) d -> p b d", p=128), [], [xk])
                    for b in range(nb):
                        xsb = xs[b % 2]; xsk = 'xs%d' % (b % 2)
                        memset('pool', stat[:, b:b + 1], 0.0, ['stat%d' % b])
                        act(xsb[:], X[:, b, :], AF.Square, [xk, 'stat%d' % b], [xsk, 'stat%d' % b], accum_out=stat[:, b:b + 1])
                        rms_rstd(stat[:, b:b + 1], D, stat[:, TB + b:TB + b + 1], 'stat%d' % b)
                        stt('dve', xsb[:], X[:, b, :], stat[:, TB + b:TB + b + 1], gbc[:], ALU.mult, ALU.mult,
                            [xk, 'stat%d' % b, 'gbc'], [xsk])
                        for kc in range(8):
                            tr(PBh[0][:, kc * 128:(kc + 1) * 128], xsb[:, kc * 128:(kc + 1) * 128], ident[:], [xsk, 'ident'], ['P0'])
                        if b % 2 == 0:
                            cp('dve', xnT[:, :, b * 128:(b + 1) * 128], V3(PBh[0], 8), ['P0'], ['xnT'])
                        else:
                            act(xnT[:, :, b * 128:(b + 1) * 128], V3(PBh[0], 8), AF.Copy, ['P0'], ['xnT'])
                    for fc in range(NFC):
                        pg = 1 + (fc % 2); pu = 3 + (fc % 2)
                        for kc in range(8):
                            mm(PB[pg][:, 0:N], Wg[:, kc, fc * 128:(fc + 1) * 128], xnT[:, kc, 0:N], kc == 0, kc == 7,
                               ['Wg%d' % kc, 'xnT'], ['P%d' % pg])
                        for kc in range(8):
                            mm(PB[pu][:, 0:N], Wu[:, kc, fc * 128:(fc + 1) * 128], xnT[:, kc, 0:N], kc == 0, kc == 7,
                               ['Wu%d' % kc, 'xnT'], ['P%d' % pu])
                        s = sg[fc % 2]; sk = 'sg%d' % (fc % 2)
                        act(s[:, 0:N], PB[pg][:, 0:N], AF.Silu, ['P%d' % pg], [sk])
                        tt('dve', aT[:, fc, 0:N], s[:, 0:N], PB[pu][:, 0:N], ALU.mult, [sk, 'P%d' % pu], ['aT%d' % fc])
                    for b in range(nb):
                        for nh in range(2):
                            pd = 5 + nh
                            for fc in range(NFC):
                                mm(PB[pd][:, :], aT[:, fc, b * 128:(b + 1) * 128], Wd[:, fc, nh * 512:(nh + 1) * 512], fc == 0, fc == NFC - 1,
                                   ['aT%d' % fc, 'Wd%d' % fc], ['P%d' % pd])
                            stt('dve', X[:, b, nh * 512:(nh + 1) * 512], PB[pd][:, :], 0.5, X[:, b, nh * 512:(nh + 1) * 512], ALU.mult, ALU.add,
                                ['P%d' % pd, xk], [xk])
                    dma('act', dst.rearrange("(b p) d -> p b d", p=128), X[:, 0:nb, :], [xk], [])
                em.barrier()

        def tile_list(src, dst, nblk):
            out = []
            b = 0
            while b < nblk:
                nb = min(TB, nblk - b)
                out.append((src[b * 128:(b + nb) * 128, :], dst[b * 128:(b + nb) * 128, :], nb))
                b += nb
            return out

        tilesA = [(x_s, h_s, 1)] + tile_list(x_pre, h_pre, NBP) + tile_list(x_own, h_own, NBO)
        if not SKIP_A:
            ffn_phase("A", tilesA, w_g1, w_u1, w_d1, n_f1)

        with ExitStack() as MB:
          if STAGE >= 2:
            Wout = sb(MB, "Wout", [128, 8, D], BF16)
            for kc in range(8):
                dma('pool', Wout[:, kc, :], w_out[kc * 128:(kc + 1) * 128, :], [], ['Wout%d' % kc])
            omix_s = sb(MB, "omix_s", [128, D], BF16)
            QT_s = sb(MB, "QT_s", [128, 4, 128], BF16)
            KT_s = sb(MB, "KT_s", [128, 4, 128], BF16)
            Vb_s = sb(MB, "Vb_s", [128, 512], BF16)
            OHG = sb(MB, "OHG", [128, NBO, 512], BF16)
            gsub = sb(MB, "gsub", [128, 512], F32)
            csel = sb(MB, "csel", [128, 2], F32)
            dma('sp', csel[:], c_csel, [], ['csel'])
            dma('sp', V3(gsub[:], 4)[:, 0, :], g_sub.partition_broadcast(128), [], ['gsub'])
            ts('dve', V3(gsub[:], 4)[:, 0, :], V3(gsub[:], 4)[:, 0, :], 1.0 - LAM_INIT, None, ALU.mult, None, ['gsub'], ['gsub'])
            for h in range(1, 4):
                cp('dve', V3(gsub[:], 4)[:, h, :], V3(gsub[:], 4)[:, 0, :], ['gsub'], ['gsub'])

            def make_common(st, Wt, wkey):
                C = {}
                gmix = sb(st, "gmix" + wkey, [128, D], F32)
                dma('sp', gmix[:], n_mix.partition_broadcast(128), [], ['gmix'])
                hb = sb(st, "hb" + wkey, [128, D], F32); hs = sb(st, "hs" + wkey, [128, D], BF16)
                hnT = sb(st, "hnT" + wkey, [128, 8, 128], BF16)
                sqt = sb(st, "sqt" + wkey, [128, D], BF16); bst = sb(st, "bst" + wkey, [128, 4], F32)
                nst = sb(st, "nst" + wkey, [128, 16], F32); ntmp = sb(st, "ntmp" + wkey, [128, 512], F32)
                C['hb'] = hb

                def load_norm_block(src_ap):
                    dma('sp', hb[:], src_ap, [], ['hb'])
                    memset('pool', bst[:, 0:1], 0.0, ['bst'])
                    act(sqt[:], hb[:], AF.Square, ['hb', 'bst'], ['sqt', 'bst'], accum_out=bst[:, 0:1])
                    rms_rstd(bst[:, 0:1], D, bst[:, 1:2], 'bst')
                    stt('dve', hs[:], hb[:], bst[:, 1:2], gmix[:], ALU.mult, ALU.mult, ['hb', 'bst', 'gmix'], ['hs'])
                    for kc in range(8):
                        tr(PBh[0][:, kc * 128:(kc + 1) * 128], hs[:, kc * 128:(kc + 1) * 128], ident[:], ['hs', 'ident'], ['P0'])
                    cp('dve', hnT[:], V3(PBh[0], 8), ['P0'], ['hnT'])

                def proj(cg, pbank):
                    for kc in range(8):
                        mm(PB[pbank][:, :], hnT[:, kc, :], Wt[:, kc, cg * 512:(cg + 1) * 512], kc == 0, kc == 7,
                           ['hnT', wkey + '%d' % kc], ['P%d' % pbank])

                def norm_heads(src_ap, nh, dh, gain_ap, out_ap, r, w):
                    act(ntmp[:, 0:nh * dh], src_ap, AF.Square, r, ['nh_tmp'])
                    red(nst[:, 0:nh], ntmp[:, 0:nh * dh].rearrange("p (h d) -> p h d", h=nh), ['nh_tmp'], ['nh_stat'])
                    rms_rstd(nst[:, 0:nh], dh, nst[:, 8:8 + nh], 'nh_stat')
                    s3 = src_ap.rearrange("p (h d) -> p h d", h=nh)
                    g3 = gain_ap.rearrange("p (h d) -> p h d", h=nh)
                    o3 = out_ap.rearrange("p (h d) -> p h d", h=nh)
                    for h in range(nh):
                        stt('dve', o3[:, h, :], s3[:, h, :], nst[:, 8 + h:9 + h], g3[:, h, :], ALU.mult, ALU.mult, r + ['nh_stat'], w)
                C['load'] = load_norm_block; C['proj'] = proj; C['norm_heads'] = norm_heads
                return C

            with ExitStack() as P1:
                Win1 = sb(P1, "Win1", [128, 8, 2048], BF16)
                for kc in range(8):
                    dma('pool', Win1[:, kc, :], w_in[kc * 128:(kc + 1) * 128, 0:2048], [], ['Wa%d' % kc])
                C = make_common(P1, Win1, 'Wa')
                load_norm_block, proj, norm_heads, hb = C['load'], C['proj'], C['norm_heads'], C['hb']
                lbb = sb(P1, "lbb", [128, 512], F32); oml = sb(P1, "oml", [128, 512], F32); ghg = sb(P1, "ghg", [128, 512], F32)
                maskT = sb(P1, "maskT", [128, 512], F32); Up = sb(P1, "Up", [128, 128], F32); Wm = sb(P1, "Wm", [128, 128], F32)
                Sel = sb(P1, "Sel", [128, 4], F32); rowsel = sb(P1, "rowsel", [128, 4], F32)
                for (t, s, k) in ((maskT, c_maskT, 'maskT'), (Up, c_up, 'Up'), (Wm, c_wm, 'Wm'), (Sel, c_sel, 'Sel'), (rowsel, c_rowsel, 'rowsel')):
                    dma('sp', t[:], s, [], [k])
                dma('sp', lbb[:], lb_log[0:1, :].partition_broadcast(128), [], ['lbb'])
                dma('sp', oml[:], lb_log[1:2, :].partition_broadcast(128), [], ['oml'])
                tt('dve', lbb[:], lbb[:], oml[:], ALU.subtract, ['lbb', 'oml'], ['lbb'])
                act(lbb[:], lbb[:], AF.Sigmoid, ['lbb'], ['lbb'])
                ts('dve', oml[:], lbb[:], -1.0, 1.0, ALU.mult, ALU.add, ['lbb'], ['oml'])
                g3_ = V3(ghg[:], 4)
                dma('sp', g3_[:, 0, :], g_hg.partition_broadcast(128), [], ['ghg'])
                for h in range(1, 4):
                    cp('dve', g3_[:, h, :], g3_[:, 0, :], ['ghg'], ['ghg'])
                S = sb(P1, "S", [128, 4, 128], F32)
                memset('pool', S[:], 0.0, ['S0', 'S1', 'S2', 'S3'])
                Sp = [sb(P1, "Sp%d" % c, [128, 4, 128], BF16) for c in range(2)]
                f_t = sb(P1, "f_t", [128, 512], F32); g_t = sb(P1, "g_t", [128, 512], F32); kk = sb(P1, "kk", [128, 512], F32)
                e1 = sb(P1, "e1", [128, 512], F32); e2 = sb(P1, "e2", [128, 512], F32); qs = sb(P1, "qs", [128, 512], F32)
                sx = sb(P1, "sx", [128, 512], F32); xtra = sb(P1, "xtra", [128, 512], F32)
                qtl = sb(P1, "qtl", [128, 512], BF16); ktl = sb(P1, "ktl", [128, 512], BF16)
                khc = [sb(P1, "khat%d" % c, [128, 512], BF16) for c in range(2)]
                vhb = sb(P1, "vhb", [128, 512], BF16)
                qT = sb(P1, "qT", [128, 4, 128], BF16); kT = sb(P1, "kT", [128, 4, 128], BF16)
                qT0 = sb(P1, "qT0", [128, 4, 128], BF16); qT1 = sb(P1, "qT1", [128, 4, 128], BF16)
                memset('pool', qT0[:], 0.0, ['qT0']); memset('pool', qT1[:], 0.0, ['qT1'])
                ecol = sb(P1, "ecol", [128, 4, 4], F32)
                ATm = sb(P1, "ATm", [128, 512], BF16)

                def gate_f():
                    proj(1, 1)
                    act(f_t[:], PB[1][:], AF.Sigmoid, ['P1'], ['f_t'])
                    tt('dve', f_t[:], f_t[:], oml[:], ALU.mult, ['f_t', 'oml'], ['f_t'])
                    tt('dve', f_t[:], f_t[:], lbb[:], ALU.add, ['f_t', 'lbb'], ['f_t'])
                    act(g_t[:], f_t[:], AF.Ln, ['f_t'], ['g_t'])
                    ts('dve', kk[:], f_t[:], -1.0, 1.0, ALU.mult, ALU.add, ['f_t'], ['kk'])

                def hg_out(o_ps_key, o_ps, dst, dkey):
                    proj(3, 1)
                    act(sx[:], PB[1][:], AF.Silu, ['P1'], ['sx'])
                    tt('dve', sx[:], sx[:], ghg[:], ALU.mult, ['sx', 'ghg'], ['sx'])
                    norm_heads(o_ps, 4, 128, sx[:], dst, [o_ps_key, 'sx'], [dkey])

                with ExitStack() as SS:
                    fT = sb(SS, "fT", [128, 4, 128], F32); qTf = sb(SS, "qTf", [128, 4, 128], F32)
                    qTm = [sb(SS, "qTm%d" % j, [128, 4, 128], F32) for j in range(4)]
                    S0t = [sb(SS, "S0t%d" % i, [128, 128], F32) for i in range(2)]
                    Snew = [sb(SS, "Snew%d" % i, [128, 128], F32) for i in range(2)]
                    load_norm_block(h_s)
                    gate_f()
                    proj(0, 2)
                    act(qs[:], PB[2][:], AF.Silu, ['P2'], ['qs'])
                    ts('dve', qs[:], qs[:], 128 ** -0.5, None, ALU.mult, None, ['qs'], ['qs'])
                    proj(2, 3)
                    act(e1[:], PB[3][:], AF.Copy, ['P3'], ['e1'])
                    for h in range(4):
                        tr(PB[4][:, h * 128:(h + 1) * 128], f_t[:, h * 128:(h + 1) * 128], identf[:], ['f_t', 'identf'], ['P4'])
                        tr(PB[5][:, h * 128:(h + 1) * 128], qs[:, h * 128:(h + 1) * 128], identf[:], ['qs', 'identf'], ['P5'])
                    act(fT[:], V3(PB[4][:], 4), AF.Copy, ['P4'], ['fT'])
                    cp('dve', qTf[:], V3(PB[5][:], 4), ['P5'], ['qTf'])
                    for j in range(4):
                        memset('pool', qTm[j][:], 0.0, ['qTm%d' % j])
                        cp('dve', qTm[j][:, :, 32 * j:32 * j + 1], qTf[:, :, 32 * j:32 * j + 1], ['qTf', 'qTm%d' % j], ['qTm%d' % j])
                    kkm = [e2, sx, g_t, xtra]; kkmk = ['e2', 'sx', 'g_t', 'xtra']
                    for j in range(4):
                        ts('dve', kkm[j][:], kk[:], rowsel[:, j:j + 1], None, ALU.mult, None, ['kk', 'rowsel'], [kkmk[j]])
                    for h in range(4):
                        for j in range(4):
                            i2 = (h * 4 + j) % 2
                            dma('sp', S0t[i2][:], state_in[j * 4 + h], [], ['S0t%d' % i2])
                            mm(PB[6][:, 0:128], kkm[j][:, h * 128:(h + 1) * 128], e1[:, h * 128:(h + 1) * 128], True, True,
                               [kkmk[j], 'e1'], ['P6'])
                            stt('dve', Snew[i2][:], S0t[i2][:], fT[:, h, 32 * j:32 * j + 1], PB[6][:, 0:128], ALU.mult, ALU.add,
                                ['S0t%d' % i2, 'fT', 'P6'], ['Snew%d' % i2])
                            dma('act', state_s[j * 4 + h], Snew[i2][:], ['Snew%d' % i2], [])
                            mm(PB[7][:, h * 128:(h + 1) * 128], qTm[j][:, h, :], Snew[i2][:], j == 0, j == 3, ['qTm%d' % j, 'Snew%d' % i2], ['P7'])
                    hg_out('P7', PB[7][:], omix_s[:, 0:512], 'omix_s_hg')
                    em.barrier()

                def hg_block(gb, is_own):
                    li = gb - NBP if is_own else gb
                    load_norm_block((h_own if is_own else h_pre)[li * 128:(li + 1) * 128, :])
                    gate_f()
                    for hh in range(4):
                        if is_own:
                            mm(PB[4][:, hh * 128:(hh + 1) * 128], Up[:], g_t[:, hh * 128:(hh + 1) * 128], True, True, ['Up', 'g_t'], ['P4'])
                        mm(PB[5][:, hh * 128:(hh + 1) * 128], Wm[:], g_t[:, hh * 128:(hh + 1) * 128], True, True, ['Wm', 'g_t'], ['P5'])
                    for h in range(4):
                        mm(PB[6][:, h * 4:(h + 1) * 4], g_t[:, h * 128:(h + 1) * 128], Sel[:], True, True, ['g_t', 'Sel'], ['P6'])
                    act(ecol[:], PB[6][:, 0:16].rearrange("p (h c) -> p h c", h=4), AF.Exp, ['P6'], ['ecol'])
                    act(e2[:], PB[5][:], AF.Exp, ['P5'], ['e2'])
                    for c in range(2):
                        stt('dve', khc[c][:], kk[:], csel[:, c:c + 1], e2[:], ALU.mult, ALU.mult, ['kk', 'e2', 'csel'], ['khat'])
                    proj(2, 3)
                    cp('dve', vhb[:], PB[3][:], ['P3'], ['vhb'])
                    if is_own:
                        act(e1[:], PB[4][:], AF.Exp, ['P4'], ['e1'])
                        act(e2[:], PB[4][:], AF.Exp, ['P4'], ['e2'], scale=-1.0)
                        proj(0, 2)
                        act(qs[:], PB[2][:], AF.Silu, ['P2'], ['qs'])
                        stt('dve', qtl[:], qs[:], 128 ** -0.5, e1[:], ALU.mult, ALU.mult, ['qs', 'e1'], ['qtl'])
                        tt('dve', ktl[:], kk[:], e2[:], ALU.mult, ['kk', 'e2'], ['ktl'])
                        for h in range(4):
                            tr(PBh[0][:, h * 128:(h + 1) * 128], qtl[:, h * 128:(h + 1) * 128], ident[:], ['qtl', 'ident'], ['P0'])
                            tr(PBh[0][:, 512 + h * 128:512 + (h + 1) * 128], ktl[:, h * 128:(h + 1) * 128], ident[:], ['ktl', 'ident'], ['P0'])
                        act(qT[:], V3(PBh[0][:, 0:512], 4), AF.Copy, ['P0'], ['qT'])
                        act(kT[:], V3(PBh[0][:, 512:1024], 4), AF.Copy, ['P0'], ['kT'])
                        cp('dve', qT0[:, :, 0:64], qT[:, :, 0:64], ['qT'], ['qT0'])
                        cp('dve', qT1[:, :, 64:128], qT[:, :, 64:128], ['qT'], ['qT1'])
                        for h in range(4):
                            mm(PB[6][:, h * 128:(h + 1) * 128], kT[:, h, :], qT[:, h, :], True, True, ['kT', 'qT'], ['P6'])
                        tt('dve', ATm[:], PB[6][:], maskT[:], ALU.mult, ['P6', 'maskT'], ['ATm'])
                    for c in range(2):
                        for h in range(4):
                            if is_own:
                                ts('dve', Sp[c][:, h, :], S[:, h, :], ecol[:, h, 2 * c:2 * c + 1], None, ALU.mult, None,
                                   ['S%d' % h, 'ecol'], ['Sp%d_%d' % (c, h)])
                            mm(PB[1][:, h * 128:(h + 1) * 128], khc[c][:, h * 128:(h + 1) * 128], vhb[:, h * 128:(h + 1) * 128],
                               True, True, ['khat', 'vhb'], ['P1'])
                            stt('dve', S[:, h, :], S[:, h, :], ecol[:, h, 2 * c + 1:2 * c + 2], PB[1][:, h * 128:(h + 1) * 128], ALU.mult, ALU.add,
                                ['S%d' % h, 'ecol', 'P1'], ['S%d' % h])
                    if is_own:
                        for h in range(4):
                            o_h = PB[7][:, h * 128:(h + 1) * 128]
                            mm(o_h, ATm[:, h * 128:(h + 1) * 128], vhb[:, h * 128:(h + 1) * 128], True, False, ['ATm', 'vhb'], ['P7'])
                            mm(o_h, qT0[:, h, :], Sp[0][:, h, :], False, False, ['qT0', 'Sp0_%d' % h], ['P7'])
                            mm(o_h, qT1[:, h, :], Sp[1][:, h, :], False, True, ['qT1', 'Sp1_%d' % h], ['P7'])
                        hg_out('P7', PB[7][:], OHG[:, li, :], 'OHG%d' % li)

                if STAGE >= 3 and not SKIP_P1:
                    for gb in range(NBP):
                        hg_block(gb, False)
                    for h in range(4):
                        ts('dve', S[:, h, :], S[:, h, :], flg[:, 1:2], None, ALU.mult, None, ['S%d' % h, 'flg'], ['S%d' % h])
                    for gb in range(NBP, NKB):
                        hg_block(gb, True)
                    for h in range(4):
                        dma('sp', state_p[h], S[:, h, :], ['S%d' % h], [])
                em.barrier()

            with ExitStack() as P2:
                Win2 = sb(P2, "Win2", [128, 8, 1536], BF16)
                for kc in range(8):
                    dma('pool', Win2[:, kc, :], w_in[kc * 128:(kc + 1) * 128, 2048:3584], [], ['Wb%d' % kc])
                C = make_common(P2, Win2, 'Wb')
                load_norm_block, proj, norm_heads, hb = C['load'], C['proj'], C['norm_heads'], C['hb']
                KT = sb(P2, "KT", [128, 4, NKB * 128], BF16)
                Vb = sb(P2, "Vb", [128, NKB, 4, 132], BF16)
                memset('pool', Vb[:, :, :, 128:132], 1.0, ['Vb_ones'])
                gqb = sb(P2, "gqb", [128, 512], F32); gkb = sb(P2, "gkb", [128, 512], F32)
                EBd = sb(P2, "EBd", [128, 512], F32)
                cbo = sb(P2, "cbo", [128, 4 * (NKB + 1)], F32); cbp = sb(P2, "cbp", [128, 4 * (NKB + 1)], F32)
                for (t, s, k) in ((EBd, c_ebdiag, 'EBd'), (cbo, c_cb, 'cbo')):
                    dma('sp', t[:], s, [], [k])
                ts('dve', cbp[:], cbo[:], flg[:, 0:1], None, ALU.add, None, ['cbo', 'flg'], ['cbp'])
                for (t, s, k) in ((gqb, g_q, 'gqb'), (gkb, g_k, 'gkb')):
                    t3 = t[:].rearrange("p (h d) -> p h d", h=8)
                    dma('sp', t3[:, 0, :], s.partition_broadcast(128), [], [k])
                    for h in range(1, 8):
                        cp('dve', t3[:, h, :], t3[:, 0, :], [k], [k])
                kout = sb(P2, "kout", [128, 512], F32); vout = sb(P2, "vout", [128, 512], F32)
                qn = sb(P2, "qn", [128, 512], BF16); knb = sb(P2, "knb", [128, 512], BF16)
                QT = sb(P2, "QT", [128, 4, 128], BF16)
                QTz = [sb(P2, "QTz%d" % m, [128, 4, 128], BF16) for m in range(2)]
                Eb = [sb(P2, "Eb%d" % i, [128, 256], F32) for i in range(2)]
                Pb = [sb(P2, "Pb%d" % i, [128, 256], BF16) for i in range(2)]
                ofin = sb(P2, "ofin", [128, 512], F32); rz = sb(P2, "rz", [128, 4], F32)
                odab = sb(P2, "odab", [128, 512], BF16); omT = sb(P2, "omT", [128, 8, 128], BF16)

                def da_kv(kdst, vdst, KT_dst, ktkey, Vb_dst, vkey, vb3):
                    proj(1, 2)
                    norm_heads(PB[2][:], 8, 64, gkb[:], kout[:], ['P2', 'gkb'], ['kout'])
                    if kdst is not None:
                        dma('act', kdst, kout[:], ['kout'], [])
                    cp('dve', knb[:], kout[:], ['kout'], ['knb'])
                    for h in range(4):
                        tr(PBh[0][:, h * 128:(h + 1) * 128], knb[:, h * 128:(h + 1) * 128], ident[:], ['knb', 'ident'], ['P0'])
                    act(KT_dst, V3(PBh[0][:, 0:512], 4), AF.Copy, ['P0'], [ktkey])
                    proj(2, 3)
                    act(vout[:], PB[3][:], AF.Copy, ['P3'], ['vout'])
                    if vdst is not None:
                        dma('act', vdst, vout[:], ['vout'], [])
                    cp('dve', Vb_dst, V3(vout[:], 4) if vb3 else vout[:], ['vout', 'Vb_ones'], [vkey])

                def q_T(dst, dkey):
                    proj(0, 2)
                    norm_heads(PB[2][:], 8, 64, gqb[:], qn[:], ['P2', 'gqb'], ['qn'])
                    for h in range(4):
                        tr(PBh[0][:, h * 128:(h + 1) * 128], qn[:, h * 128:(h + 1) * 128], ident[:], ['qn', 'ident'], ['P0'])
                    act(dst, V3(PBh[0][:, 0:512], 4), AF.Copy, ['P0'], [dkey])

                load_norm_block(h_s)
                da_kv(k_s, v_s, KT_s[:], 'KT_s', Vb_s[:], 'Vb_s', False)
                q_T(QT_s[:], 'QT_s')

                def da_block(gb, is_own):
                    li = gb - NBP if is_own else gb
                    load_norm_block((h_own if is_own else h_pre)[li * 128:(li + 1) * 128, :])
                    da_kv(k_own[li * 128:(li + 1) * 128, :] if is_own else None, v_own[li * 128:(li + 1) * 128, :] if is_own else None,
                          KT[:, :, gb * 128:(gb + 1) * 128], 'KT%d' % gb, Vb[:, gb, :, 0:128], 'Vb%d' % gb, True)
                    if not is_own or DA_SUB < 3:
                        return
                    q_T(QT[:], 'QT')
                    if DA_SUB < 4:
                        return
                    for m in range(2):
                        ts('dve', QTz[m][:], QT[:], csel[:, m:m + 1], None, ALU.mult, None, ['QT', 'csel'], ['QTz%d' % m])
                    cnt = 0
                    for h in range(4):
                        po = 3 + 2 * (h % 2)
                        Om = [PB[po][:, 0:132], PB[po + 1][:, 0:132]]
                        pok = ['P%d' % po, 'P%d' % (po + 1)]
                        for kb in range(gb + 1):
                            i2 = cnt % 2; cnt += 1
                            pst = 1 + i2
                            for m in range(2):
                                mm(PB[pst][:, m * 128:(m + 1) * 128], KT[:, h, kb * 128:(kb + 1) * 128], QTz[m][:, h, :],
                                   True, True, ['KT%d' % kb, 'QTz%d' % m], ['P%d' % pst])
                            dl = gb - kb
                            cbt = cbp if kb < NBP else cbo
                            bcol = cbt[:, h * (NKB + 1) + dl:h * (NKB + 1) + dl + 1]
                            if dl == 0:
                                act(Eb[i2][:], PB[pst][:, 0:256], AF.Exp, ['P%d' % pst, 'cbp', 'cbo'], ['Eb%d' % i2], scale=DA_SCALE, bias=bcol)
                                for m in range(2):
                                    tt('dve', Pb[i2][:, m * 128:(m + 1) * 128], Eb[i2][:, m * 128:(m + 1) * 128],
                                       EBd[:, 0:128], ALU.mult, ['Eb%d' % i2, 'EBd'], ['Pb%d' % i2])
                            else:
                                act(Pb[i2][:], PB[pst][:, 0:256], AF.Exp, ['P%d' % pst, 'cbp', 'cbo'], ['Pb%d' % i2], scale=DA_SCALE, bias=bcol)
                            for m in range(2):
                                mm(Om[m], Pb[i2][:, m * 128:(m + 1) * 128], Vb[:, kb, h, 0:132], kb == 0, kb == gb,
                                   ['Pb%d' % i2, 'Vb%d' % kb, 'Vb_ones'], [pok[m]])
                        recip(rz[:, 0:1], Om[0][:, 128:129], [pok[0]], ['rz'])
                        recip(rz[:, 1:2], Om[1][:, 128:129], [pok[1], 'rz'], ['rz'])
                        tt('dve', rz[:, 1:2], rz[:, 1:2], lam_t[:, 1:2], ALU.mult, ['rz', 'lam_t'], ['rz'])
                        ts('dve', ofin[:, h * 128:(h + 1) * 128], Om[0][:, 0:128], rz[:, 0:1], None, ALU.mult, None, [pok[0], 'rz'], ['ofin'])
                        stt('dve', ofin[:, h * 128:(h + 1) * 128], Om[1][:, 0:128], rz[:, 1:2], ofin[:, h * 128:(h + 1) * 128], ALU.mult, ALU.add,
                            [pok[1], 'rz', 'ofin'], ['ofin'])
                    norm_heads(ofin[:], 4, 128, gsub[:], odab[:], ['ofin', 'gsub'], ['odab'])
                    if DA_SUB < 5:
                        return
                    for kc in range(8):
                        src = OHG[:, li, kc * 128:(kc + 1) * 128] if kc < 4 else odab[:, (kc - 4) * 128:(kc - 3) * 128]
                        tr(PBh[0][:, kc * 128:(kc + 1) * 128], src, ident[:], ['OHG%d' % li, 'odab', 'ident'], ['P0'])
                    cp('dve', omT[:], V3(PBh[0], 8), ['P0'], ['omT'])
                    for nh in range(2):
                        pd = 5 + nh
                        for kc in range(8):
                            mm(PB[pd][:, :], omT[:, kc, :], Wout[:, kc, nh * 512:(nh + 1) * 512], kc == 0, kc == 7,
                               ['omT', 'Wout%d' % kc], ['P%d' % pd])
                        tt('dve', hb[:, nh * 512:(nh + 1) * 512], hb[:, nh * 512:(nh + 1) * 512], PB[pd][:, :], ALU.add, ['hb', 'P%d' % pd], ['hb'])
                    dma('act', h2_own[li * 128:(li + 1) * 128, :], hb[:], ['hb'], [])

                if STAGE >= 4 and DA_SUB >= 2:
                    for gb in range(NKB):
                        da_block(gb, gb >= NBP)
                em.barrier()

            with ExitStack() as B2:
                biasS = sb(B2, "biasS", [128, NG * 128], F32); selfb = sb(B2, "selfb", [128, 4], F32)
                dma('sp', biasS[:], c_biasS, [], ['biasS']); dma('sp', selfb[:], c_selfb, [], ['selfb'])
                hb_s = sb(B2, "hb_s", [128, D], F32)
                dma('sp', hb_s[:], h_s, [], ['hb_s'])
                pti = sb(B2, "pti", [128, 4 * NG], I32); cfi = sb(B2, "cfi", [128, 1], I32)
                ptf = sb(B2, "ptf", [128, 4 * NG], F32); cff = sb(B2, "cff", [128, 1], F32); idx = sb(B2, "idx", [128, 4 * NG], I32)
                dma('sp', pti[:], ptrep, [], ['pti']); dma('sp', cfi[:], c_coff, [], ['cfi'])
                cp('dve', ptf[:], pti[:], ['pti'], ['ptf']); cp('dve', cff[:], cfi[:], ['cfi'], ['cff'])
                ts('dve', ptf[:], ptf[:], 128.0, cff[:, 0:1], ALU.mult, ALU.add, ['ptf', 'cff'], ['ptf'])
                cp('dve', idx[:], ptf[:], ['ptf'], ['idx'])
                idxc = []
                for c_ in range(4 * NG):
                    t_ = sb(B2, "idxc%d" % c_, [128, 1], I32)
                    cp('dve', t_[:], idx[:, c_:c_ + 1], ['idx'], ['idxc'])
                    idxc.append(t_)
                Kg = [sb(B2, "Kg%d" % i, [128, 16, 512], BF16) for i in range(2)]
                Vg = [sb(B2, "Vg%d" % i, [128, 16, 512], BF16) for i in range(2)]
                KTt = [sb(B2, "KTt%d" % i, [128, 4, 128], BF16) for i in range(2)]
                qbd = sb(B2, "qbd", [128, 4, 2], BF16)
                E = sb(B2, "E", [128, NT * 8], F32); tmpS = sb(B2, "tmpS", [128, 128], F32)
                A = sb(B2, "A", [128, NT, 4], BF16); Af = sb(B2, "Af", [128, NT], F32)
                zp = sb(B2, "zp", [128, 8], F32); zc = sb(B2, "zc", [128, 16], F32)
                pv = sb(B2, "pv", [4, 512], F32)
                oda = sb(B2, "oda", [128, 512], F32)
                nst2 = sb(B2, "nst2", [128, 16], F32); ntmp2 = sb(B2, "ntmp2", [128, 512], F32)
                omT2 = sb(B2, "omT2", [128, 8, 128], BF16)
                memset('pool', oda[:], 0.0, ['oda'])
                gcount = 0
                for j in range(4 if STAGE >= 5 else 0):
                    for m in range(2):
                        ts('dve', qbd[:, :, m:m + 1], QT_s[:, :, 32 * j:32 * j + 1], csel[:, m:m + 1], None, ALU.mult, None, ['QT_s', 'csel'], ['qbd'])
                    E3 = E[:].rearrange("p (t n) -> p t n", n=8)
                    for g in range(NG):
                        gi = gcount % 2; gcount += 1
                        col = j * NG + g
                        em.dma('pool', (lambda K_, c_: (lambda e: e.indirect_dma_start(
                            out=K_, out_offset=None, in_=cache_k,
                            in_offset=bass.IndirectOffsetOnAxis(ap=idxc[c_][:, :], axis=0))))(Kg[gi][:].rearrange("p a b -> p (a b)"), col),
                            ['idxc'], ['Kg%d' % gi])
                        for i in range(16):
                            t2 = i % 2
                            for h in range(4):
                                tr(PBh[t2][:, h * 128:(h + 1) * 128], Kg[gi][:, i, h * 128:(h + 1) * 128], ident[:], ['Kg%d' % gi, 'ident'], ['P%d' % t2])
                            if i % 2 == 0:
                                act(KTt[t2][:], V3(PBh[t2][:, 0:512], 4), AF.Copy, ['P%d' % t2], ['KTt%d' % t2])
                            else:
                                cp('dve', KTt[t2][:], V3(PBh[t2][:, 0:512], 4), ['P%d' % t2], ['KTt%d' % t2])
                            for h in range(4):
                                mm(PB[2][:, i * 8 + 2 * h:i * 8 + 2 * h + 2], KTt[t2][:, h, :], qbd[:, h, :], True, True, ['KTt%d' % t2, 'qbd'], ['P2'])
                        stt('dve', tmpS[:], PB[2][:, 0:128], DA_SCALE, biasS[:, g * 128:(g + 1) * 128], ALU.mult, ALU.add, ['P2', 'biasS'], ['tmpS'])
                        act(E[:, g * 128:(g + 1) * 128], tmpS[:], AF.Exp, ['tmpS'], ['E'])
                    for h in range(4):
                        mm(PB[2][:, 2 * h:2 * h + 2], KT_s[:, h, :], qbd[:, h, :], True, True, ['KT_s', 'qbd'], ['P2'])
                    ts('dve', tmpS[:, 0:8], PB[2][:, 0:8], DA_SCALE, selfb[:, j:j + 1], ALU.mult, ALU.add, ['P2', 'selfb'], ['tmpS'])
                    act(E[:, NPG * 8:NPG * 8 + 8], tmpS[:, 0:8], AF.Exp, ['tmpS'], ['E'])
                    red(zp[:], E[:].rearrange("p (t n) -> p n t", n=8), ['E'], ['zp'])
                    mm(PB[3][:, 0:8], onesf[:], zp[:], True, True, ['onesf', 'zp'], ['P3'])
                    recip(zc[:, 0:8], PB[3][:, 0:8], ['P3'], ['zc'])
                    for h in range(4):
                        tt('dve', zc[:, 8 + h:9 + h], zc[:, 2 * h + 1:2 * h + 2], lam_t[:, 1:2], ALU.mult, ['zc', 'lam_t'], ['zc'])
                        ts('dve', Af[:], E3[:, :, 2 * h], zc[:, 2 * h:2 * h + 1], None, ALU.mult, None, ['E', 'zc'], ['Af'])
                        stt('dve', A[:, :, h], E3[:, :, 2 * h + 1], zc[:, 8 + h:9 + h], Af[:], ALU.mult, ALU.add, ['E', 'zc', 'Af'], ['A'])
                    for g in range(NG):
                        gi = gcount % 2; gcount += 1
                        col = j * NG + g
                        em.dma('pool', (lambda V_, c_: (lambda e: e.indirect_dma_start(
                            out=V_, out_offset=None, in_=cache_v,
                            in_offset=bass.IndirectOffsetOnAxis(ap=idxc[c_][:, :], axis=0))))(Vg[gi][:].rearrange("p a b -> p (a b)"), col),
                            ['idxc'], ['Vg%d' % gi])
                        for i in range(16):
                            mm(PB[4][0:4, :], A[:, g * 16 + i, :], Vg[gi][:, i, :], g == 0 and i == 0, False, ['A', 'Vg%d' % gi], ['P4'])
                    mm(PB[4][0:4, :], A[:, NPG, :], Vb_s[:], False, True, ['A', 'Vb_s'], ['P4'])
                    act(pv[:], PB[4][0:4, :], AF.Copy, ['P4'], ['pv'])
                    for h in range(4):
                        dma('sp', oda[32 * j:32 * j + 1, h * 128:(h + 1) * 128], pv[h:h + 1, h * 128:(h + 1) * 128], ['pv', 'oda'], ['oda'])
                act(ntmp2[:], oda[:], AF.Square, ['oda'], ['nh2'])
                red(nst2[:, 0:4], V3(ntmp2[:], 4), ['nh2'], ['nst2'])
                rms_rstd(nst2[:, 0:4], 128, nst2[:, 8:12], 'nst2')
                for h in range(4):
                    stt('dve', omix_s[:, 512 + h * 128:512 + (h + 1) * 128], oda[:, h * 128:(h + 1) * 128], nst2[:, 8 + h:9 + h], gsub[:, h * 128:(h + 1) * 128],
                        ALU.mult, ALU.mult, ['oda', 'nst2', 'gsub'], ['omix_s_da'])
                for kc in range(8):
                    tr(PBh[0][:, kc * 128:(kc + 1) * 128], omix_s[:, kc * 128:(kc + 1) * 128], ident[:], ['omix_s_da', 'omix_s_hg', 'ident'], ['P0'])
                cp('dve', omT2[:], V3(PBh[0], 8), ['P0'], ['omT2'])
                for nh in range(2):
                    pd = 5 + nh
                    for kc in range(8):
                        mm(PB[pd][:, :], omT2[:, kc, :], Wout[:, kc, nh * 512:(nh + 1) * 512], kc == 0, kc == 7, ['omT2', 'Wout%d' % kc], ['P%d' % pd])
                    tt('dve', hb_s[:, nh * 512:(nh + 1) * 512], hb_s[:, nh * 512:(nh + 1) * 512], PB[pd][:, :], ALU.add, ['hb_s', 'P%d' % pd], ['hb_s'])
                dma('act', h2_s, hb_s[:], ['hb_s'], [])
                em.barrier()

        tilesC = [(h2_s, y_s, 1)] + tile_list(h2_own, y_own, NBO)
        if STAGE >= 6:
            ffn_phase("C", tilesC, w_g2, w_u2, w_d2, n_f2)

        with nc.Block() as block:
            em.finish(block)
        nc._em_names = em.names
    return nc


def make_consts(NBO, NBP, NPG, past_len):
    NKB = NBO + NBP
    NG = NPG // 16
    c = {}
    c["c_ident"] = np.eye(128, dtype=np.float32)
    s = np.arange(128)[:, None]; t = np.arange(128)[None, :]
    same = (s // 64) == (t // 64)
    c["c_maskT"] = np.tile((same & (s <= t)).astype(np.float32), (1, 4))
    mid = (t // 64) * 64 + 31
    c["c_up"] = (same * ((s <= t).astype(np.float32) - (s <= mid).astype(np.float32))).astype(np.float32)
    c["c_wm"] = (same & (s > t)).astype(np.float32)
    sel = np.zeros((128, 4), np.float32)
    sel[0:32, 0] = 1; sel[0:64, 1] = 1; sel[64:96, 2] = 1; sel[64:128, 3] = 1
    c["c_sel"] = sel
    ki = np.arange(128)[:, None].astype(np.float64); qi = np.arange(128)[None, :].astype(np.float64)
    ebo = np.zeros((128, 4, 128), np.float32); ebd = np.zeros((128, 4, 128), np.float32)
    for h in range(4):
        ebo[:, h, :] = 1.0
        ebd[:, h, :] = (qi >= ki)
    c["c_eboff"] = ebo.reshape(128, 512); c["c_ebdiag"] = ebd.reshape(128, 512)
    cb = np.zeros((128, 4, NKB + 1), np.float32)
    for h in range(4):
        cb[:, h, :] = -SLOPES[h] * 128.0 * np.arange(NKB + 1)[None, :] + SLOPES[h] * np.arange(128)[:, None]
    c["c_cb"] = cb.reshape(128, -1)
    p = np.arange(128)[:, None, None, None]; g = np.arange(NG)[None, :, None, None]
    i = np.arange(16)[None, None, :, None]; n = np.arange(8)[None, None, None, :]
    pos = g * 2048 + 16 * p + i
    sl = np.array(SLOPES)[n // 2]
    c["c_biasS"] = (-(sl * (past_len - pos))).astype(np.float32).reshape(128, NG * 128)
    sb_ = np.full((128, 4), NEG, np.float32)
    for j in range(4):
        sb_[32 * j, j] = 0.0
    c["c_selfb"] = sb_
    c["c_rowsel"] = (sb_ == 0.0).astype(np.float32)
    cs = np.zeros((128, 2), np.float32); cs[0:64, 0] = 1.0; cs[64:128, 1] = 1.0
    c["c_csel"] = cs
    c["c_coff"] = ((np.arange(128) % 8) * 16).astype(np.int32).reshape(128, 1)
    return c


def run(inputs, n_cores=8, debug=False, trace=False):
    xp = np.asarray(inputs["x_prompt"]); xs = np.asarray(inputs["x_sample"])
    B, L, _ = xp.shape
    DB = xs.shape[0]
    assert n_cores == 2 * B and DB == 4 * n_cores
    half = L // 2
    NBO = NBP = half // 128
    pt = np.asarray(inputs["page_table"])
    NPG = pt.shape[1]
    ck = np.asarray(inputs["cache_k"]); cv = np.asarray(inputs["cache_v"])
    NPOOL = ck.shape[1]
    past_len = NPG * 128
    NG = NPG // 16
    nc = build_program(NBO, NBP, NPG, NPOOL, debug=debug)
    consts = make_consts(NBO, NBP, NPG, past_len)
    ck2 = np.ascontiguousarray(ck[0].reshape(NPOOL * 128, 512)); cv2 = np.ascontiguousarray(cv[0].reshape(NPOOL * 128, 512))
    shared = {
        "cache_k": ck2, "cache_v": cv2,
        "w_g1": np.asarray(inputs["ffn1_w_gate"])[0], "w_u1": np.asarray(inputs["ffn1_w_up"])[0], "w_d1": np.asarray(inputs["ffn1_w_down"])[0],
        "w_g2": np.asarray(inputs["ffn2_w_gate"])[0], "w_u2": np.asarray(inputs["ffn2_w_up"])[0], "w_d2": np.asarray(inputs["ffn2_w_down"])[0],
        "w_in": np.asarray(inputs["w_in"])[0], "w_out": np.asarray(inputs["w_out"])[0],
        "n_f1": np.asarray(inputs["ffn1_norm"]), "n_mix": np.asarray(inputs["mix_norm"]), "n_f2": np.asarray(inputs["ffn2_norm"]),
        "lb_log": np.asarray(inputs["hg_lb_logits"]),
        "g_hg": np.asarray(inputs["hg_out_norm"]), "g_q": np.asarray(inputs["da_q_norm"]), "g_k": np.asarray(inputs["da_k_norm"]),
        "g_sub": np.asarray(inputs["da_subln"]),
        "lam_p": np.concatenate([np.asarray(inputs[k]) for k in ("da_lambda_q1", "da_lambda_k1", "da_lambda_q2", "da_lambda_k2")], axis=0),
    }
    shared.update(consts)
    st = np.asarray(inputs["state_hgrn"])[0]
    in_maps = []
    for c in range(n_cores):
        b, hf = c // 2, c % 2
        m = dict(shared)
        m["x_own"] = np.ascontiguousarray(xp[b, hf * half:(hf + 1) * half, :])
        m["x_pre"] = np.ascontiguousarray(xp[b, 0:half, :]) if hf == 1 else np.zeros((half, D), np.float32)
        xsb = np.zeros((128, D), np.float32)
        for j in range(4):
            xsb[32 * j] = xs[4 * c + j, 0]
        m["x_s"] = xsb
        m["state_in"] = np.ascontiguousarray(st[4 * c:4 * c + 4].reshape(16, 128, 128))
        pr = np.zeros((128, 4 * NG), np.int32)
        for j in range(4):
            for g in range(NG):
                pr[:, j * NG + g] = np.repeat(pt[4 * c + j, g * 16:(g + 1) * 16], 8)
        m["ptrep"] = pr
        fl = np.zeros((128, 2), np.float32)
        fl[:, 0] = 0.0 if hf == 1 else NEG
        fl[:, 1] = 1.0 if hf == 1 else 0.0
        m["flags"] = fl
        in_maps.append(m)
    res = run_bass_kernel_spmd(nc, in_maps, core_ids=list(range(n_cores)), **({"trace": True} if trace else {}))
    R = res.results
    y_p = np.zeros((B, L, D), np.float32); k_p = np.zeros((1, B, L, 8, 64), np.float32); v_p = np.zeros((1, B, L, 4, 128), np.float32)
    s_p = np.zeros((1, B, 4, 128, 128), np.float32)
    y_s = np.zeros((DB, 1, D), np.float32); k_s = np.zeros((1, DB, 1, 8, 64), np.float32); v_s = np.zeros((1, DB, 1, 4, 128), np.float32)
    s_s = np.zeros((1, DB, 4, 128, 128), np.float32)
    for c in range(n_cores):
        b, hf = c // 2, c % 2
        r = R[c]
        sl = slice(hf * half, (hf + 1) * half)
        y_p[b, sl] = r["y_own"]; k_p[0, b, sl] = r["k_own"].reshape(half, 8, 64); v_p[0, b, sl] = r["v_own"].reshape(half, 4, 128)
        if hf == 1:
            s_p[0, b] = r["state_p"]
        for j in range(4):
            y_s[4 * c + j, 0] = r["y_s"][32 * j]
            k_s[0, 4 * c + j, 0] = r["k_s"][32 * j].reshape(8, 64)
            v_s[0, 4 * c + j, 0] = r["v_s"][32 * j].reshape(4, 128)
        s_s[0, 4 * c:4 * c + 4] = r["state_s"].reshape(4, 4, 128, 128)
    outs = (y_p, y_s, k_p, v_p, s_p, k_s, v_s, s_s)
    if debug:
        return outs, R
    return outs


def kernel(**inputs):
    return run(inputs)
```

```python
import math
from contextlib import ExitStack
import numpy as np
import concourse.bass as bass
import concourse.mybir as mybir
from concourse.bass_utils import run_bass_kernel_spmd

F32 = mybir.dt.float32
BF16 = mybir.dt.bfloat16
I32 = mybir.dt.int32
AF = mybir.ActivationFunctionType
ALU = mybir.AluOpType
AX = mybir.AxisListType

D = 1024
DFF = 2816
NFC = DFF // 128
INC = 3584
EPS = 1e-6
DA_SCALE = 64 ** -0.5
NEG = -30000.0
SLOPES = [2.0 ** (-8.0 * (h + 1) / 4) for h in range(4)]
LAM_INIT = 0.8 - 0.6 * math.exp(0.0)
STAGE = 9
TB = 4
SKIP_A = False
SKIP_P1 = False
DA_SUB = 9


class Emitter:
    ENG = ('pe', 'act', 'dve', 'pool', 'sp')
    SAME_SYNC = ('act', 'dve', 'pool')
    W = 30000

    def __init__(self, nc, es, n_dma_slots=10, max_ops=200000):
        self.nc = nc
        self.es = es
        self.ops = {e: [] for e in self.ENG}
        self.cnt = {e: 0 for e in self.ENG}
        self.sems = {e: [] for e in self.ENG}
        self.dma_q = ('sp', 'act', 'pool')
        self.nslots = n_dma_slots
        self.dsem = {}
        self.dcnt = {}
        self.dnext = {q: 0 for q in self.dma_q}
        for q in self.dma_q:
            for i in range(n_dma_slots):
                self.dsem[(q, i)] = es.enter_context(nc.semaphore('ds_%s_%d' % (q, i)))
                self.dcnt[(q, i)] = 0
        self.waited = {e: {} for e in self.ENG}
        self.bufs = {}
        self.names = {}

    def _esem(self, e, k):
        i = (k - 1) // self.W
        while len(self.sems[e]) <= i:
            self.sems[e].append(self.es.enter_context(self.nc.semaphore('s_%s_%d' % (e, len(self.sems[e])))))
        return self.sems[e][i], (k - 1) % self.W + 1

    def _semval(self, src, val):
        if isinstance(src, str):
            return self._esem(src, val)
        return self.dsem[src], val

    def _deps(self, e, reads, writes):
        deps = []
        for b in reads:
            st = self.bufs.get(b)
            if st and st['w']:
                deps.append(st['w'])
        for b in writes:
            st = self.bufs.get(b)
            if st:
                if st['w']:
                    deps.append(st['w'])
                deps.extend(st['r'])
        need = {}
        for (src, val) in deps:
            if src == e and e not in self.SAME_SYNC:
                continue
            if self.waited[e].get(src, 0) >= val:
                continue
            need[src] = max(need.get(src, 0), val)
        for src, val in need.items():
            self.waited[e][src] = val
        return list(need.items())

    def _update(self, tok, reads, writes):
        for b in reads:
            st = self.bufs.setdefault(b, {'w': None, 'r': []})
            st['r'].append(tok)
            if len(st['r']) > 48:
                mx = {}
                for (s, v) in st['r']:
                    mx[s] = max(mx.get(s, 0), v)
                st['r'] = list(mx.items())
        for b in writes:
            self.bufs[b] = {'w': tok, 'r': []}

    def op(self, e, fn, reads=(), writes=()):
        waits = self._deps(e, reads, writes)
        self.cnt[e] += 1
        tok = (e, self.cnt[e])
        sem, _ = self._esem(e, self.cnt[e])
        wl = [self._semval(s, v) for s, v in waits]

        desc = (e, tuple(reads), tuple(writes))

        def emit(eng, fn=fn, wl=wl, sem=sem, desc=desc):
            for (s, v) in wl:
                eng.wait_ge(s, v)
            ins = fn(eng)
            try:
                self.names[ins.ins.name] = desc
            except Exception:
                pass
            ins.then_inc(sem, 1)
        self.ops[e].append(emit)
        self._update(tok, reads, writes)
        return tok

    def dma(self, q, fn, reads=(), writes=()):
        waits = self._deps(q, reads, writes)
        i = self.dnext[q]
        self.dnext[q] = (i + 1) % self.nslots
        key = (q, i)
        prev = self.dcnt[key]
        if prev > 0 and self.waited[q].get(key, 0) < prev:
            waits.append((key, prev))
            self.waited[q][key] = prev
        self.dcnt[key] = prev + 16
        tok = (key, prev + 16)
        sem = self.dsem[key]
        wl = [self._semval(s, v) for s, v in waits]

        def emit(eng, fn=fn, wl=wl, sem=sem):
            for (s, v) in wl:
                eng.wait_ge(s, v)
            fn(eng).then_inc(sem, 16)
        self.ops[q].append(emit)
        self._update(tok, reads, writes)
        return tok

    def barrier(self):
        targets = []
        for e in self.ENG:
            if self.cnt[e] > 0:
                targets.append((e, self.cnt[e]))
        for key, v in self.dcnt.items():
            if v > 0:
                targets.append((key, v))
        for e in self.ENG:
            wl = []
            for (src, val) in targets:
                if src == e and e not in self.SAME_SYNC:
                    continue
                if self.waited[e].get(src, 0) >= val:
                    continue
                self.waited[e][src] = val
                wl.append(self._semval(src, val))

            def emit(eng, wl=wl):
                for (s, v) in wl:
                    eng.wait_ge(s, v)
            self.ops[e].append(emit)
        self.bufs = {}

    def finish(self, block):
        self.barrier()
        ops = self.ops

        @block.tensor
        def _(eng):
            for f in ops['pe']:
                f(eng)

        @block.scalar
        def _(eng):
            for f in ops['act']:
                f(eng)

        @block.vector
        def _(eng):
            for f in ops['dve']:
                f(eng)

        @block.gpsimd
        def _(eng):
            for f in ops['pool']:
                f(eng)

        @block.sync
        def _(eng):
            for f in ops['sp']:
                f(eng)


def V3(ap, h):
    return ap.rearrange("p (h d) -> p h d", h=h)


def build_program(NBO, NBP, NPG, NPOOL, debug=False):
    NG = NPG // 16
    NT = NPG + 1
    NKB = NBP + NBO
    nc = bass.Bass("TRN2", target_bir_lowering=False)

    def din(name, shape, dt=F32):
        return nc.dram_tensor(name, shape, dt, kind="ExternalInput").ap()

    def dout(name, shape, dt=F32):
        return nc.dram_tensor(name, shape, dt, kind="ExternalOutput").ap()

    def dscr(name, shape, dt=F32):
        return nc.dram_tensor(name, shape, dt, kind=("ExternalOutput" if debug else "Internal")).ap()

    x_own = din("x_own", [NBO * 128, D]); x_pre = din("x_pre", [NBP * 128, D]); x_s = din("x_s", [128, D])
    cache_k = din("cache_k", [NPOOL * 128, 512]); cache_v = din("cache_v", [NPOOL * 128, 512])
    state_in = din("state_in", [16, 128, 128])
    ptrep = din("ptrep", [128, 4 * NG], I32)
    flags = din("flags", [128, 2])
    w_g1 = din("w_g1", [D, DFF]); w_u1 = din("w_u1", [D, DFF]); w_d1 = din("w_d1", [DFF, D])
    w_g2 = din("w_g2", [D, DFF]); w_u2 = din("w_u2", [D, DFF]); w_d2 = din("w_d2", [DFF, D])
    w_in = din("w_in", [D, INC]); w_out = din("w_out", [D, D])
    n_f1 = din("n_f1", [1, D]); n_mix = din("n_mix", [1, D]); n_f2 = din("n_f2", [1, D])
    lb_log = din("lb_log", [2, 512])
    g_hg = din("g_hg", [1, 128]); g_q = din("g_q", [1, 64]); g_k = din("g_k", [1, 64]); g_sub = din("g_sub", [1, 128])
    lam_p = din("lam_p", [4, 64])
    c_ident = din("c_ident", [128, 128]); c_maskT = din("c_maskT", [128, 512]); c_up = din("c_up", [128, 128])
    c_wm = din("c_wm", [128, 128]); c_sel = din("c_sel", [128, 4])
    c_eboff = din("c_eboff", [128, 512]); c_ebdiag = din("c_ebdiag", [128, 512])
    c_cb = din("c_cb", [128, 4 * (NKB + 1)])
    c_biasS = din("c_biasS", [128, NG * 16 * 8]); c_selfb = din("c_selfb", [128, 4])
    c_coff = din("c_coff", [128, 1], I32)
    c_rowsel = din("c_rowsel", [128, 4])
    c_csel = din("c_csel", [128, 2])

    y_own = dout("y_own", [NBO * 128, D]); y_s = dout("y_s", [128, D])
    k_own = dout("k_own", [NBO * 128, 512]); v_own = dout("v_own", [NBO * 128, 512])
    k_s = dout("k_s", [128, 512]); v_s = dout("v_s", [128, 512])
    state_p = dout("state_p", [4, 128, 128]); state_s = dout("state_s", [16, 128, 128])
    h_own = dscr("h_own", [NBO * 128, D]); h_pre = dscr("h_pre", [NBP * 128, D]); h_s = dscr("h_s", [128, D])
    h2_own = dscr("h2_own", [NBO * 128, D]); h2_s = dscr("h2_s", [128, D])

    with ExitStack() as es:
        em = Emitter(nc, es)

        def sb(st, name, shape, dt):
            return st.enter_context(nc.sbuf_tensor(name, shape, dt))

        def dma(q, out, in_, r, w):
            em.dma(q, lambda e: e.dma_start(out=out, in_=in_), r, w)

        def act(out, in_, func, r, w, **kw):
            em.op('act', lambda e: e.activation(out=out, in_=in_, func=func, **kw), r, w)

        def tt(eng, out, in0, in1, op, r, w):
            em.op(eng, lambda e: e.tensor_tensor(out=out, in0=in0, in1=in1, op=op), r, w)

        def ts(eng, out, in0, s1, s2, op0, op1, r, w):
            if s2 is None:
                em.op(eng, lambda e: e.tensor_scalar(out=out, in0=in0, scalar1=s1, scalar2=None, op0=op0), r, w)
            else:
                em.op(eng, lambda e: e.tensor_scalar(out=out, in0=in0, scalar1=s1, scalar2=s2, op0=op0, op1=op1), r, w)

        def stt(eng, out, in0, scalar, in1, op0, op1, r, w):
            em.op(eng, lambda e: e.scalar_tensor_tensor(out=out, in0=in0, scalar=scalar, in1=in1, op0=op0, op1=op1), r, w)

        def cp(eng, out, in_, r, w):
            em.op(eng, lambda e: e.tensor_copy(out=out, in_=in_), r, w)

        def mm(out, lhsT, rhs, start, stop, r, w):
            em.op('pe', lambda e: e.matmul(out, lhsT=lhsT, rhs=rhs, start=start, stop=stop), r, w)

        def tr(out, in_, idn, r, w):
            em.op('pe', lambda e: e.transpose(out=out, in_=in_, identity=idn), r, w)

        def memset(eng, ap, val, w):
            em.op(eng, lambda e: e.memset(ap, val), [], w)

        def red(out, in_, r, w):
            em.op('dve', lambda e: e.tensor_reduce(out=out, in_=in_, axis=AX.X, op=ALU.add), r, w)

        def recip(out, in_, r, w):
            em.op('dve', lambda e: e.reciprocal(out=out, in_=in_), r, w)

        PB = [es.enter_context(nc.psum_tensor("pb%d" % i, [128, 512], F32)) for i in range(8)]
        PBh = [p[:].bitcast(BF16) for p in PB]

        G = es
        identf = sb(G, "identf", [128, 128], F32); ident = sb(G, "ident", [128, 128], BF16)
        onesf = sb(G, "onesf", [128, 128], F32)
        flg = sb(G, "flg", [128, 2], F32)
        lam_t = sb(G, "lam_t", [128, 4], F32)
        dma('sp', identf[:], c_ident, [], ['identf'])
        cp('dve', ident[:], identf[:], ['identf'], ['ident'])
        memset('pool', onesf[:], 1.0, ['onesf'])
        dma('sp', flg[:], flags, [], ['flg'])
        with ExitStack() as t0:
            lp = sb(t0, "lp", [128, 4, 64], F32); lpr = sb(t0, "lpr", [128, 2, 64], F32); ls = sb(t0, "ls", [128, 2], F32)
            for i in range(4):
                dma('sp', lp[:, i, :], lam_p[i:i + 1, :].partition_broadcast(128), [], ['lp%d' % i])
            tt('dve', lpr[:, 0, :], lp[:, 0, :], lp[:, 1, :], ALU.mult, ['lp0', 'lp1'], ['lpr0'])
            tt('dve', lpr[:, 1, :], lp[:, 2, :], lp[:, 3, :], ALU.mult, ['lp2', 'lp3'], ['lpr1'])
            red(ls[:, 0:2], lpr[:], ['lpr0', 'lpr1'], ['ls'])
            act(ls[:], ls[:], AF.Exp, ['ls'], ['ls'])
            stt('dve', lam_t[:, 0:1], ls[:, 0:1], LAM_INIT, ls[:, 1:2], ALU.add, ALU.subtract, ['ls'], ['lam_t'])
            ts('dve', lam_t[:, 1:2], lam_t[:, 0:1], -1.0, None, ALU.mult, None, ['lam_t'], ['lam_t'])
            em.barrier()

        def rms_rstd(ss_ap, n, out_ap, key):
            act(out_ap, ss_ap, AF.Sqrt, [key], [key], scale=1.0 / n, bias=epsc[:, 0:1])
            recip(out_ap, out_ap, [key], [key])

        epsc = sb(G, "epsc", [128, 1], F32)
        memset('pool', epsc[:], EPS, ['epsc'])

        def ffn_phase(tag, tiles, wg_d, wu_d, wd_d, nvec_d):
            with ExitStack() as ph:
                Wg = sb(ph, "Wg" + tag, [128, 8, DFF], BF16)
                Wu = sb(ph, "Wu" + tag, [128, 8, DFF], BF16)
                Wd = sb(ph, "Wd" + tag, [128, NFC, D], BF16)
                gbc = sb(ph, "gbc" + tag, [128, D], F32)
                xt = [sb(ph, "xt%d" % i + tag, [128, TB, D], F32) for i in range(2)]
                xs = [sb(ph, "xs%d" % i + tag, [128, D], BF16) for i in range(2)]
                xnT = sb(ph, "xnT" + tag, [128, 8, TB * 128], BF16)
                sg = [sb(ph, "sg%d" % i + tag, [128, TB * 128], F32) for i in range(2)]
                aT = sb(ph, "aT" + tag, [128, NFC, TB * 128], BF16)
                stat = sb(ph, "stat" + tag, [128, 2 * TB], F32)
                dma('sp', gbc[:], nvec_d.partition_broadcast(128), [], ['gbc'])
                for kc in range(8):
                    dma('pool', Wg[:, kc, :], wg_d[kc * 128:(kc + 1) * 128, :], [], ['Wg%d' % kc])
                    dma('pool', Wu[:, kc, :], wu_d[kc * 128:(kc + 1) * 128, :], [], ['Wu%d' % kc])
                for fc in range(NFC):
                    dma('pool', Wd[:, fc, :], wd_d[fc * 128:(fc + 1) * 128, :], [], ['Wd%d' % fc])
                for ti, (src, dst, nb) in enumerate(tiles):
                    N = nb * 128
                    X = xt[ti % 2]
                    xk = 'xt%d' % (ti % 2)
                    dma('sp', X[:, 0:nb, :], src.rearrange("(b p) d -> p b d", p=128), [], [xk])
                    for b in range(nb):
                        xsb = xs[b % 2]; xsk = 'xs%d' % (b % 2)
                        memset('pool', stat[:, b:b + 1], 0.0, ['stat%d' % b])
                        act(xsb[:], X[:, b, :], AF.Square, [xk, 'stat%d' % b], [xsk, 'stat%d' % b], accum_out=stat[:, b:b + 1])
                        rms_rstd(stat[:, b:b + 1], D, stat[:, TB + b:TB + b + 1], 'stat%d' % b)
                        stt('dve', xsb[:], X[:, b, :], stat[:, TB + b:TB + b + 1], gbc[:], ALU.mult, ALU.mult,
                            [xk, 'stat%d' % b, 'gbc'], [xsk])
                        for kc in range(8):
                            tr(PBh[0][:, kc * 128:(kc + 1) * 128], xsb[:, kc * 128:(kc + 1) * 128], ident[:], [xsk, 'ident'], ['P0'])
                        if b % 2 == 0:
                            cp('dve', xnT[:, :, b * 128:(b + 1) * 128], V3(PBh[0], 8), ['P0'], ['xnT'])
                        else:
                            act(xnT[:, :, b * 128:(b + 1) * 128], V3(PBh[0], 8), AF.Copy, ['P0'], ['xnT'])
                    for fc in range(NFC):
                        pg = 1 + (fc % 2); pu = 3 + (fc % 2)
                        for kc in range(8):
                            mm(PB[pg][:, 0:N], Wg[:, kc, fc * 128:(fc + 1) * 128], xnT[:, kc, 0:N], kc == 0, kc == 7,
                               ['Wg%d' % kc, 'xnT'], ['P%d' % pg])
                        for kc in range(8):
                            mm(PB[pu][:, 0:N], Wu[:, kc, fc * 128:(fc + 1) * 128], xnT[:, kc, 0:N], kc == 0, kc == 7,
                               ['Wu%d' % kc, 'xnT'], ['P%d' % pu])
                        s = sg[fc % 2]; sk = 'sg%d' % (fc % 2)
                        act(s[:, 0:N], PB[pg][:, 0:N], AF.Silu, ['P%d' % pg], [sk])
                        tt('dve', aT[:, fc, 0:N], s[:, 0:N], PB[pu][:, 0:N], ALU.mult, [sk, 'P%d' % pu], ['aT%d' % fc])
                    for b in range(nb):
                        for nh in range(2):
                            pd = 5 + nh
                            for fc in range(NFC):
                                mm(PB[pd][:, :], aT[:, fc, b * 128:(b + 1) * 128], Wd[:, fc, nh * 512:(nh + 1) * 512], fc == 0, fc == NFC - 1,
                                   ['aT%d' % fc, 'Wd%d' % fc], ['P%d' % pd])
                            stt('dve', X[:, b, nh * 512:(nh + 1) * 512], PB[pd][:, :], 0.5, X[:, b, nh * 512:(nh + 1) * 512], ALU.mult, ALU.add,
                                ['P%d' % pd, xk], [xk])
                    dma('act', dst.rearrange("(b p) d -> p b d", p=128), X[:, 0:nb, :], [xk], [])
                em.barrier()

        def tile_list(src, dst, nblk):
            out = []
            b = 0
            while b < nblk:
                nb = min(TB, nblk - b)
                out.append((src[b * 128:(b + nb) * 128, :], dst[b * 128:(b + nb) * 128, :], nb))
                b += nb
            return out

        tilesA = [(x_s, h_s, 1)] + tile_list(x_pre, h_pre, NBP) + tile_list(x_own, h_own, NBO)
        if not SKIP_A:
            ffn_phase("A", tilesA, w_g1, w_u1, w_d1, n_f1)

        with ExitStack() as MB:
          if STAGE >= 2:
            Wout = sb(MB, "Wout", [128, 8, D], BF16)
            for kc in range(8):
                dma('pool', Wout[:, kc, :], w_out[kc * 128:(kc + 1) * 128, :], [], ['Wout%d' % kc])
            omix_s = sb(MB, "omix_s", [128, D], BF16)
            QT_s = sb(MB, "QT_s", [128, 4, 128], BF16)
            KT_s = sb(MB, "KT_s", [128, 4, 128], BF16)
            Vb_s = sb(MB, "Vb_s", [128, 512], BF16)
            OHG = sb(MB, "OHG", [128, NBO, 512], BF16)
            gsub = sb(MB, "gsub", [128, 512], F32)
            csel = sb(MB, "csel", [128, 2], F32)
            dma('sp', csel[:], c_csel, [], ['csel'])
            dma('sp', V3(gsub[:], 4)[:, 0, :], g_sub.partition_broadcast(128), [], ['gsub'])
            ts('dve', V3(gsub[:], 4)[:, 0, :], V3(gsub[:], 4)[:, 0, :], 1.0 - LAM_INIT, None, ALU.mult, None, ['gsub'], ['gsub'])
            for h in range(1, 4):
                cp('dve', V3(gsub[:], 4)[:, h, :], V3(gsub[:], 4)[:, 0, :], ['gsub'], ['gsub'])

            def make_common(st, Wt, wkey):
                C = {}
                gmix = sb(st, "gmix" + wkey, [128, D], F32)
                dma('sp', gmix[:], n_mix.partition_broadcast(128), [], ['gmix'])
                hb = sb(st, "hb" + wkey, [128, D], F32); hs = sb(st, "hs" + wkey, [128, D], BF16)
                hnT = sb(st, "hnT" + wkey, [128, 8, 128], BF16)
                sqt = sb(st, "sqt" + wkey, [128, D], BF16); bst = sb(st, "bst" + wkey, [128, 4], F32)
                nst = sb(st, "nst" + wkey, [128, 16], F32); ntmp = sb(st, "ntmp" + wkey, [128, 512], F32)
                C['hb'] = hb

                def load_norm_block(src_ap):
                    dma('sp', hb[:], src_ap, [], ['hb'])
                    memset('pool', bst[:, 0:1], 0.0, ['bst'])
                    act(sqt[:], hb[:], AF.Square, ['hb', 'bst'], ['sqt', 'bst'], accum_out=bst[:, 0:1])
                    rms_rstd(bst[:, 0:1], D, bst[:, 1:2], 'bst')
                    stt('dve', hs[:], hb[:], bst[:, 1:2], gmix[:], ALU.mult, ALU.mult, ['hb', 'bst', 'gmix'], ['hs'])
                    for kc in range(8):
                        tr(PBh[0][:, kc * 128:(kc + 1) * 128], hs[:, kc * 128:(kc + 1) * 128], ident[:], ['hs', 'ident'], ['P0'])
                    cp('dve', hnT[:], V3(PBh[0], 8), ['P0'], ['hnT'])

                def proj(cg, pbank):
                    for kc in range(8):
                        mm(PB[pbank][:, :], hnT[:, kc, :], Wt[:, kc, cg * 512:(cg + 1) * 512], kc == 0, kc == 7,
                           ['hnT', wkey + '%d' % kc], ['P%d' % pbank])

                def norm_heads(src_ap, nh, dh, gain_ap, out_ap, r, w):
                    act(ntmp[:, 0:nh * dh], src_ap, AF.Square, r, ['nh_tmp'])
                    red(nst[:, 0:nh], ntmp[:, 0:nh * dh].rearrange("p (h d) -> p h d", h=nh), ['nh_tmp'], ['nh_stat'])
                    rms_rstd(nst[:, 0:nh], dh, nst[:, 8:8 + nh], 'nh_stat')
                    s3 = src_ap.rearrange("p (h d) -> p h d", h=nh)
                    g3 = gain_ap.rearrange("p (h d) -> p h d", h=nh)
                    o3 = out_ap.rearrange("p (h d) -> p h d", h=nh)
                    t3 = ntmp[:, 0:nh * dh].rearrange("p (h d) -> p h d", h=nh)
                    tt('dve', t3, s3, nst[:, 8:8 + nh].unsqueeze(2).to_broadcast([128, nh, dh]), ALU.mult, r + ['nh_stat', 'nh_tmp'], ['nh_tmp'])
                    tt('dve', o3, t3, g3, ALU.mult, r + ['nh_tmp'], w)
                C['load'] = load_norm_block; C['proj'] = proj; C['norm_heads'] = norm_heads
                return C

            with ExitStack() as P1:
                Win1 = sb(P1, "Win1", [128, 8, 2048], BF16)
                for kc in range(8):
                    dma('pool', Win1[:, kc, :], w_in[kc * 128:(kc + 1) * 128, 0:2048], [], ['Wa%d' % kc])
                C = make_common(P1, Win1, 'Wa')
                load_norm_block, proj, norm_heads, hb = C['load'], C['proj'], C['norm_heads'], C['hb']
                lbb = sb(P1, "lbb", [128, 512], F32); oml = sb(P1, "oml", [128, 512], F32); ghg = sb(P1, "ghg", [128, 512], F32)
                maskT = sb(P1, "maskT", [128, 512], F32); Up = sb(P1, "Up", [128, 128], F32); Wm = sb(P1, "Wm", [128, 128], F32)
                Sel = sb(P1, "Sel", [128, 4], F32); rowsel = sb(P1, "rowsel", [128, 4], F32)
                for (t, s, k) in ((maskT, c_maskT, 'maskT'), (Up, c_up, 'Up'), (Wm, c_wm, 'Wm'), (Sel, c_sel, 'Sel'), (rowsel, c_rowsel, 'rowsel')):
                    dma('sp', t[:], s, [], [k])
                dma('sp', lbb[:], lb_log[0:1, :].partition_broadcast(128), [], ['lbb'])
                dma('sp', oml[:], lb_log[1:2, :].partition_broadcast(128), [], ['oml'])
                tt('dve', lbb[:], lbb[:], oml[:], ALU.subtract, ['lbb', 'oml'], ['lbb'])
                act(lbb[:], lbb[:], AF.Sigmoid, ['lbb'], ['lbb'])
                ts('dve', oml[:], lbb[:], -1.0, 1.0, ALU.mult, ALU.add, ['lbb'], ['oml'])
                g3_ = V3(ghg[:], 4)
                dma('sp', g3_[:, 0, :], g_hg.partition_broadcast(128), [], ['ghg'])
                for h in range(1, 4):
                    cp('dve', g3_[:, h, :], g3_[:, 0, :], ['ghg'], ['ghg'])
                S = sb(P1, "S", [128, 4, 128], F32)
                memset('pool', S[:], 0.0, ['S0', 'S1', 'S2', 'S3'])
                Sp = [sb(P1, "Sp%d" % c, [128, 4, 128], BF16) for c in range(2)]
                f_t = sb(P1, "f_t", [128, 512], F32); g_t = sb(P1, "g_t", [128, 512], F32); kk = sb(P1, "kk", [128, 512], F32)
                e1 = sb(P1, "e1", [128, 512], F32); e2 = sb(P1, "e2", [128, 512], F32); qs = sb(P1, "qs", [128, 512], F32)
                sx = sb(P1, "sx", [128, 512], F32); xtra = sb(P1, "xtra", [128, 512], F32)
                qtl = sb(P1, "qtl", [128, 512], BF16); ktl = sb(P1, "ktl", [128, 512], BF16)
                khc = [sb(P1, "khat%d" % c, [128, 512], BF16) for c in range(2)]
                vhb = sb(P1, "vhb", [128, 512], BF16)
                qT = sb(P1, "qT", [128, 4, 128], BF16); kT = sb(P1, "kT", [128, 4, 128], BF16)
                qT0 = sb(P1, "qT0", [128, 4, 128], BF16); qT1 = sb(P1, "qT1", [128, 4, 128], BF16)
                memset('pool', qT0[:], 0.0, ['qT0']); memset('pool', qT1[:], 0.0, ['qT1'])
                ecol = sb(P1, "ecol", [128, 4, 4], F32)
                ATm = sb(P1, "ATm", [128, 512], BF16)

                def gate_f():
                    proj(1, 1)
                    act(f_t[:], PB[1][:], AF.Sigmoid, ['P1'], ['f_t'])
                    tt('dve', f_t[:], f_t[:], oml[:], ALU.mult, ['f_t', 'oml'], ['f_t'])
                    tt('dve', f_t[:], f_t[:], lbb[:], ALU.add, ['f_t', 'lbb'], ['f_t'])
                    act(g_t[:], f_t[:], AF.Ln, ['f_t'], ['g_t'])
                    ts('dve', kk[:], f_t[:], -1.0, 1.0, ALU.mult, ALU.add, ['f_t'], ['kk'])

                def hg_out(o_ps_key, o_ps, dst, dkey):
                    proj(3, 1)
                    act(sx[:], PB[1][:], AF.Silu, ['P1'], ['sx'])
                    tt('dve', sx[:], sx[:], ghg[:], ALU.mult, ['sx', 'ghg'], ['sx'])
                    norm_heads(o_ps, 4, 128, sx[:], dst, [o_ps_key, 'sx'], [dkey])

                with ExitStack() as SS:
                    fT = sb(SS, "fT", [128, 4, 128], F32); qTf = sb(SS, "qTf", [128, 4, 128], F32)
                    qTm = [sb(SS, "qTm%d" % j, [128, 4, 128], F32) for j in range(4)]
                    S0t = [sb(SS, "S0t%d" % i, [128, 128], F32) for i in range(2)]
                    Snew = [sb(SS, "Snew%d" % i, [128, 128], F32) for i in range(2)]
                    load_norm_block(h_s)
                    gate_f()
                    proj(0, 2)
                    act(qs[:], PB[2][:], AF.Silu, ['P2'], ['qs'])
                    ts('dve', qs[:], qs[:], 128 ** -0.5, None, ALU.mult, None, ['qs'], ['qs'])
                    proj(2, 3)
                    act(e1[:], PB[3][:], AF.Copy, ['P3'], ['e1'])
                    for h in range(4):
                        tr(PB[4][:, h * 128:(h + 1) * 128], f_t[:, h * 128:(h + 1) * 128], identf[:], ['f_t', 'identf'], ['P4'])
                        tr(PB[5][:, h * 128:(h + 1) * 128], qs[:, h * 128:(h + 1) * 128], identf[:], ['qs', 'identf'], ['P5'])
                    act(fT[:], V3(PB[4][:], 4), AF.Copy, ['P4'], ['fT'])
                    cp('dve', qTf[:], V3(PB[5][:], 4), ['P5'], ['qTf'])
                    for j in range(4):
                        memset('pool', qTm[j][:], 0.0, ['qTm%d' % j])
                        cp('dve', qTm[j][:, :, 32 * j:32 * j + 1], qTf[:, :, 32 * j:32 * j + 1], ['qTf', 'qTm%d' % j], ['qTm%d' % j])
                    kkm = [e2, sx, g_t, xtra]; kkmk = ['e2', 'sx', 'g_t', 'xtra']
                    for j in range(4):
                        ts('dve', kkm[j][:], kk[:], rowsel[:, j:j + 1], None, ALU.mult, None, ['kk', 'rowsel'], [kkmk[j]])
                    for h in range(4):
                        for j in range(4):
                            i2 = (h * 4 + j) % 2
                            dma('sp', S0t[i2][:], state_in[j * 4 + h], [], ['S0t%d' % i2])
                            mm(PB[6][:, 0:128], kkm[j][:, h * 128:(h + 1) * 128], e1[:, h * 128:(h + 1) * 128], True, True,
                               [kkmk[j], 'e1'], ['P6'])
                            stt('dve', Snew[i2][:], S0t[i2][:], fT[:, h, 32 * j:32 * j + 1], PB[6][:, 0:128], ALU.mult, ALU.add,
                                ['S0t%d' % i2, 'fT', 'P6'], ['Snew%d' % i2])
                            dma('act', state_s[j * 4 + h], Snew[i2][:], ['Snew%d' % i2], [])
                            mm(PB[7][:, h * 128:(h + 1) * 128], qTm[j][:, h, :], Snew[i2][:], j == 0, j == 3, ['qTm%d' % j, 'Snew%d' % i2], ['P7'])
                    hg_out('P7', PB[7][:], omix_s[:, 0:512], 'omix_s_hg')
                    em.barrier()

                def hg_block(gb, is_own):
                    li = gb - NBP if is_own else gb
                    load_norm_block((h_own if is_own else h_pre)[li * 128:(li + 1) * 128, :])
                    gate_f()
                    for hh in range(4):
                        if is_own:
                            mm(PB[4][:, hh * 128:(hh + 1) * 128], Up[:], g_t[:, hh * 128:(hh + 1) * 128], True, True, ['Up', 'g_t'], ['P4'])
                        mm(PB[5][:, hh * 128:(hh + 1) * 128], Wm[:], g_t[:, hh * 128:(hh + 1) * 128], True, True, ['Wm', 'g_t'], ['P5'])
                    for h in range(4):
                        mm(PB[6][:, h * 4:(h + 1) * 4], g_t[:, h * 128:(h + 1) * 128], Sel[:], True, True, ['g_t', 'Sel'], ['P6'])
                    act(ecol[:], PB[6][:, 0:16].rearrange("p (h c) -> p h c", h=4), AF.Exp, ['P6'], ['ecol'])
                    act(e2[:], PB[5][:], AF.Exp, ['P5'], ['e2'])
                    for c in range(2):
                        stt('dve', khc[c][:], kk[:], csel[:, c:c + 1], e2[:], ALU.mult, ALU.mult, ['kk', 'e2', 'csel'], ['khat'])
                    proj(2, 3)
                    cp('dve', vhb[:], PB[3][:], ['P3'], ['vhb'])
                    if is_own:
                        act(e1[:], PB[4][:], AF.Exp, ['P4'], ['e1'])
                        act(e2[:], PB[4][:], AF.Exp, ['P4'], ['e2'], scale=-1.0)
                        proj(0, 2)
                        act(qs[:], PB[2][:], AF.Silu, ['P2'], ['qs'])
                        stt('dve', qtl[:], qs[:], 128 ** -0.5, e1[:], ALU.mult, ALU.mult, ['qs', 'e1'], ['qtl'])
                        tt('dve', ktl[:], kk[:], e2[:], ALU.mult, ['kk', 'e2'], ['ktl'])
                        for h in range(4):
                            tr(PBh[0][:, h * 128:(h + 1) * 128], qtl[:, h * 128:(h + 1) * 128], ident[:], ['qtl', 'ident'], ['P0'])
                            tr(PBh[0][:, 512 + h * 128:512 + (h + 1) * 128], ktl[:, h * 128:(h + 1) * 128], ident[:], ['ktl', 'ident'], ['P0'])
                        act(qT[:], V3(PBh[0][:, 0:512], 4), AF.Copy, ['P0'], ['qT'])
                        act(kT[:], V3(PBh[0][:, 512:1024], 4), AF.Copy, ['P0'], ['kT'])
                        cp('dve', qT0[:, :, 0:64], qT[:, :, 0:64], ['qT'], ['qT0'])
                        cp('dve', qT1[:, :, 64:128], qT[:, :, 64:128], ['qT'], ['qT1'])
                        for h in range(4):
                            mm(PB[6][:, h * 128:(h + 1) * 128], kT[:, h, :], qT[:, h, :], True, True, ['kT', 'qT'], ['P6'])
                        tt('dve', ATm[:], PB[6][:], maskT[:], ALU.mult, ['P6', 'maskT'], ['ATm'])
                    SK = ['S0', 'S1', 'S2', 'S3']
                    for c in range(2):
                        pbs = 1 + c
                        for h in range(4):
                            mm(PB[pbs][:, h * 128:(h + 1) * 128], khc[c][:, h * 128:(h + 1) * 128], vhb[:, h * 128:(h + 1) * 128],
                               True, True, ['khat', 'vhb'], ['P%d' % pbs])
                        if is_own:
                            tt('dve', Sp[c][:], S[:], ecol[:, :, 2 * c:2 * c + 1].to_broadcast([128, 4, 128]), ALU.mult,
                               SK + ['ecol'], ['Sp%d_%d' % (c, h) for h in range(4)])
                        tt('dve', S[:], S[:], ecol[:, :, 2 * c + 1:2 * c + 2].to_broadcast([128, 4, 128]), ALU.mult, SK + ['ecol'], SK)
                        tt('dve', S[:], S[:], V3(PB[pbs][:], 4), ALU.add, SK + ['P%d' % pbs], SK)
                    if is_own:
                        for h in range(4):
                            o_h = PB[7][:, h * 128:(h + 1) * 128]
                            mm(o_h, ATm[:, h * 128:(h + 1) * 128], vhb[:, h * 128:(h + 1) * 128], True, False, ['ATm', 'vhb'], ['P7'])
                            mm(o_h, qT0[:, h, :], Sp[0][:, h, :], False, False, ['qT0', 'Sp0_%d' % h], ['P7'])
                            mm(o_h, qT1[:, h, :], Sp[1][:, h, :], False, True, ['qT1', 'Sp1_%d' % h], ['P7'])
                        hg_out('P7', PB[7][:], OHG[:, li, :], 'OHG%d' % li)

                if STAGE >= 3 and not SKIP_P1:
                    for gb in range(NBP):
                        hg_block(gb, False)
                    for h in range(4):
                        ts('dve', S[:, h, :], S[:, h, :], flg[:, 1:2], None, ALU.mult, None, ['S%d' % h, 'flg'], ['S%d' % h])
                    for gb in range(NBP, NKB):
                        hg_block(gb, True)
                    for h in range(4):
                        dma('sp', state_p[h], S[:, h, :], ['S%d' % h], [])
                em.barrier()

            with ExitStack() as P2:
                Win2 = sb(P2, "Win2", [128, 8, 1536], BF16)
                for kc in range(8):
                    dma('pool', Win2[:, kc, :], w_in[kc * 128:(kc + 1) * 128, 2048:3584], [], ['Wb%d' % kc])
                C = make_common(P2, Win2, 'Wb')
                load_norm_block, proj, norm_heads, hb = C['load'], C['proj'], C['norm_heads'], C['hb']
                KT = sb(P2, "KT", [128, 4, NKB * 128], BF16)
                Vb = sb(P2, "Vb", [128, NKB, 4, 132], BF16)
                memset('pool', Vb[:, :, :, 128:132], 1.0, ['Vb_ones'])
                gqb = sb(P2, "gqb", [128, 512], F32); gkb = sb(P2, "gkb", [128, 512], F32)
                EBd = sb(P2, "EBd", [128, 512], F32)
                cbo = sb(P2, "cbo", [128, 4 * (NKB + 1)], F32); cbp = sb(P2, "cbp", [128, 4 * (NKB + 1)], F32)
                for (t, s, k) in ((EBd, c_ebdiag, 'EBd'), (cbo, c_cb, 'cbo')):
                    dma('sp', t[:], s, [], [k])
                ts('dve', cbp[:], cbo[:], flg[:, 0:1], None, ALU.add, None, ['cbo', 'flg'], ['cbp'])
                for (t, s, k) in ((gqb, g_q, 'gqb'), (gkb, g_k, 'gkb')):
                    t3 = t[:].rearrange("p (h d) -> p h d", h=8)
                    dma('sp', t3[:, 0, :], s.partition_broadcast(128), [], [k])
                    for h in range(1, 8):
                        cp('dve', t3[:, h, :], t3[:, 0, :], [k], [k])
                kout = sb(P2, "kout", [128, 512], F32); vout = sb(P2, "vout", [128, 512], F32)
                qn = sb(P2, "qn", [128, 512], BF16); knb = sb(P2, "knb", [128, 512], BF16)
                QT = sb(P2, "QT", [128, 4, 128], BF16)
                QTz = [sb(P2, "QTz%d" % m, [128, 4, 128], BF16) for m in range(2)]
                Eb = [sb(P2, "Eb%d" % i, [128, 256], F32) for i in range(2)]
                Pb = [sb(P2, "Pb%d" % i, [128, 256], BF16) for i in range(2)]
                ofin = sb(P2, "ofin", [128, 512], F32); rz = sb(P2, "rz", [128, 4], F32)
                odab = sb(P2, "odab", [128, 512], BF16); omT = sb(P2, "omT", [128, 8, 128], BF16)

                def da_kv(kdst, vdst, KT_dst, ktkey, Vb_dst, vkey, vb3):
                    proj(1, 2)
                    norm_heads(PB[2][:], 8, 64, gkb[:], kout[:], ['P2', 'gkb'], ['kout'])
                    if kdst is not None:
                        dma('act', kdst, kout[:], ['kout'], [])
                    cp('dve', knb[:], kout[:], ['kout'], ['knb'])
                    for h in range(4):
                        tr(PBh[0][:, h * 128:(h + 1) * 128], knb[:, h * 128:(h + 1) * 128], ident[:], ['knb', 'ident'], ['P0'])
                    act(KT_dst, V3(PBh[0][:, 0:512], 4), AF.Copy, ['P0'], [ktkey])
                    proj(2, 3)
                    act(vout[:], PB[3][:], AF.Copy, ['P3'], ['vout'])
                    if vdst is not None:
                        dma('act', vdst, vout[:], ['vout'], [])
                    cp('dve', Vb_dst, V3(vout[:], 4) if vb3 else vout[:], ['vout', 'Vb_ones'], [vkey])

                def q_T(dst, dkey):
                    proj(0, 2)
                    norm_heads(PB[2][:], 8, 64, gqb[:], qn[:], ['P2', 'gqb'], ['qn'])
                    for h in range(4):
                        tr(PBh[0][:, h * 128:(h + 1) * 128], qn[:, h * 128:(h + 1) * 128], ident[:], ['qn', 'ident'], ['P0'])
                    act(dst, V3(PBh[0][:, 0:512], 4), AF.Copy, ['P0'], [dkey])

                load_norm_block(h_s)
                da_kv(k_s, v_s, KT_s[:], 'KT_s', Vb_s[:], 'Vb_s', False)
                q_T(QT_s[:], 'QT_s')

                def da_block(gb, is_own):
                    li = gb - NBP if is_own else gb
                    load_norm_block((h_own if is_own else h_pre)[li * 128:(li + 1) * 128, :])
                    da_kv(k_own[li * 128:(li + 1) * 128, :] if is_own else None, v_own[li * 128:(li + 1) * 128, :] if is_own else None,
                          KT[:, :, gb * 128:(gb + 1) * 128], 'KT%d' % gb, Vb[:, gb, :, 0:128], 'Vb%d' % gb, True)
                    if not is_own or DA_SUB < 3:
                        return
                    q_T(QT[:], 'QT')
                    if DA_SUB < 4:
                        return
                    for m in range(2):
                        ts('dve', QTz[m][:], QT[:], csel[:, m:m + 1], None, ALU.mult, None, ['QT', 'csel'], ['QTz%d' % m])
                    cnt = 0
                    for h in range(4):
                        po = 3 + 2 * (h % 2)
                        Om = [PB[po][:, 0:132], PB[po + 1][:, 0:132]]
                        pok = ['P%d' % po, 'P%d' % (po + 1)]
                        for kb in range(gb + 1):
                            i2 = cnt % 2; cnt += 1
                            pst = 1 + i2
                            for m in range(2):
                                mm(PB[pst][:, m * 128:(m + 1) * 128], KT[:, h, kb * 128:(kb + 1) * 128], QTz[m][:, h, :],
                                   True, True, ['KT%d' % kb, 'QTz%d' % m], ['P%d' % pst])
                            dl = gb - kb
                            cbt = cbp if kb < NBP else cbo
                            bcol = cbt[:, h * (NKB + 1) + dl:h * (NKB + 1) + dl + 1]
                            if dl == 0:
                                act(Eb[i2][:], PB[pst][:, 0:256], AF.Exp, ['P%d' % pst, 'cbp', 'cbo'], ['Eb%d' % i2], scale=DA_SCALE, bias=bcol)
                                for m in range(2):
                                    tt('dve', Pb[i2][:, m * 128:(m + 1) * 128], Eb[i2][:, m * 128:(m + 1) * 128],
                                       EBd[:, 0:128], ALU.mult, ['Eb%d' % i2, 'EBd'], ['Pb%d' % i2])
                            else:
                                act(Pb[i2][:], PB[pst][:, 0:256], AF.Exp, ['P%d' % pst, 'cbp', 'cbo'], ['Pb%d' % i2], scale=DA_SCALE, bias=bcol)
                            for m in range(2):
                                mm(Om[m], Pb[i2][:, m * 128:(m + 1) * 128], Vb[:, kb, h, 0:132], kb == 0, kb == gb,
                                   ['Pb%d' % i2, 'Vb%d' % kb, 'Vb_ones'], [pok[m]])
                        recip(rz[:, 0:1], Om[0][:, 128:129], [pok[0]], ['rz'])
                        recip(rz[:, 1:2], Om[1][:, 128:129], [pok[1], 'rz'], ['rz'])
                        tt('dve', rz[:, 1:2], rz[:, 1:2], lam_t[:, 1:2], ALU.mult, ['rz', 'lam_t'], ['rz'])
                        ts('dve', ofin[:, h * 128:(h + 1) * 128], Om[0][:, 0:128], rz[:, 0:1], None, ALU.mult, None, [pok[0], 'rz'], ['ofin'])
                        stt('dve', ofin[:, h * 128:(h + 1) * 128], Om[1][:, 0:128], rz[:, 1:2], ofin[:, h * 128:(h + 1) * 128], ALU.mult, ALU.add,
                            [pok[1], 'rz', 'ofin'], ['ofin'])
                    norm_heads(ofin[:], 4, 128, gsub[:], odab[:], ['ofin', 'gsub'], ['odab'])
                    if DA_SUB < 5:
                        return
                    for kc in range(8):
                        src = OHG[:, li, kc * 128:(kc + 1) * 128] if kc < 4 else odab[:, (kc - 4) * 128:(kc - 3) * 128]
                        tr(PBh[0][:, kc * 128:(kc + 1) * 128], src, ident[:], ['OHG%d' % li, 'odab', 'ident'], ['P0'])
                    cp('dve', omT[:], V3(PBh[0], 8), ['P0'], ['omT'])
                    for nh in range(2):
                        pd = 5 + nh
                        for kc in range(8):
                            mm(PB[pd][:, :], omT[:, kc, :], Wout[:, kc, nh * 512:(nh + 1) * 512], kc == 0, kc == 7,
                               ['omT', 'Wout%d' % kc], ['P%d' % pd])
                        tt('dve', hb[:, nh * 512:(nh + 1) * 512], hb[:, nh * 512:(nh + 1) * 512], PB[pd][:, :], ALU.add, ['hb', 'P%d' % pd], ['hb'])
                    dma('act', h2_own[li * 128:(li + 1) * 128, :], hb[:], ['hb'], [])

                if STAGE >= 4 and DA_SUB >= 2:
                    for gb in range(NKB):
                        da_block(gb, gb >= NBP)
                em.barrier()

            with ExitStack() as B2:
                biasS = sb(B2, "biasS", [128, NG * 128], F32); selfb = sb(B2, "selfb", [128, 4], F32)
                dma('sp', biasS[:], c_biasS, [], ['biasS']); dma('sp', selfb[:], c_selfb, [], ['selfb'])
                hb_s = sb(B2, "hb_s", [128, D], F32)
                dma('sp', hb_s[:], h_s, [], ['hb_s'])
                pti = sb(B2, "pti", [128, 4 * NG], I32); cfi = sb(B2, "cfi", [128, 1], I32)
                ptf = sb(B2, "ptf", [128, 4 * NG], F32); cff = sb(B2, "cff", [128, 1], F32); idx = sb(B2, "idx", [128, 4 * NG], I32)
                dma('sp', pti[:], ptrep, [], ['pti']); dma('sp', cfi[:], c_coff, [], ['cfi'])
                cp('dve', ptf[:], pti[:], ['pti'], ['ptf']); cp('dve', cff[:], cfi[:], ['cfi'], ['cff'])
                ts('dve', ptf[:], ptf[:], 128.0, cff[:, 0:1], ALU.mult, ALU.add, ['ptf', 'cff'], ['ptf'])
                cp('dve', idx[:], ptf[:], ['ptf'], ['idx'])
                idxc = []
                for c_ in range(4 * NG):
                    t_ = sb(B2, "idxc%d" % c_, [128, 1], I32)
                    cp('dve', t_[:], idx[:, c_:c_ + 1], ['idx'], ['idxc'])
                    idxc.append(t_)
                Kg = [sb(B2, "Kg%d" % i, [128, 16, 512], BF16) for i in range(2)]
                Vg = [sb(B2, "Vg%d" % i, [128, 16, 512], BF16) for i in range(2)]
                KTt = [sb(B2, "KTt%d" % i, [128, 4, 128], BF16) for i in range(2)]
                qbd = sb(B2, "qbd", [128, 4, 2], BF16)
                E = sb(B2, "E", [128, NT * 8], F32); tmpS = sb(B2, "tmpS", [128, 128], F32)
                A = sb(B2, "A", [128, NT, 4], BF16); Af = sb(B2, "Af", [128, NT], F32)
                zp = sb(B2, "zp", [128, 8], F32); zc = sb(B2, "zc", [128, 16], F32)
                pv = sb(B2, "pv", [4, 512], F32)
                oda = sb(B2, "oda", [128, 512], F32)
                nst2 = sb(B2, "nst2", [128, 16], F32); ntmp2 = sb(B2, "ntmp2", [128, 512], F32)
                omT2 = sb(B2, "omT2", [128, 8, 128], BF16)
                memset('pool', oda[:], 0.0, ['oda'])
                gcount = 0
                for j in range(4 if STAGE >= 5 else 0):
                    for m in range(2):
                        ts('dve', qbd[:, :, m:m + 1], QT_s[:, :, 32 * j:32 * j + 1], csel[:, m:m + 1], None, ALU.mult, None, ['QT_s', 'csel'], ['qbd'])
                    E3 = E[:].rearrange("p (t n) -> p t n", n=8)
                    for g in range(NG):
                        gi = gcount % 2; gcount += 1
                        col = j * NG + g
                        em.dma('pool', (lambda K_, c_: (lambda e: e.indirect_dma_start(
                            out=K_, out_offset=None, in_=cache_k,
                            in_offset=bass.IndirectOffsetOnAxis(ap=idxc[c_][:, :], axis=0))))(Kg[gi][:].rearrange("p a b -> p (a b)"), col),
                            ['idxc'], ['Kg%d' % gi])
                        for i in range(16):
                            t2 = i % 2
                            for h in range(4):
                                tr(PBh[t2][:, h * 128:(h + 1) * 128], Kg[gi][:, i, h * 128:(h + 1) * 128], ident[:], ['Kg%d' % gi, 'ident'], ['P%d' % t2])
                            if i % 2 == 0:
                                act(KTt[t2][:], V3(PBh[t2][:, 0:512], 4), AF.Copy, ['P%d' % t2], ['KTt%d' % t2])
                            else:
                                cp('dve', KTt[t2][:], V3(PBh[t2][:, 0:512], 4), ['P%d' % t2], ['KTt%d' % t2])
                            for h in range(4):
                                mm(PB[2][:, i * 8 + 2 * h:i * 8 + 2 * h + 2], KTt[t2][:, h, :], qbd[:, h, :], True, True, ['KTt%d' % t2, 'qbd'], ['P2'])
                        stt('dve', tmpS[:], PB[2][:, 0:128], DA_SCALE, biasS[:, g * 128:(g + 1) * 128], ALU.mult, ALU.add, ['P2', 'biasS'], ['tmpS'])
                        act(E[:, g * 128:(g + 1) * 128], tmpS[:], AF.Exp, ['tmpS'], ['E'])
                    for h in range(4):
                        mm(PB[2][:, 2 * h:2 * h + 2], KT_s[:, h, :], qbd[:, h, :], True, True, ['KT_s', 'qbd'], ['P2'])
                    ts('dve', tmpS[:, 0:8], PB[2][:, 0:8], DA_SCALE, selfb[:, j:j + 1], ALU.mult, ALU.add, ['P2', 'selfb'], ['tmpS'])
                    act(E[:, NPG * 8:NPG * 8 + 8], tmpS[:, 0:8], AF.Exp, ['tmpS'], ['E'])
                    red(zp[:], E[:].rearrange("p (t n) -> p n t", n=8), ['E'], ['zp'])
                    mm(PB[3][:, 0:8], onesf[:], zp[:], True, True, ['onesf', 'zp'], ['P3'])
                    recip(zc[:, 0:8], PB[3][:, 0:8], ['P3'], ['zc'])
                    for h in range(4):
                        tt('dve', zc[:, 8 + h:9 + h], zc[:, 2 * h + 1:2 * h + 2], lam_t[:, 1:2], ALU.mult, ['zc', 'lam_t'], ['zc'])
                        ts('dve', Af[:], E3[:, :, 2 * h], zc[:, 2 * h:2 * h + 1], None, ALU.mult, None, ['E', 'zc'], ['Af'])
                        stt('dve', A[:, :, h], E3[:, :, 2 * h + 1], zc[:, 8 + h:9 + h], Af[:], ALU.mult, ALU.add, ['E', 'zc', 'Af'], ['A'])
                    for g in range(NG):
                        gi = gcount % 2; gcount += 1
                        col = j * NG + g
                        em.dma('pool', (lambda V_, c_: (lambda e: e.indirect_dma_start(
                            out=V_, out_offset=None, in_=cache_v,
                            in_offset=bass.IndirectOffsetOnAxis(ap=idxc[c_][:, :], axis=0))))(Vg[gi][:].rearrange("p a b -> p (a b)"), col),
                            ['idxc'], ['Vg%d' % gi])
                        for i in range(16):
                            mm(PB[4][0:4, :], A[:, g * 16 + i, :], Vg[gi][:, i, :], g == 0 and i == 0, False, ['A', 'Vg%d' % gi], ['P4'])
                    mm(PB[4][0:4, :], A[:, NPG, :], Vb_s[:], False, True, ['A', 'Vb_s'], ['P4'])
                    act(pv[:], PB[4][0:4, :], AF.Copy, ['P4'], ['pv'])
                    for h in range(4):
                        dma('sp', oda[32 * j:32 * j + 1, h * 128:(h + 1) * 128], pv[h:h + 1, h * 128:(h + 1) * 128], ['pv', 'oda'], ['oda'])
                act(ntmp2[:], oda[:], AF.Square, ['oda'], ['nh2'])
                red(nst2[:, 0:4], V3(ntmp2[:], 4), ['nh2'], ['nst2'])
                rms_rstd(nst2[:, 0:4], 128, nst2[:, 8:12], 'nst2')
                for h in range(4):
                    stt('dve', omix_s[:, 512 + h * 128:512 + (h + 1) * 128], oda[:, h * 128:(h + 1) * 128], nst2[:, 8 + h:9 + h], gsub[:, h * 128:(h + 1) * 128],
                        ALU.mult, ALU.mult, ['oda', 'nst2', 'gsub'], ['omix_s_da'])
                for kc in range(8):
                    tr(PBh[0][:, kc * 128:(kc + 1) * 128], omix_s[:, kc * 128:(kc + 1) * 128], ident[:], ['omix_s_da', 'omix_s_hg', 'ident'], ['P0'])
                cp('dve', omT2[:], V3(PBh[0], 8), ['P0'], ['omT2'])
                for nh in range(2):
                    pd = 5 + nh
                    for kc in range(8):
                        mm(PB[pd][:, :], omT2[:, kc, :], Wout[:, kc, nh * 512:(nh + 1) * 512], kc == 0, kc == 7, ['omT2', 'Wout%d' % kc], ['P%d' % pd])
                    tt('dve', hb_s[:, nh * 512:(nh + 1) * 512], hb_s[:, nh * 512:(nh + 1) * 512], PB[pd][:, :], ALU.add, ['hb_s', 'P%d' % pd], ['hb_s'])
                dma('act', h2_s, hb_s[:], ['hb_s'], [])
                em.barrier()

        tilesC = [(h2_s, y_s, 1)] + tile_list(h2_own, y_own, NBO)
        if STAGE >= 6:
            ffn_phase("C", tilesC, w_g2, w_u2, w_d2, n_f2)

        with nc.Block() as block:
            em.finish(block)
        nc._em_names = em.names
    return nc


def make_consts(NBO, NBP, NPG, past_len):
    NKB = NBO + NBP
    NG = NPG // 16
    c = {}
    c["c_ident"] = np.eye(128, dtype=np.float32)
    s = np.arange(128)[:, None]; t = np.arange(128)[None, :]
    same = (s // 64) == (t // 64)
    c["c_maskT"] = np.tile((same & (s <= t)).astype(np.float32), (1, 4))
    mid = (t // 64) * 64 + 31
    c["c_up"] = (same * ((s <= t).astype(np.float32) - (s <= mid).astype(np.float32))).astype(np.float32)
    c["c_wm"] = (same & (s > t)).astype(np.float32)
    sel = np.zeros((128, 4), np.float32)
    sel[0:32, 0] = 1; sel[0:64, 1] = 1; sel[64:96, 2] = 1; sel[64:128, 3] = 1
    c["c_sel"] = sel
    ki = np.arange(128)[:, None].astype(np.float64); qi = np.arange(128)[None, :].astype(np.float64)
    ebo = np.zeros((128, 4, 128), np.float32); ebd = np.zeros((128, 4, 128), np.float32)
    for h in range(4):
        ebo[:, h, :] = 1.0
        ebd[:, h, :] = (qi >= ki)
    c["c_eboff"] = ebo.reshape(128, 512); c["c_ebdiag"] = ebd.reshape(128, 512)
    cb = np.zeros((128, 4, NKB + 1), np.float32)
    for h in range(4):
        cb[:, h, :] = -SLOPES[h] * 128.0 * np.arange(NKB + 1)[None, :] + SLOPES[h] * np.arange(128)[:, None]
    c["c_cb"] = cb.reshape(128, -1)
    p = np.arange(128)[:, None, None, None]; g = np.arange(NG)[None, :, None, None]
    i = np.arange(16)[None, None, :, None]; n = np.arange(8)[None, None, None, :]
    pos = g * 2048 + 16 * p + i
    sl = np.array(SLOPES)[n // 2]
    c["c_biasS"] = (-(sl * (past_len - pos))).astype(np.float32).reshape(128, NG * 128)
    sb_ = np.full((128, 4), NEG, np.float32)
    for j in range(4):
        sb_[32 * j, j] = 0.0
    c["c_selfb"] = sb_
    c["c_rowsel"] = (sb_ == 0.0).astype(np.float32)
    cs = np.zeros((128, 2), np.float32); cs[0:64, 0] = 1.0; cs[64:128, 1] = 1.0
    c["c_csel"] = cs
    c["c_coff"] = ((np.arange(128) % 8) * 16).astype(np.int32).reshape(128, 1)
    return c


def run(inputs, n_cores=8, debug=False, trace=False):
    xp = np.asarray(inputs["x_prompt"]); xs = np.asarray(inputs["x_sample"])
    B, L, _ = xp.shape
    DB = xs.shape[0]
    assert n_cores == 2 * B and DB == 4 * n_cores
    half = L // 2
    NBO = NBP = half // 128
    pt = np.asarray(inputs["page_table"])
    NPG = pt.shape[1]
    ck = np.asarray(inputs["cache_k"]); cv = np.asarray(inputs["cache_v"])
    NPOOL = ck.shape[1]
    past_len = NPG * 128
    NG = NPG // 16
    nc = build_program(NBO, NBP, NPG, NPOOL, debug=debug)
    consts = make_consts(NBO, NBP, NPG, past_len)
    ck2 = np.ascontiguousarray(ck[0].reshape(NPOOL * 128, 512)); cv2 = np.ascontiguousarray(cv[0].reshape(NPOOL * 128, 512))
    shared = {
        "cache_k": ck2, "cache_v": cv2,
        "w_g1": np.asarray(inputs["ffn1_w_gate"])[0], "w_u1": np.asarray(inputs["ffn1_w_up"])[0], "w_d1": np.asarray(inputs["ffn1_w_down"])[0],
        "w_g2": np.asarray(inputs["ffn2_w_gate"])[0], "w_u2": np.asarray(inputs["ffn2_w_up"])[0], "w_d2": np.asarray(inputs["ffn2_w_down"])[0],
        "w_in": np.asarray(inputs["w_in"])[0], "w_out": np.asarray(inputs["w_out"])[0],
        "n_f1": np.asarray(inputs["ffn1_norm"]), "n_mix": np.asarray(inputs["mix_norm"]), "n_f2": np.asarray(inputs["ffn2_norm"]),
        "lb_log": np.asarray(inputs["hg_lb_logits"]),
        "g_hg": np.asarray(inputs["hg_out_norm"]), "g_q": np.asarray(inputs["da_q_norm"]), "g_k": np.asarray(inputs["da_k_norm"]),
        "g_sub": np.asarray(inputs["da_subln"]),
        "lam_p": np.concatenate([np.asarray(inputs[k]) for k in ("da_lambda_q1", "da_lambda_k1", "da_lambda_q2", "da_lambda_k2")], axis=0),
    }
    shared.update(consts)
    st = np.asarray(inputs["state_hgrn"])[0]
    in_maps = []
    for c in range(n_cores):
        b, hf = c // 2, c % 2
        m = dict(shared)
        m["x_own"] = np.ascontiguousarray(xp[b, hf * half:(hf + 1) * half, :])
        m["x_pre"] = np.ascontiguousarray(xp[b, 0:half, :]) if hf == 1 else np.zeros((half, D), np.float32)
        xsb = np.zeros((128, D), np.float32)
        for j in range(4):
            xsb[32 * j] = xs[4 * c + j, 0]
        m["x_s"] = xsb
        m["state_in"] = np.ascontiguousarray(st[4 * c:4 * c + 4].reshape(16, 128, 128))
        pr = np.zeros((128, 4 * NG), np.int32)
        for j in range(4):
            for g in range(NG):
                pr[:, j * NG + g] = np.repeat(pt[4 * c + j, g * 16:(g + 1) * 16], 8)
        m["ptrep"] = pr
        fl = np.zeros((128, 2), np.float32)
        fl[:, 0] = 0.0 if hf == 1 else NEG
        fl[:, 1] = 1.0 if hf == 1 else 0.0
        m["flags"] = fl
        in_maps.append(m)
    res = run_bass_kernel_spmd(nc, in_maps, core_ids=list(range(n_cores)), **({"trace": True} if trace else {}))
    R = res.results
    y_p = np.zeros((B, L, D), np.float32); k_p = np.zeros((1, B, L, 8, 64), np.float32); v_p = np.zeros((1, B, L, 4, 128), np.float32)
    s_p = np.zeros((1, B, 4, 128, 128), np.float32)
    y_s = np.zeros((DB, 1, D), np.float32); k_s = np.zeros((1, DB, 1, 8, 64), np.float32); v_s = np.zeros((1, DB, 1, 4, 128), np.float32)
    s_s = np.zeros((1, DB, 4, 128, 128), np.float32)
    for c in range(n_cores):
        b, hf = c // 2, c % 2
        r = R[c]
        sl = slice(hf * half, (hf + 1) * half)
        y_p[b, sl] = r["y_own"]; k_p[0, b, sl] = r["k_own"].reshape(half, 8, 64); v_p[0, b, sl] = r["v_own"].reshape(half, 4, 128)
        if hf == 1:
            s_p[0, b] = r["state_p"]
        for j in range(4):
            y_s[4 * c + j, 0] = r["y_s"][32 * j]
            k_s[0, 4 * c + j, 0] = r["k_s"][32 * j].reshape(8, 64)
            v_s[0, 4 * c + j, 0] = r["v_s"][32 * j].reshape(4, 128)
        s_s[0, 4 * c:4 * c + 4] = r["state_s"].reshape(4, 4, 128, 128)
    outs = (y_p, y_s, k_p, v_p, s_p, k_s, v_s, s_s)
    if debug:
        return outs, R
    return outs


def kernel(**inputs):
    return run(inputs)
```

```python
import math
from contextlib import ExitStack
import numpy as np
import concourse.bass as bass
import concourse.mybir as mybir
from concourse.bass_utils import run_bass_kernel_spmd

F32 = mybir.dt.float32
BF16 = mybir.dt.bfloat16
I32 = mybir.dt.int32
AF = mybir.ActivationFunctionType
ALU = mybir.AluOpType
AX = mybir.AxisListType

D = 1024
DFF = 2816
NFC = DFF // 128
INC = 3584
EPS = 1e-6
DA_SCALE = 64 ** -0.5
NEG = -30000.0
SLOPES = [2.0 ** (-8.0 * (h + 1) / 4) for h in range(4)]
LAM_INIT = 0.8 - 0.6 * math.exp(0.0)
STAGE = 9
TB = 4
SKIP_A = False
SKIP_P1 = False
DA_SUB = 9


class Emitter:
    ENG = ('pe', 'act', 'dve', 'pool', 'sp')
    SAME_SYNC = ('act', 'dve', 'pool')
    W = 30000

    def __init__(self, nc, es, n_dma_slots=10, max_ops=200000):
        self.nc = nc
        self.es = es
        self.ops = {e: [] for e in self.ENG}
        self.cnt = {e: 0 for e in self.ENG}
        self.sems = {e: [] for e in self.ENG}
        self.dma_q = ('sp', 'act', 'pool')
        self.nslots = n_dma_slots
        self.dsem = {}
        self.dcnt = {}
        self.dnext = {q: 0 for q in self.dma_q}
        for q in self.dma_q:
            for i in range(n_dma_slots):
                self.dsem[(q, i)] = es.enter_context(nc.semaphore('ds_%s_%d' % (q, i)))
                self.dcnt[(q, i)] = 0
        self.waited = {e: {} for e in self.ENG}
        self.bufs = {}
        self.names = {}

    def _esem(self, e, k):
        i = (k - 1) // self.W
        while len(self.sems[e]) <= i:
            self.sems[e].append(self.es.enter_context(self.nc.semaphore('s_%s_%d' % (e, len(self.sems[e])))))
        return self.sems[e][i], (k - 1) % self.W + 1

    def _semval(self, src, val):
        if isinstance(src, str):
            return self._esem(src, val)
        return self.dsem[src], val

    def _deps(self, e, reads, writes):
        deps = []
        for b in reads:
            st = self.bufs.get(b)
            if st and st['w']:
                deps.append(st['w'])
        for b in writes:
            st = self.bufs.get(b)
            if st:
                if st['w']:
                    deps.append(st['w'])
                deps.extend(st['r'])
        need = {}
        for (src, val) in deps:
            if src == e and e not in self.SAME_SYNC:
                continue
            if self.waited[e].get(src, 0) >= val:
                continue
            need[src] = max(need.get(src, 0), val)
        for src, val in need.items():
            self.waited[e][src] = val
        return list(need.items())

    def _update(self, tok, reads, writes):
        for b in reads:
            st = self.bufs.setdefault(b, {'w': None, 'r': []})
            st['r'].append(tok)
            if len(st['r']) > 48:
                mx = {}
                for (s, v) in st['r']:
                    mx[s] = max(mx.get(s, 0), v)
                st['r'] = list(mx.items())
        for b in writes:
            self.bufs[b] = {'w': tok, 'r': []}

    def op(self, e, fn, reads=(), writes=()):
        waits = self._deps(e, reads, writes)
        self.cnt[e] += 1
        tok = (e, self.cnt[e])
        sem, _ = self._esem(e, self.cnt[e])
        wl = [self._semval(s, v) for s, v in waits]

        desc = (e, tuple(reads), tuple(writes))

        def emit(eng, fn=fn, wl=wl, sem=sem, desc=desc):
            for (s, v) in wl:
                eng.wait_ge(s, v)
            ins = fn(eng)
            try:
                self.names[ins.ins.name] = desc
            except Exception:
                pass
            ins.then_inc(sem, 1)
        self.ops[e].append(emit)
        self._update(tok, reads, writes)
        return tok

    def dma(self, q, fn, reads=(), writes=()):
        waits = self._deps(q, reads, writes)
        i = self.dnext[q]
        self.dnext[q] = (i + 1) % self.nslots
        key = (q, i)
        prev = self.dcnt[key]
        if prev > 0 and self.waited[q].get(key, 0) < prev:
            waits.append((key, prev))
            self.waited[q][key] = prev
        self.dcnt[key] = prev + 16
        tok = (key, prev + 16)
        sem = self.dsem[key]
        wl = [self._semval(s, v) for s, v in waits]

        def emit(eng, fn=fn, wl=wl, sem=sem):
            for (s, v) in wl:
                eng.wait_ge(s, v)
            fn(eng).then_inc(sem, 16)
        self.ops[q].append(emit)
        self._update(tok, reads, writes)
        return tok

    def barrier(self):
        targets = []
        for e in self.ENG:
            if self.cnt[e] > 0:
                targets.append((e, self.cnt[e]))
        for key, v in self.dcnt.items():
            if v > 0:
                targets.append((key, v))
        for e in self.ENG:
            wl = []
            for (src, val) in targets:
                if src == e and e not in self.SAME_SYNC:
                    continue
                if self.waited[e].get(src, 0) >= val:
                    continue
                self.waited[e][src] = val
                wl.append(self._semval(src, val))

            def emit(eng, wl=wl):
                for (s, v) in wl:
                    eng.wait_ge(s, v)
            self.ops[e].append(emit)
        self.bufs = {}

    def finish(self, block):
        self.barrier()
        ops = self.ops

        @block.tensor
        def _(eng):
            for f in ops['pe']:
                f(eng)

        @block.scalar
        def _(eng):
            for f in ops['act']:
                f(eng)

        @block.vector
        def _(eng):
            for f in ops['dve']:
                f(eng)

        @block.gpsimd
        def _(eng):
            for f in ops['pool']:
                f(eng)

        @block.sync
        def _(eng):
            for f in ops['sp']:
                f(eng)


def V3(ap, h):
    return ap.rearrange("p (h d) -> p h d", h=h)


def build_program(NBO, NBP, NPG, NPOOL, debug=False):
    NG = NPG // 16
    NT = NPG + 1
    NKB = NBP + NBO
    nc = bass.Bass("TRN2", target_bir_lowering=False)

    def din(name, shape, dt=F32):
        return nc.dram_tensor(name, shape, dt, kind="ExternalInput").ap()

    def dout(name, shape, dt=F32):
        return nc.dram_tensor(name, shape, dt, kind="ExternalOutput").ap()

    def dscr(name, shape, dt=F32):
        return nc.dram_tensor(name, shape, dt, kind=("ExternalOutput" if debug else "Internal")).ap()

    x_own = din("x_own", [NBO * 128, D]); x_pre = din("x_pre", [NBP * 128, D]); x_s = din("x_s", [128, D])
    cache_k = din("cache_k", [NPOOL * 128, 512]); cache_v = din("cache_v", [NPOOL * 128, 512])
    state_in = din("state_in", [16, 128, 128])
    ptrep = din("ptrep", [128, 4 * NG], I32)
    flags = din("flags", [128, 2])
    w_g1 = din("w_g1", [D, DFF]); w_u1 = din("w_u1", [D, DFF]); w_d1 = din("w_d1", [DFF, D])
    w_g2 = din("w_g2", [D, DFF]); w_u2 = din("w_u2", [D, DFF]); w_d2 = din("w_d2", [DFF, D])
    w_in = din("w_in", [D, INC]); w_out = din("w_out", [D, D])
    n_f1 = din("n_f1", [1, D]); n_mix = din("n_mix", [1, D]); n_f2 = din("n_f2", [1, D])
    lb_log = din("lb_log", [2, 512])
    g_hg = din("g_hg", [1, 128]); g_q = din("g_q", [1, 64]); g_k = din("g_k", [1, 64]); g_sub = din("g_sub", [1, 128])
    lam_p = din("lam_p", [4, 64])
    c_ident = din("c_ident", [128, 128]); c_maskT = din("c_maskT", [128, 512]); c_up = din("c_up", [128, 128])
    c_wm = din("c_wm", [128, 128]); c_sel = din("c_sel", [128, 4])
    c_eboff = din("c_eboff", [128, 512]); c_ebdiag = din("c_ebdiag", [128, 512])
    c_cb = din("c_cb", [128, 4 * (NKB + 1)])
    c_biasS = din("c_biasS", [128, NG * 16 * 8]); c_selfb = din("c_selfb", [128, 4])
    c_coff = din("c_coff", [128, 1], I32)
    c_rowsel = din("c_rowsel", [128, 4])
    c_csel = din("c_csel", [128, 2])

    y_own = dout("y_own", [NBO * 128, D]); y_s = dout("y_s", [128, D])
    k_own = dout("k_own", [NBO * 128, 512]); v_own = dout("v_own", [NBO * 128, 512])
    k_s = dout("k_s", [128, 512]); v_s = dout("v_s", [128, 512])
    state_p = dout("state_p", [4, 128, 128]); state_s = dout("state_s", [16, 128, 128])
    h_own = dscr("h_own", [NBO * 128, D]); h_pre = dscr("h_pre", [NBP * 128, D]); h_s = dscr("h_s", [128, D])
    h2_own = dscr("h2_own", [NBO * 128, D]); h2_s = dscr("h2_s", [128, D])

    with ExitStack() as es:
        em = Emitter(nc, es)

        def sb(st, name, shape, dt):
            return st.enter_context(nc.sbuf_tensor(name, shape, dt))

        def dma(q, out, in_, r, w):
            em.dma(q, lambda e: e.dma_start(out=out, in_=in_), r, w)

        def act(out, in_, func, r, w, **kw):
            em.op('act', lambda e: e.activation(out=out, in_=in_, func=func, **kw), r, w)

        def tt(eng, out, in0, in1, op, r, w):
            em.op(eng, lambda e: e.tensor_tensor(out=out, in0=in0, in1=in1, op=op), r, w)

        def ts(eng, out, in0, s1, s2, op0, op1, r, w):
            if s2 is None:
                em.op(eng, lambda e: e.tensor_scalar(out=out, in0=in0, scalar1=s1, scalar2=None, op0=op0), r, w)
            else:
                em.op(eng, lambda e: e.tensor_scalar(out=out, in0=in0, scalar1=s1, scalar2=s2, op0=op0, op1=op1), r, w)

        def stt(eng, out, in0, scalar, in1, op0, op1, r, w):
            em.op(eng, lambda e: e.scalar_tensor_tensor(out=out, in0=in0, scalar=scalar, in1=in1, op0=op0, op1=op1), r, w)

        def cp(eng, out, in_, r, w):
            em.op(eng, lambda e: e.tensor_copy(out=out, in_=in_), r, w)

        def mm(out, lhsT, rhs, start, stop, r, w):
            em.op('pe', lambda e: e.matmul(out, lhsT=lhsT, rhs=rhs, start=start, stop=stop), r, w)

        def tr(out, in_, idn, r, w):
            em.op('pe', lambda e: e.transpose(out=out, in_=in_, identity=idn), r, w)

        def memset(eng, ap, val, w):
            em.op(eng, lambda e: e.memset(ap, val), [], w)

        def red(out, in_, r, w):
            em.op('dve', lambda e: e.tensor_reduce(out=out, in_=in_, axis=AX.X, op=ALU.add), r, w)

        def recip(out, in_, r, w):
            em.op('dve', lambda e: e.reciprocal(out=out, in_=in_), r, w)

        PB = [es.enter_context(nc.psum_tensor("pb%d" % i, [128, 512], F32)) for i in range(8)]
        PBh = [p[:].bitcast(BF16) for p in PB]

        G = es
        identf = sb(G, "identf", [128, 128], F32); ident = sb(G, "ident", [128, 128], BF16)
        onesf = sb(G, "onesf", [128, 128], F32)
        flg = sb(G, "flg", [128, 2], F32)
        lam_t = sb(G, "lam_t", [128, 4], F32)
        dma('sp', identf[:], c_ident, [], ['identf'])
        cp('dve', ident[:], identf[:], ['identf'], ['ident'])
        memset('pool', onesf[:], 1.0, ['onesf'])
        dma('sp', flg[:], flags, [], ['flg'])
        with ExitStack() as t0:
            lp = sb(t0, "lp", [128, 4, 64], F32); lpr = sb(t0, "lpr", [128, 2, 64], F32); ls = sb(t0, "ls", [128, 2], F32)
            for i in range(4):
                dma('sp', lp[:, i, :], lam_p[i:i + 1, :].partition_broadcast(128), [], ['lp%d' % i])
            tt('dve', lpr[:, 0, :], lp[:, 0, :], lp[:, 1, :], ALU.mult, ['lp0', 'lp1'], ['lpr0'])
            tt('dve', lpr[:, 1, :], lp[:, 2, :], lp[:, 3, :], ALU.mult, ['lp2', 'lp3'], ['lpr1'])
            red(ls[:, 0:2], lpr[:], ['lpr0', 'lpr1'], ['ls'])
            act(ls[:], ls[:], AF.Exp, ['ls'], ['ls'])
            stt('dve', lam_t[:, 0:1], ls[:, 0:1], LAM_INIT, ls[:, 1:2], ALU.add, ALU.subtract, ['ls'], ['lam_t'])
            ts('dve', lam_t[:, 1:2], lam_t[:, 0:1], -1.0, None, ALU.mult, None, ['lam_t'], ['lam_t'])
            em.barrier()

        def rms_rstd(ss_ap, n, out_ap, key):
            act(out_ap, ss_ap, AF.Sqrt, [key], [key], scale=1.0 / n, bias=epsc[:, 0:1])
            recip(out_ap, out_ap, [key], [key])

        epsc = sb(G, "epsc", [128, 1], F32)
        memset('pool', epsc[:], EPS, ['epsc'])

        def ffn_phase(tag, tiles, wg_d, wu_d, wd_d, nvec_d):
            with ExitStack() as ph:
                Wg = sb(ph, "Wg" + tag, [128, 8, DFF], BF16)
                Wu = sb(ph, "Wu" + tag, [128, 8, DFF], BF16)
                Wd = sb(ph, "Wd" + tag, [128, NFC, D], BF16)
                gbc = sb(ph, "gbc" + tag, [128, D], F32)
                xt = [sb(ph, "xt%d" % i + tag, [128, TB, D], F32) for i in range(2)]
                xs = [sb(ph, "xs%d" % i + tag, [128, D], BF16) for i in range(2)]
                xnT = sb(ph, "xnT" + tag, [128, 8, TB * 128], BF16)
                sg = [sb(ph, "sg%d" % i + tag, [128, TB * 128], F32) for i in range(2)]
                aT = sb(ph, "aT" + tag, [128, NFC, TB * 128], BF16)
                stat = sb(ph, "stat" + tag, [128, 2 * TB], F32)
                dma('sp', gbc[:], nvec_d.partition_broadcast(128), [], ['gbc'])
                for kc in range(8):
                    dma('pool', Wg[:, kc, :], wg_d[kc * 128:(kc + 1) * 128, :], [], ['Wg%d' % kc])
                    dma('pool', Wu[:, kc, :], wu_d[kc * 128:(kc + 1) * 128, :], [], ['Wu%d' % kc])
                for fc in range(NFC):
                    dma('pool', Wd[:, fc, :], wd_d[fc * 128:(fc + 1) * 128, :], [], ['Wd%d' % fc])
                def _hdr(ti):
                    src, dst, nb = tiles[ti]
                    return src, dst, nb, nb * 128, xt[ti % 2], 'xt%d' % (ti % 2)

                def p_dma(ti):
                    src, dst, nb, N, X, xk = _hdr(ti)
                    dma('sp', X[:, 0:nb, :], src.rearrange("(b p) d -> p b d", p=128), [], [xk])

                def p_norm(ti, b):
                    src, dst, nb, N, X, xk = _hdr(ti)
                    if b >= nb:
                        return
                    xsb = xs[b % 2]; xsk = 'xs%d' % (b % 2)
                    memset('pool', stat[:, b:b + 1], 0.0, ['stat%d' % b])
                    act(xsb[:], X[:, b, :], AF.Square, [xk, 'stat%d' % b], [xsk, 'stat%d' % b], accum_out=stat[:, b:b + 1])
                    rms_rstd(stat[:, b:b + 1], D, stat[:, TB + b:TB + b + 1], 'stat%d' % b)
                    stt('dve', xsb[:], X[:, b, :], stat[:, TB + b:TB + b + 1], gbc[:], ALU.mult, ALU.mult,
                        [xk, 'stat%d' % b, 'gbc'], [xsk])

                def p_trans(ti, b):
                    src, dst, nb, N, X, xk = _hdr(ti)
                    if b >= nb:
                        return
                    xsb = xs[b % 2]; xsk = 'xs%d' % (b % 2)
                    for kc in range(8):
                        tr(PBh[0][:, kc * 128:(kc + 1) * 128], xsb[:, kc * 128:(kc + 1) * 128], ident[:], [xsk, 'ident'], ['P0'])
                    if b % 2 == 0:
                        cp('dve', xnT[:, :, b * 128:(b + 1) * 128], V3(PBh[0], 8), ['P0'], ['xnT'])
                    else:
                        act(xnT[:, :, b * 128:(b + 1) * 128], V3(PBh[0], 8), AF.Copy, ['P0'], ['xnT'])

                def gateup(ti):
                    src, dst, nb, N, X, xk = _hdr(ti)
                    for fc in range(NFC):
                        pg = 1 + (fc % 2); pu = 3 + (fc % 2)
                        for kc in range(8):
                            mm(PB[pg][:, 0:N], Wg[:, kc, fc * 128:(fc + 1) * 128], xnT[:, kc, 0:N], kc == 0, kc == 7,
                               ['Wg%d' % kc, 'xnT'], ['P%d' % pg])
                        for kc in range(8):
                            mm(PB[pu][:, 0:N], Wu[:, kc, fc * 128:(fc + 1) * 128], xnT[:, kc, 0:N], kc == 0, kc == 7,
                               ['Wu%d' % kc, 'xnT'], ['P%d' % pu])
                        s = sg[fc % 2]; sk = 'sg%d' % (fc % 2)
                        act(s[:, 0:N], PB[pg][:, 0:N], AF.Silu, ['P%d' % pg], [sk])
                        tt('dve', aT[:, fc, 0:N], s[:, 0:N], PB[pu][:, 0:N], ALU.mult, [sk, 'P%d' % pu], ['aT%d' % fc])

                def down_blk(ti, b):
                    src, dst, nb, N, X, xk = _hdr(ti)
                    if b >= nb:
                        return
                    for nh in range(2):
                        pd = 5 + nh
                        for fc in range(NFC):
                            mm(PB[pd][:, :], aT[:, fc, b * 128:(b + 1) * 128], Wd[:, fc, nh * 512:(nh + 1) * 512], fc == 0, fc == NFC - 1,
                               ['aT%d' % fc, 'Wd%d' % fc], ['P%d' % pd])
                        stt('dve', X[:, b, nh * 512:(nh + 1) * 512], PB[pd][:, :], 0.5, X[:, b, nh * 512:(nh + 1) * 512], ALU.mult, ALU.add,
                            ['P%d' % pd, xk], [xk])

                def down_out(ti):
                    src, dst, nb, N, X, xk = _hdr(ti)
                    dma('act', dst.rearrange("(b p) d -> p b d", p=128), X[:, 0:nb, :], [xk], [])

                p_dma(0)
                for b in range(TB):
                    p_norm(0, b); p_trans(0, b)
                for ti in range(len(tiles)):
                    nxt = ti + 1 if ti + 1 < len(tiles) else None
                    if nxt is not None:
                        p_dma(nxt)
                    gateup(ti)
                    if nxt is not None:
                        p_norm(nxt, 0); p_norm(nxt, 1)
                    down_blk(ti, 0)
                    if nxt is not None:
                        p_trans(nxt, 0); p_trans(nxt, 1)
                        p_norm(nxt, 2); p_norm(nxt, 3)
                    down_blk(ti, 1)
                    if nxt is not None:
                        p_trans(nxt, 2); p_trans(nxt, 3)
                    down_blk(ti, 2); down_blk(ti, 3)
                    down_out(ti)
                em.barrier()

        def tile_list(src, dst, nblk):
            out = []
            b = 0
            while b < nblk:
                nb = min(TB, nblk - b)
                out.append((src[b * 128:(b + nb) * 128, :], dst[b * 128:(b + nb) * 128, :], nb))
                b += nb
            return out

        tilesA = [(x_s, h_s, 1)] + tile_list(x_pre, h_pre, NBP) + tile_list(x_own, h_own, NBO)
        if not SKIP_A:
            ffn_phase("A", tilesA, w_g1, w_u1, w_d1, n_f1)

        with ExitStack() as MB:
          if STAGE >= 2:
            Wout = sb(MB, "Wout", [128, 8, D], BF16)
            for kc in range(8):
                dma('pool', Wout[:, kc, :], w_out[kc * 128:(kc + 1) * 128, :], [], ['Wout%d' % kc])
            omix_s = sb(MB, "omix_s", [128, D], BF16)
            QT_s = sb(MB, "QT_s", [128, 4, 128], BF16)
            KT_s = sb(MB, "KT_s", [128, 4, 128], BF16)
            Vb_s = sb(MB, "Vb_s", [128, 512], BF16)
            OHG = sb(MB, "OHG", [128, NBO, 512], BF16)
            gsub = sb(MB, "gsub", [128, 512], F32)
            csel = sb(MB, "csel", [128, 2], F32)
            dma('sp', csel[:], c_csel, [], ['csel'])
            dma('sp', V3(gsub[:], 4)[:, 0, :], g_sub.partition_broadcast(128), [], ['gsub'])
            ts('dve', V3(gsub[:], 4)[:, 0, :], V3(gsub[:], 4)[:, 0, :], 1.0 - LAM_INIT, None, ALU.mult, None, ['gsub'], ['gsub'])
            for h in range(1, 4):
                cp('dve', V3(gsub[:], 4)[:, h, :], V3(gsub[:], 4)[:, 0, :], ['gsub'], ['gsub'])

            def make_common(st, Wt, wkey):
                C = {}
                gmix = sb(st, "gmix" + wkey, [128, D], F32)
                dma('sp', gmix[:], n_mix.partition_broadcast(128), [], ['gmix'])
                hb = sb(st, "hb" + wkey, [128, D], F32); hs = sb(st, "hs" + wkey, [128, D], BF16)
                hnT = sb(st, "hnT" + wkey, [128, 8, 128], BF16)
                sqt = sb(st, "sqt" + wkey, [128, D], BF16); bst = sb(st, "bst" + wkey, [128, 4], F32)
                nst = sb(st, "nst" + wkey, [128, 16], F32); ntmp = sb(st, "ntmp" + wkey, [128, 512], F32)
                C['hb'] = hb

                def load_norm_block(src_ap):
                    dma('sp', hb[:], src_ap, [], ['hb'])
                    memset('pool', bst[:, 0:1], 0.0, ['bst'])
                    act(sqt[:], hb[:], AF.Square, ['hb', 'bst'], ['sqt', 'bst'], accum_out=bst[:, 0:1])
                    rms_rstd(bst[:, 0:1], D, bst[:, 1:2], 'bst')
                    stt('dve', hs[:], hb[:], bst[:, 1:2], gmix[:], ALU.mult, ALU.mult, ['hb', 'bst', 'gmix'], ['hs'])
                    for kc in range(8):
                        tr(PBh[0][:, kc * 128:(kc + 1) * 128], hs[:, kc * 128:(kc + 1) * 128], ident[:], ['hs', 'ident'], ['P0'])
                    cp('dve', hnT[:], V3(PBh[0], 8), ['P0'], ['hnT'])

                def proj(cg, pbank):
                    for kc in range(8):
                        mm(PB[pbank][:, :], hnT[:, kc, :], Wt[:, kc, cg * 512:(cg + 1) * 512], kc == 0, kc == 7,
                           ['hnT', wkey + '%d' % kc], ['P%d' % pbank])

                def norm_heads(src_ap, nh, dh, gain_ap, out_ap, r, w):
                    act(ntmp[:, 0:nh * dh], src_ap, AF.Square, r, ['nh_tmp'])
                    red(nst[:, 0:nh], ntmp[:, 0:nh * dh].rearrange("p (h d) -> p h d", h=nh), ['nh_tmp'], ['nh_stat'])
                    rms_rstd(nst[:, 0:nh], dh, nst[:, 8:8 + nh], 'nh_stat')
                    s3 = src_ap.rearrange("p (h d) -> p h d", h=nh)
                    g3 = gain_ap.rearrange("p (h d) -> p h d", h=nh)
                    o3 = out_ap.rearrange("p (h d) -> p h d", h=nh)
                    t3 = ntmp[:, 0:nh * dh].rearrange("p (h d) -> p h d", h=nh)
                    tt('dve', t3, s3, nst[:, 8:8 + nh].unsqueeze(2).to_broadcast([128, nh, dh]), ALU.mult, r + ['nh_stat', 'nh_tmp'], ['nh_tmp'])
                    tt('dve', o3, t3, g3, ALU.mult, r + ['nh_tmp'], w)
                C['load'] = load_norm_block; C['proj'] = proj; C['norm_heads'] = norm_heads
                return C

            with ExitStack() as P1:
                Win1 = sb(P1, "Win1", [128, 8, 2048], BF16)
                for kc in range(8):
                    dma('pool', Win1[:, kc, :], w_in[kc * 128:(kc + 1) * 128, 0:2048], [], ['Wa%d' % kc])
                C = make_common(P1, Win1, 'Wa')
                load_norm_block, proj, norm_heads, hb = C['load'], C['proj'], C['norm_heads'], C['hb']
                lbb = sb(P1, "lbb", [128, 512], F32); oml = sb(P1, "oml", [128, 512], F32); ghg = sb(P1, "ghg", [128, 512], F32)
                maskT = sb(P1, "maskT", [128, 512], F32); Up = sb(P1, "Up", [128, 128], F32); Wm = sb(P1, "Wm", [128, 128], F32)
                Sel = sb(P1, "Sel", [128, 4], F32); rowsel = sb(P1, "rowsel", [128, 4], F32)
                for (t, s, k) in ((maskT, c_maskT, 'maskT'), (Up, c_up, 'Up'), (Wm, c_wm, 'Wm'), (Sel, c_sel, 'Sel'), (rowsel, c_rowsel, 'rowsel')):
                    dma('sp', t[:], s, [], [k])
                dma('sp', lbb[:], lb_log[0:1, :].partition_broadcast(128), [], ['lbb'])
                dma('sp', oml[:], lb_log[1:2, :].partition_broadcast(128), [], ['oml'])
                tt('dve', lbb[:], lbb[:], oml[:], ALU.subtract, ['lbb', 'oml'], ['lbb'])
                act(lbb[:], lbb[:], AF.Sigmoid, ['lbb'], ['lbb'])
                ts('dve', oml[:], lbb[:], -1.0, 1.0, ALU.mult, ALU.add, ['lbb'], ['oml'])
                g3_ = V3(ghg[:], 4)
                dma('sp', g3_[:, 0, :], g_hg.partition_broadcast(128), [], ['ghg'])
                for h in range(1, 4):
                    cp('dve', g3_[:, h, :], g3_[:, 0, :], ['ghg'], ['ghg'])
                S = sb(P1, "S", [128, 4, 128], F32)
                memset('pool', S[:], 0.0, ['S0', 'S1', 'S2', 'S3'])
                Sp = [sb(P1, "Sp%d" % c, [128, 4, 128], BF16) for c in range(2)]
                f_t = sb(P1, "f_t", [128, 512], F32); g_t = sb(P1, "g_t", [128, 512], F32); kk = sb(P1, "kk", [128, 512], F32)
                e1 = sb(P1, "e1", [128, 512], F32); e2 = sb(P1, "e2", [128, 512], F32); qs = sb(P1, "qs", [128, 512], F32)
                sx = sb(P1, "sx", [128, 512], F32); xtra = sb(P1, "xtra", [128, 512], F32)
                qtl = sb(P1, "qtl", [128, 512], BF16); ktl = sb(P1, "ktl", [128, 512], BF16)
                khc = [sb(P1, "khat%d" % c, [128, 512], BF16) for c in range(2)]
                vhb = sb(P1, "vhb", [128, 512], BF16)
                qT = sb(P1, "qT", [128, 4, 128], BF16); kT = sb(P1, "kT", [128, 4, 128], BF16)
                qT0 = sb(P1, "qT0", [128, 4, 128], BF16); qT1 = sb(P1, "qT1", [128, 4, 128], BF16)
                memset('pool', qT0[:], 0.0, ['qT0']); memset('pool', qT1[:], 0.0, ['qT1'])
                ecol = sb(P1, "ecol", [128, 4, 4], F32)
                ATm = sb(P1, "ATm", [128, 512], BF16)

                def gate_f():
                    proj(1, 1)
                    act(f_t[:], PB[1][:], AF.Sigmoid, ['P1'], ['f_t'])
                    tt('dve', f_t[:], f_t[:], oml[:], ALU.mult, ['f_t', 'oml'], ['f_t'])
                    tt('dve', f_t[:], f_t[:], lbb[:], ALU.add, ['f_t', 'lbb'], ['f_t'])
                    act(g_t[:], f_t[:], AF.Ln, ['f_t'], ['g_t'])
                    ts('dve', kk[:], f_t[:], -1.0, 1.0, ALU.mult, ALU.add, ['f_t'], ['kk'])

                def hg_out(o_ps_key, o_ps, dst, dkey):
                    proj(3, 1)
                    act(sx[:], PB[1][:], AF.Silu, ['P1'], ['sx'])
                    tt('dve', sx[:], sx[:], ghg[:], ALU.mult, ['sx', 'ghg'], ['sx'])
                    norm_heads(o_ps, 4, 128, sx[:], dst, [o_ps_key, 'sx'], [dkey])

                with ExitStack() as SS:
                    fT = sb(SS, "fT", [128, 4, 128], F32); qTf = sb(SS, "qTf", [128, 4, 128], F32)
                    qTm = [sb(SS, "qTm%d" % j, [128, 4, 128], F32) for j in range(4)]
                    S0t = [sb(SS, "S0t%d" % i, [128, 128], F32) for i in range(2)]
                    Snew = [sb(SS, "Snew%d" % i, [128, 128], F32) for i in range(2)]
                    load_norm_block(h_s)
                    gate_f()
                    proj(0, 2)
                    act(qs[:], PB[2][:], AF.Silu, ['P2'], ['qs'])
                    ts('dve', qs[:], qs[:], 128 ** -0.5, None, ALU.mult, None, ['qs'], ['qs'])
                    proj(2, 3)
                    act(e1[:], PB[3][:], AF.Copy, ['P3'], ['e1'])
                    for h in range(4):
                        tr(PB[4][:, h * 128:(h + 1) * 128], f_t[:, h * 128:(h + 1) * 128], identf[:], ['f_t', 'identf'], ['P4'])
                        tr(PB[5][:, h * 128:(h + 1) * 128], qs[:, h * 128:(h + 1) * 128], identf[:], ['qs', 'identf'], ['P5'])
                    act(fT[:], V3(PB[4][:], 4), AF.Copy, ['P4'], ['fT'])
                    cp('dve', qTf[:], V3(PB[5][:], 4), ['P5'], ['qTf'])
                    for j in range(4):
                        memset('pool', qTm[j][:], 0.0, ['qTm%d' % j])
                        cp('dve', qTm[j][:, :, 32 * j:32 * j + 1], qTf[:, :, 32 * j:32 * j + 1], ['qTf', 'qTm%d' % j], ['qTm%d' % j])
                    kkm = [e2, sx, g_t, xtra]; kkmk = ['e2', 'sx', 'g_t', 'xtra']
                    for j in range(4):
                        ts('dve', kkm[j][:], kk[:], rowsel[:, j:j + 1], None, ALU.mult, None, ['kk', 'rowsel'], [kkmk[j]])
                    for h in range(4):
                        for j in range(4):
                            i2 = (h * 4 + j) % 2
                            dma('sp', S0t[i2][:], state_in[j * 4 + h], [], ['S0t%d' % i2])
                            mm(PB[6][:, 0:128], kkm[j][:, h * 128:(h + 1) * 128], e1[:, h * 128:(h + 1) * 128], True, True,
                               [kkmk[j], 'e1'], ['P6'])
                            stt('dve', Snew[i2][:], S0t[i2][:], fT[:, h, 32 * j:32 * j + 1], PB[6][:, 0:128], ALU.mult, ALU.add,
                                ['S0t%d' % i2, 'fT', 'P6'], ['Snew%d' % i2])
                            dma('act', state_s[j * 4 + h], Snew[i2][:], ['Snew%d' % i2], [])
                            mm(PB[7][:, h * 128:(h + 1) * 128], qTm[j][:, h, :], Snew[i2][:], j == 0, j == 3, ['qTm%d' % j, 'Snew%d' % i2], ['P7'])
                    hg_out('P7', PB[7][:], omix_s[:, 0:512], 'omix_s_hg')
                    em.barrier()

                def hg_block(gb, is_own):
                    li = gb - NBP if is_own else gb
                    load_norm_block((h_own if is_own else h_pre)[li * 128:(li + 1) * 128, :])
                    gate_f()
                    for hh in range(4):
                        if is_own:
                            mm(PB[4][:, hh * 128:(hh + 1) * 128], Up[:], g_t[:, hh * 128:(hh + 1) * 128], True, True, ['Up', 'g_t'], ['P4'])
                        mm(PB[5][:, hh * 128:(hh + 1) * 128], Wm[:], g_t[:, hh * 128:(hh + 1) * 128], True, True, ['Wm', 'g_t'], ['P5'])
                    for h in range(4):
                        mm(PB[6][:, h * 4:(h + 1) * 4], g_t[:, h * 128:(h + 1) * 128], Sel[:], True, True, ['g_t', 'Sel'], ['P6'])
                    act(ecol[:], PB[6][:, 0:16].rearrange("p (h c) -> p h c", h=4), AF.Exp, ['P6'], ['ecol'])
                    act(e2[:], PB[5][:], AF.Exp, ['P5'], ['e2'])
                    for c in range(2):
                        stt('dve', khc[c][:], kk[:], csel[:, c:c + 1], e2[:], ALU.mult, ALU.mult, ['kk', 'e2', 'csel'], ['khat'])
                    proj(2, 3)
                    cp('dve', vhb[:], PB[3][:], ['P3'], ['vhb'])
                    if is_own:
                        act(e1[:], PB[4][:], AF.Exp, ['P4'], ['e1'])
                        act(e2[:], PB[4][:], AF.Exp, ['P4'], ['e2'], scale=-1.0)
                        proj(0, 2)
                        act(qs[:], PB[2][:], AF.Silu, ['P2'], ['qs'])
                        stt('dve', qtl[:], qs[:], 128 ** -0.5, e1[:], ALU.mult, ALU.mult, ['qs', 'e1'], ['qtl'])
                        tt('dve', ktl[:], kk[:], e2[:], ALU.mult, ['kk', 'e2'], ['ktl'])
                        for h in range(4):
                            tr(PBh[0][:, h * 128:(h + 1) * 128], qtl[:, h * 128:(h + 1) * 128], ident[:], ['qtl', 'ident'], ['P0'])
                            tr(PBh[0][:, 512 + h * 128:512 + (h + 1) * 128], ktl[:, h * 128:(h + 1) * 128], ident[:], ['ktl', 'ident'], ['P0'])
                        act(qT[:], V3(PBh[0][:, 0:512], 4), AF.Copy, ['P0'], ['qT'])
                        act(kT[:], V3(PBh[0][:, 512:1024], 4), AF.Copy, ['P0'], ['kT'])
                        cp('dve', qT0[:, :, 0:64], qT[:, :, 0:64], ['qT'], ['qT0'])
                        cp('dve', qT1[:, :, 64:128], qT[:, :, 64:128], ['qT'], ['qT1'])
                        for h in range(4):
                            mm(PB[6][:, h * 128:(h + 1) * 128], kT[:, h, :], qT[:, h, :], True, True, ['kT', 'qT'], ['P6'])
                        tt('dve', ATm[:], PB[6][:], maskT[:], ALU.mult, ['P6', 'maskT'], ['ATm'])
                    SK = ['S0', 'S1', 'S2', 'S3']
                    for c in range(2):
                        pbs = 1 + c
                        for h in range(4):
                            mm(PB[pbs][:, h * 128:(h + 1) * 128], khc[c][:, h * 128:(h + 1) * 128], vhb[:, h * 128:(h + 1) * 128],
                               True, True, ['khat', 'vhb'], ['P%d' % pbs])
                        if is_own:
                            tt('dve', Sp[c][:], S[:], ecol[:, :, 2 * c:2 * c + 1].to_broadcast([128, 4, 128]), ALU.mult,
                               SK + ['ecol'], ['Sp%d_%d' % (c, h) for h in range(4)])
                        tt('dve', S[:], S[:], ecol[:, :, 2 * c + 1:2 * c + 2].to_broadcast([128, 4, 128]), ALU.mult, SK + ['ecol'], SK)
                        tt('dve', S[:], S[:], V3(PB[pbs][:], 4), ALU.add, SK + ['P%d' % pbs], SK)
                    if is_own:
                        for h in range(4):
                            o_h = PB[7][:, h * 128:(h + 1) * 128]
                            mm(o_h, ATm[:, h * 128:(h + 1) * 128], vhb[:, h * 128:(h + 1) * 128], True, False, ['ATm', 'vhb'], ['P7'])
                            mm(o_h, qT0[:, h, :], Sp[0][:, h, :], False, False, ['qT0', 'Sp0_%d' % h], ['P7'])
                            mm(o_h, qT1[:, h, :], Sp[1][:, h, :], False, True, ['qT1', 'Sp1_%d' % h], ['P7'])
                        hg_out('P7', PB[7][:], OHG[:, li, :], 'OHG%d' % li)

                if STAGE >= 3 and not SKIP_P1:
                    for gb in range(NBP):
                        hg_block(gb, False)
                    for h in range(4):
                        ts('dve', S[:, h, :], S[:, h, :], flg[:, 1:2], None, ALU.mult, None, ['S%d' % h, 'flg'], ['S%d' % h])
                    for gb in range(NBP, NKB):
                        hg_block(gb, True)
                    for h in range(4):
                        dma('sp', state_p[h], S[:, h, :], ['S%d' % h], [])
                em.barrier()

            with ExitStack() as P2:
                Win2 = sb(P2, "Win2", [128, 8, 1536], BF16)
                for kc in range(8):
                    dma('pool', Win2[:, kc, :], w_in[kc * 128:(kc + 1) * 128, 2048:3584], [], ['Wb%d' % kc])
                C = make_common(P2, Win2, 'Wb')
                load_norm_block, proj, norm_heads, hb = C['load'], C['proj'], C['norm_heads'], C['hb']
                KT = sb(P2, "KT", [128, 4, NKB * 128], BF16)
                Vb = sb(P2, "Vb", [128, NKB, 4, 132], BF16)
                memset('pool', Vb[:, :, :, 128:132], 1.0, ['Vb_ones'])
                gqb = sb(P2, "gqb", [128, 512], F32); gkb = sb(P2, "gkb", [128, 512], F32)
                EBd = sb(P2, "EBd", [128, 512], F32)
                cbo = sb(P2, "cbo", [128, 4 * (NKB + 1)], F32); cbp = sb(P2, "cbp", [128, 4 * (NKB + 1)], F32)
                for (t, s, k) in ((EBd, c_ebdiag, 'EBd'), (cbo, c_cb, 'cbo')):
                    dma('sp', t[:], s, [], [k])
                ts('dve', cbp[:], cbo[:], flg[:, 0:1], None, ALU.add, None, ['cbo', 'flg'], ['cbp'])
                for (t, s, k) in ((gqb, g_q, 'gqb'), (gkb, g_k, 'gkb')):
                    t3 = t[:].rearrange("p (h d) -> p h d", h=8)
                    dma('sp', t3[:, 0, :], s.partition_broadcast(128), [], [k])
                    for h in range(1, 8):
                        cp('dve', t3[:, h, :], t3[:, 0, :], [k], [k])
                kout = sb(P2, "kout", [128, 512], F32); vout = sb(P2, "vout", [128, 512], F32)
                qn = sb(P2, "qn", [128, 512], BF16); knb = sb(P2, "knb", [128, 512], BF16)
                QT = sb(P2, "QT", [128, 4, 128], BF16)
                QTz = [sb(P2, "QTz%d" % m, [128, 4, 128], BF16) for m in range(2)]
                Eb = [sb(P2, "Eb%d" % i, [128, 256], F32) for i in range(2)]
                Pb = [sb(P2, "Pb%d" % i, [128, 256], BF16) for i in range(2)]
                ofin = sb(P2, "ofin", [128, 512], F32); rz = sb(P2, "rz", [128, 4], F32)
                odab = sb(P2, "odab", [128, 512], BF16); omT = sb(P2, "omT", [128, 8, 128], BF16)

                def da_kv(kdst, vdst, KT_dst, ktkey, Vb_dst, vkey, vb3):
                    proj(1, 2)
                    norm_heads(PB[2][:], 8, 64, gkb[:], kout[:], ['P2', 'gkb'], ['kout'])
                    if kdst is not None:
                        dma('act', kdst, kout[:], ['kout'], [])
                    cp('dve', knb[:], kout[:], ['kout'], ['knb'])
                    for h in range(4):
                        tr(PBh[0][:, h * 128:(h + 1) * 128], knb[:, h * 128:(h + 1) * 128], ident[:], ['knb', 'ident'], ['P0'])
                    act(KT_dst, V3(PBh[0][:, 0:512], 4), AF.Copy, ['P0'], [ktkey])
                    proj(2, 3)
                    act(vout[:], PB[3][:], AF.Copy, ['P3'], ['vout'])
                    if vdst is not None:
                        dma('act', vdst, vout[:], ['vout'], [])
                    cp('dve', Vb_dst, V3(vout[:], 4) if vb3 else vout[:], ['vout', 'Vb_ones'], [vkey])

                def q_T(dst, dkey):
                    proj(0, 2)
                    norm_heads(PB[2][:], 8, 64, gqb[:], qn[:], ['P2', 'gqb'], ['qn'])
                    for h in range(4):
                        tr(PBh[0][:, h * 128:(h + 1) * 128], qn[:, h * 128:(h + 1) * 128], ident[:], ['qn', 'ident'], ['P0'])
                    act(dst, V3(PBh[0][:, 0:512], 4), AF.Copy, ['P0'], [dkey])

                load_norm_block(h_s)
                da_kv(k_s, v_s, KT_s[:], 'KT_s', Vb_s[:], 'Vb_s', False)
                q_T(QT_s[:], 'QT_s')

                def da_block(gb, is_own):
                    li = gb - NBP if is_own else gb
                    load_norm_block((h_own if is_own else h_pre)[li * 128:(li + 1) * 128, :])
                    da_kv(k_own[li * 128:(li + 1) * 128, :] if is_own else None, v_own[li * 128:(li + 1) * 128, :] if is_own else None,
                          KT[:, :, gb * 128:(gb + 1) * 128], 'KT%d' % gb, Vb[:, gb, :, 0:128], 'Vb%d' % gb, True)
                    if not is_own or DA_SUB < 3:
                        return
                    q_T(QT[:], 'QT')
                    if DA_SUB < 4:
                        return
                    for m in range(2):
                        ts('dve', QTz[m][:], QT[:], csel[:, m:m + 1], None, ALU.mult, None, ['QT', 'csel'], ['QTz%d' % m])
                    cnt = 0
                    for h in range(4):
                        po = 3 + 2 * (h % 2)
                        Om = [PB[po][:, 0:132], PB[po + 1][:, 0:132]]
                        pok = ['P%d' % po, 'P%d' % (po + 1)]
                        for kb in range(gb + 1):
                            i2 = cnt % 2; cnt += 1
                            pst = 1 + i2
                            for m in range(2):
                                mm(PB[pst][:, m * 128:(m + 1) * 128], KT[:, h, kb * 128:(kb + 1) * 128], QTz[m][:, h, :],
                                   True, True, ['KT%d' % kb, 'QTz%d' % m], ['P%d' % pst])
                            dl = gb - kb
                            cbt = cbp if kb < NBP else cbo
                            bcol = cbt[:, h * (NKB + 1) + dl:h * (NKB + 1) + dl + 1]
                            if dl == 0:
                                act(Eb[i2][:], PB[pst][:, 0:256], AF.Exp, ['P%d' % pst, 'cbp', 'cbo'], ['Eb%d' % i2], scale=DA_SCALE, bias=bcol)
                                for m in range(2):
                                    tt('dve', Pb[i2][:, m * 128:(m + 1) * 128], Eb[i2][:, m * 128:(m + 1) * 128],
                                       EBd[:, 0:128], ALU.mult, ['Eb%d' % i2, 'EBd'], ['Pb%d' % i2])
                            else:
                                act(Pb[i2][:], PB[pst][:, 0:256], AF.Exp, ['P%d' % pst, 'cbp', 'cbo'], ['Pb%d' % i2], scale=DA_SCALE, bias=bcol)
                            for m in range(2):
                                mm(Om[m], Pb[i2][:, m * 128:(m + 1) * 128], Vb[:, kb, h, 0:132], kb == 0, kb == gb,
                                   ['Pb%d' % i2, 'Vb%d' % kb, 'Vb_ones'], [pok[m]])
                        recip(rz[:, 0:1], Om[0][:, 128:129], [pok[0]], ['rz'])
                        recip(rz[:, 1:2], Om[1][:, 128:129], [pok[1], 'rz'], ['rz'])
                        tt('dve', rz[:, 1:2], rz[:, 1:2], lam_t[:, 1:2], ALU.mult, ['rz', 'lam_t'], ['rz'])
                        ts('dve', ofin[:, h * 128:(h + 1) * 128], Om[0][:, 0:128], rz[:, 0:1], None, ALU.mult, None, [pok[0], 'rz'], ['ofin'])
                        stt('dve', ofin[:, h * 128:(h + 1) * 128], Om[1][:, 0:128], rz[:, 1:2], ofin[:, h * 128:(h + 1) * 128], ALU.mult, ALU.add,
                            [pok[1], 'rz', 'ofin'], ['ofin'])
                    norm_heads(ofin[:], 4, 128, gsub[:], odab[:], ['ofin', 'gsub'], ['odab'])
                    if DA_SUB < 5:
                        return
                    for kc in range(8):
                        src = OHG[:, li, kc * 128:(kc + 1) * 128] if kc < 4 else odab[:, (kc - 4) * 128:(kc - 3) * 128]
                        tr(PBh[0][:, kc * 128:(kc + 1) * 128], src, ident[:], ['OHG%d' % li, 'odab', 'ident'], ['P0'])
                    cp('dve', omT[:], V3(PBh[0], 8), ['P0'], ['omT'])
                    for nh in range(2):
                        pd = 5 + nh
                        for kc in range(8):
                            mm(PB[pd][:, :], omT[:, kc, :], Wout[:, kc, nh * 512:(nh + 1) * 512], kc == 0, kc == 7,
                               ['omT', 'Wout%d' % kc], ['P%d' % pd])
                        tt('dve', hb[:, nh * 512:(nh + 1) * 512], hb[:, nh * 512:(nh + 1) * 512], PB[pd][:, :], ALU.add, ['hb', 'P%d' % pd], ['hb'])
                    dma('act', h2_own[li * 128:(li + 1) * 128, :], hb[:], ['hb'], [])

                if STAGE >= 4 and DA_SUB >= 2:
                    for gb in range(NKB):
                        da_block(gb, gb >= NBP)
                em.barrier()

            with ExitStack() as B2:
                biasS = sb(B2, "biasS", [128, NG * 128], F32); selfb = sb(B2, "selfb", [128, 4], F32)
                dma('sp', biasS[:], c_biasS, [], ['biasS']); dma('sp', selfb[:], c_selfb, [], ['selfb'])
                hb_s = sb(B2, "hb_s", [128, D], F32)
                dma('sp', hb_s[:], h_s, [], ['hb_s'])
                pti = sb(B2, "pti", [128, 4 * NG], I32); cfi = sb(B2, "cfi", [128, 1], I32)
                ptf = sb(B2, "ptf", [128, 4 * NG], F32); cff = sb(B2, "cff", [128, 1], F32); idx = sb(B2, "idx", [128, 4 * NG], I32)
                dma('sp', pti[:], ptrep, [], ['pti']); dma('sp', cfi[:], c_coff, [], ['cfi'])
                cp('dve', ptf[:], pti[:], ['pti'], ['ptf']); cp('dve', cff[:], cfi[:], ['cfi'], ['cff'])
                ts('dve', ptf[:], ptf[:], 128.0, cff[:, 0:1], ALU.mult, ALU.add, ['ptf', 'cff'], ['ptf'])
                cp('dve', idx[:], ptf[:], ['ptf'], ['idx'])
                idxc = []
                for c_ in range(4 * NG):
                    t_ = sb(B2, "idxc%d" % c_, [128, 1], I32)
                    cp('dve', t_[:], idx[:, c_:c_ + 1], ['idx'], ['idxc'])
                    idxc.append(t_)
                Kg = [sb(B2, "Kg%d" % i, [128, 16, 512], BF16) for i in range(2)]
                Vg = [sb(B2, "Vg%d" % i, [128, 16, 512], BF16) for i in range(2)]
                KTt = [sb(B2, "KTt%d" % i, [128, 4, 128], BF16) for i in range(2)]
                qbd = sb(B2, "qbd", [128, 4, 2], BF16)
                E = sb(B2, "E", [128, NT * 8], F32); tmpS = sb(B2, "tmpS", [128, 128], F32)
                A = sb(B2, "A", [128, NT, 4], BF16); Af = sb(B2, "Af", [128, NT], F32)
                zp = sb(B2, "zp", [128, 8], F32); zc = sb(B2, "zc", [128, 16], F32)
                pv = sb(B2, "pv", [4, 512], F32)
                oda = sb(B2, "oda", [128, 512], F32)
                nst2 = sb(B2, "nst2", [128, 16], F32); ntmp2 = sb(B2, "ntmp2", [128, 512], F32)
                omT2 = sb(B2, "omT2", [128, 8, 128], BF16)
                memset('pool', oda[:], 0.0, ['oda'])
                gcount = 0
                for j in range(4 if STAGE >= 5 else 0):
                    for m in range(2):
                        ts('dve', qbd[:, :, m:m + 1], QT_s[:, :, 32 * j:32 * j + 1], csel[:, m:m + 1], None, ALU.mult, None, ['QT_s', 'csel'], ['qbd'])
                    E3 = E[:].rearrange("p (t n) -> p t n", n=8)

                    def v_gather(g):
                        em.dma('pool', (lambda V_, c_: (lambda e: e.indirect_dma_start(
                            out=V_, out_offset=None, in_=cache_v,
                            in_offset=bass.IndirectOffsetOnAxis(ap=idxc[c_][:, :], axis=0))))(Vg[g % 2][:].rearrange("p a b -> p (a b)"), j * NG + g),
                            ['idxc'], ['Vg%d' % (g % 2)])
                    NPRE = min(2, NG)
                    for g in range(NPRE):
                        v_gather(g)
                    for g in range(NG):
                        gi = gcount % 2; gcount += 1
                        col = j * NG + g
                        em.dma('pool', (lambda K_, c_: (lambda e: e.indirect_dma_start(
                            out=K_, out_offset=None, in_=cache_k,
                            in_offset=bass.IndirectOffsetOnAxis(ap=idxc[c_][:, :], axis=0))))(Kg[gi][:].rearrange("p a b -> p (a b)"), col),
                            ['idxc'], ['Kg%d' % gi])
                        for i in range(16):
                            t2 = i % 2
                            for h in range(4):
                                tr(PBh[t2][:, h * 128:(h + 1) * 128], Kg[gi][:, i, h * 128:(h + 1) * 128], ident[:], ['Kg%d' % gi, 'ident'], ['P%d' % t2])
                            if i % 2 == 0:
                                act(KTt[t2][:], V3(PBh[t2][:, 0:512], 4), AF.Copy, ['P%d' % t2], ['KTt%d' % t2])
                            else:
                                cp('dve', KTt[t2][:], V3(PBh[t2][:, 0:512], 4), ['P%d' % t2], ['KTt%d' % t2])
                            for h in range(4):
                                mm(PB[2][:, i * 8 + 2 * h:i * 8 + 2 * h + 2], KTt[t2][:, h, :], qbd[:, h, :], True, True, ['KTt%d' % t2, 'qbd'], ['P2'])
                        stt('dve', tmpS[:], PB[2][:, 0:128], DA_SCALE, biasS[:, g * 128:(g + 1) * 128], ALU.mult, ALU.add, ['P2', 'biasS'], ['tmpS'])
                        act(E[:, g * 128:(g + 1) * 128], tmpS[:], AF.Exp, ['tmpS'], ['E'])
                    for h in range(4):
                        mm(PB[2][:, 2 * h:2 * h + 2], KT_s[:, h, :], qbd[:, h, :], True, True, ['KT_s', 'qbd'], ['P2'])
                    ts('dve', tmpS[:, 0:8], PB[2][:, 0:8], DA_SCALE, selfb[:, j:j + 1], ALU.mult, ALU.add, ['P2', 'selfb'], ['tmpS'])
                    act(E[:, NPG * 8:NPG * 8 + 8], tmpS[:, 0:8], AF.Exp, ['tmpS'], ['E'])
                    red(zp[:], E[:].rearrange("p (t n) -> p n t", n=8), ['E'], ['zp'])
                    mm(PB[3][:, 0:8], onesf[:], zp[:], True, True, ['onesf', 'zp'], ['P3'])
                    recip(zc[:, 0:8], PB[3][:, 0:8], ['P3'], ['zc'])
                    for h in range(4):
                        tt('dve', zc[:, 8 + h:9 + h], zc[:, 2 * h + 1:2 * h + 2], lam_t[:, 1:2], ALU.mult, ['zc', 'lam_t'], ['zc'])
                        ts('dve', Af[:], E3[:, :, 2 * h], zc[:, 2 * h:2 * h + 1], None, ALU.mult, None, ['E', 'zc'], ['Af'])
                        stt('dve', A[:, :, h], E3[:, :, 2 * h + 1], zc[:, 8 + h:9 + h], Af[:], ALU.mult, ALU.add, ['E', 'zc', 'Af'], ['A'])
                    for g in range(NG):
                        gi = g % 2
                        if g >= NPRE:
                            v_gather(g)
                        for i in range(16):
                            mm(PB[4][0:4, :], A[:, g * 16 + i, :], Vg[gi][:, i, :], g == 0 and i == 0, False, ['A', 'Vg%d' % gi], ['P4'])
                    mm(PB[4][0:4, :], A[:, NPG, :], Vb_s[:], False, True, ['A', 'Vb_s'], ['P4'])
                    act(pv[:], PB[4][0:4, :], AF.Copy, ['P4'], ['pv'])
                    for h in range(4):
                        dma('sp', oda[32 * j:32 * j + 1, h * 128:(h + 1) * 128], pv[h:h + 1, h * 128:(h + 1) * 128], ['pv', 'oda'], ['oda'])
                act(ntmp2[:], oda[:], AF.Square, ['oda'], ['nh2'])
                red(nst2[:, 0:4], V3(ntmp2[:], 4), ['nh2'], ['nst2'])
                rms_rstd(nst2[:, 0:4], 128, nst2[:, 8:12], 'nst2')
                for h in range(4):
                    stt('dve', omix_s[:, 512 + h * 128:512 + (h + 1) * 128], oda[:, h * 128:(h + 1) * 128], nst2[:, 8 + h:9 + h], gsub[:, h * 128:(h + 1) * 128],
                        ALU.mult, ALU.mult, ['oda', 'nst2', 'gsub'], ['omix_s_da'])
                for kc in range(8):
                    tr(PBh[0][:, kc * 128:(kc + 1) * 128], omix_s[:, kc * 128:(kc + 1) * 128], ident[:], ['omix_s_da', 'omix_s_hg', 'ident'], ['P0'])
                cp('dve', omT2[:], V3(PBh[0], 8), ['P0'], ['omT2'])
                for nh in range(2):
                    pd = 5 + nh
                    for kc in range(8):
                        mm(PB[pd][:, :], omT2[:, kc, :], Wout[:, kc, nh * 512:(nh + 1) * 512], kc == 0, kc == 7, ['omT2', 'Wout%d' % kc], ['P%d' % pd])
                    tt('dve', hb_s[:, nh * 512:(nh + 1) * 512], hb_s[:, nh * 512:(nh + 1) * 512], PB[pd][:, :], ALU.add, ['hb_s', 'P%d' % pd], ['hb_s'])
                dma('act', h2_s, hb_s[:], ['hb_s'], [])
                em.barrier()

        tilesC = [(h2_s, y_s, 1)] + tile_list(h2_own, y_own, NBO)
        if STAGE >= 6:
            ffn_phase("C", tilesC, w_g2, w_u2, w_d2, n_f2)

        with nc.Block() as block:
            em.finish(block)
        nc._em_names = em.names
    return nc


def make_consts(NBO, NBP, NPG, past_len):
    NKB = NBO + NBP
    NG = NPG // 16
    c = {}
    c["c_ident"] = np.eye(128, dtype=np.float32)
    s = np.arange(128)[:, None]; t = np.arange(128)[None, :]
    same = (s // 64) == (t // 64)
    c["c_maskT"] = np.tile((same & (s <= t)).astype(np.float32), (1, 4))
    mid = (t // 64) * 64 + 31
    c["c_up"] = (same * ((s <= t).astype(np.float32) - (s <= mid).astype(np.float32))).astype(np.float32)
    c["c_wm"] = (same & (s > t)).astype(np.float32)
    sel = np.zeros((128, 4), np.float32)
    sel[0:32, 0] = 1; sel[0:64, 1] = 1; sel[64:96, 2] = 1; sel[64:128, 3] = 1
    c["c_sel"] = sel
    ki = np.arange(128)[:, None].astype(np.float64); qi = np.arange(128)[None, :].astype(np.float64)
    ebo = np.zeros((128, 4, 128), np.float32); ebd = np.zeros((128, 4, 128), np.float32)
    for h in range(4):
        ebo[:, h, :] = 1.0
        ebd[:, h, :] = (qi >= ki)
    c["c_eboff"] = ebo.reshape(128, 512); c["c_ebdiag"] = ebd.reshape(128, 512)
    cb = np.zeros((128, 4, NKB + 1), np.float32)
    for h in range(4):
        cb[:, h, :] = -SLOPES[h] * 128.0 * np.arange(NKB + 1)[None, :] + SLOPES[h] * np.arange(128)[:, None]
    c["c_cb"] = cb.reshape(128, -1)
    p = np.arange(128)[:, None, None, None]; g = np.arange(NG)[None, :, None, None]
    i = np.arange(16)[None, None, :, None]; n = np.arange(8)[None, None, None, :]
    pos = g * 2048 + 16 * p + i
    sl = np.array(SLOPES)[n // 2]
    c["c_biasS"] = (-(sl * (past_len - pos))).astype(np.float32).reshape(128, NG * 128)
    sb_ = np.full((128, 4), NEG, np.float32)
    for j in range(4):
        sb_[32 * j, j] = 0.0
    c["c_selfb"] = sb_
    c["c_rowsel"] = (sb_ == 0.0).astype(np.float32)
    cs = np.zeros((128, 2), np.float32); cs[0:64, 0] = 1.0; cs[64:128, 1] = 1.0
    c["c_csel"] = cs
    c["c_coff"] = ((np.arange(128) % 8) * 16).astype(np.int32).reshape(128, 1)
    return c


def run(inputs, n_cores=8, debug=False, trace=False):
    xp = np.asarray(inputs["x_prompt"]); xs = np.asarray(inputs["x_sample"])
    B, L, _ = xp.shape
    DB = xs.shape[0]
    assert n_cores == 2 * B and DB == 4 * n_cores
    half = L // 2
    NBO = NBP = half // 128
    pt = np.asarray(inputs["page_table"])
    NPG = pt.shape[1]
    ck = np.asarray(inputs["cache_k"]); cv = np.asarray(inputs["cache_v"])
    NPOOL = ck.shape[1]
    past_len = NPG * 128
    NG = NPG // 16
    nc = build_program(NBO, NBP, NPG, NPOOL, debug=debug)
    consts = make_consts(NBO, NBP, NPG, past_len)
    ck2 = np.ascontiguousarray(ck[0].reshape(NPOOL * 128, 512)); cv2 = np.ascontiguousarray(cv[0].reshape(NPOOL * 128, 512))
    shared = {
        "cache_k": ck2, "cache_v": cv2,
        "w_g1": np.asarray(inputs["ffn1_w_gate"])[0], "w_u1": np.asarray(inputs["ffn1_w_up"])[0], "w_d1": np.asarray(inputs["ffn1_w_down"])[0],
        "w_g2": np.asarray(inputs["ffn2_w_gate"])[0], "w_u2": np.asarray(inputs["ffn2_w_up"])[0], "w_d2": np.asarray(inputs["ffn2_w_down"])[0],
        "w_in": np.asarray(inputs["w_in"])[0], "w_out": np.asarray(inputs["w_out"])[0],
        "n_f1": np.asarray(inputs["ffn1_norm"]), "n_mix": np.asarray(inputs["mix_norm"]), "n_f2": np.asarray(inputs["ffn2_norm"]),
        "lb_log": np.asarray(inputs["hg_lb_logits"]),
        "g_hg": np.asarray(inputs["hg_out_norm"]), "g_q": np.asarray(inputs["da_q_norm"]), "g_k": np.asarray(inputs["da_k_norm"]),
        "g_sub": np.asarray(inputs["da_subln"]),
        "lam_p": np.concatenate([np.asarray(inputs[k]) for k in ("da_lambda_q1", "da_lambda_k1", "da_lambda_q2", "da_lambda_k2")], axis=0),
    }
    shared.update(consts)
    st = np.asarray(inputs["state_hgrn"])[0]
    in_maps = []
    for c in range(n_cores):
        b, hf = c // 2, c % 2
        m = dict(shared)
        m["x_own"] = np.ascontiguousarray(xp[b, hf * half:(hf + 1) * half, :])
        m["x_pre"] = np.ascontiguousarray(xp[b, 0:half, :]) if hf == 1 else np.zeros((half, D), np.float32)
        xsb = np.zeros((128, D), np.float32)
        for j in range(4):
            xsb[32 * j] = xs[4 * c + j, 0]
        m["x_s"] = xsb
        m["state_in"] = np.ascontiguousarray(st[4 * c:4 * c + 4].reshape(16, 128, 128))
        pr = np.zeros((128, 4 * NG), np.int32)
        for j in range(4):
            for g in range(NG):
                pr[:, j * NG + g] = np.repeat(pt[4 * c + j, g * 16:(g + 1) * 16], 8)
        m["ptrep"] = pr
        fl = np.zeros((128, 2), np.float32)
        fl[:, 0] = 0.0 if hf == 1 else NEG
        fl[:, 1] = 1.0 if hf == 1 else 0.0
        m["flags"] = fl
        in_maps.append(m)
    res = run_bass_kernel_spmd(nc, in_maps, core_ids=list(range(n_cores)), **({"trace": True} if trace else {}))
    R = res.results
    y_p = np.zeros((B, L, D), np.float32); k_p = np.zeros((1, B, L, 8, 64), np.float32); v_p = np.zeros((1, B, L, 4, 128), np.float32)
    s_p = np.zeros((1, B, 4, 128, 128), np.float32)
    y_s = np.zeros((DB, 1, D), np.float32); k_s = np.zeros((1, DB, 1, 8, 64), np.float32); v_s = np.zeros((1, DB, 1, 4, 128), np.float32)
    s_s = np.zeros((1, DB, 4, 128, 128), np.float32)
    for c in range(n_cores):
        b, hf = c // 2, c % 2
        r = R[c]
        sl = slice(hf * half, (hf + 1) * half)
        y_p[b, sl] = r["y_own"]; k_p[0, b, sl] = r["k_own"].reshape(half, 8, 64); v_p[0, b, sl] = r["v_own"].reshape(half, 4, 128)
        if hf == 1:
            s_p[0, b] = r["state_p"]
        for j in range(4):
            y_s[4 * c + j, 0] = r["y_s"][32 * j]
            k_s[0, 4 * c + j, 0] = r["k_s"][32 * j].reshape(8, 64)
            v_s[0, 4 * c + j, 0] = r["v_s"][32 * j].reshape(4, 128)
        s_s[0, 4 * c:4 * c + 4] = r["state_s"].reshape(4, 4, 128, 128)
    outs = (y_p, y_s, k_p, v_p, s_p, k_s, v_s, s_s)
    if debug:
        return outs, R
    return outs


def kernel(**inputs):
    return run(inputs)
```

```python
import math
from contextlib import ExitStack
import numpy as np
import concourse.bass as bass
import concourse.mybir as mybir
from concourse.bass_utils import run_bass_kernel_spmd

F32 = mybir.dt.float32
BF16 = mybir.dt.bfloat16
I32 = mybir.dt.int32
AF = mybir.ActivationFunctionType
ALU = mybir.AluOpType
AX = mybir.AxisListType

D = 1024
DFF = 2816
NFC = DFF // 128
INC = 3584
EPS = 1e-6
DA_SCALE = 64 ** -0.5
NEG = -30000.0
SLOPES = [2.0 ** (-8.0 * (h + 1) / 4) for h in range(4)]
LAM_INIT = 0.8 - 0.6 * math.exp(0.0)
STAGE = 9
DLMAX = [4, 15, 1 << 30, 1 << 30]
TB = 4
SKIP_A = False
SKIP_P1 = False
DA_SUB = 9


class Emitter:
    ENG = ('pe', 'act', 'dve', 'pool', 'sp')
    SAME_SYNC = ('act', 'dve', 'pool')
    W = 30000

    def __init__(self, nc, es, n_dma_slots=10, max_ops=200000):
        self.nc = nc
        self.es = es
        self.ops = {e: [] for e in self.ENG}
        self.cnt = {e: 0 for e in self.ENG}
        self.sems = {e: [] for e in self.ENG}
        self.dma_q = ('sp', 'act', 'pool')
        self.nslots = n_dma_slots
        self.dsem = {}
        self.dcnt = {}
        self.dnext = {q: 0 for q in self.dma_q}
        for q in self.dma_q:
            for i in range(n_dma_slots):
                self.dsem[(q, i)] = es.enter_context(nc.semaphore('ds_%s_%d' % (q, i)))
                self.dcnt[(q, i)] = 0
        self.waited = {e: {} for e in self.ENG}
        self.bufs = {}
        self.names = {}

    def _esem(self, e, k):
        i = (k - 1) // self.W
        while len(self.sems[e]) <= i:
            self.sems[e].append(self.es.enter_context(self.nc.semaphore('s_%s_%d' % (e, len(self.sems[e])))))
        return self.sems[e][i], (k - 1) % self.W + 1

    def _semval(self, src, val):
        if isinstance(src, str):
            return self._esem(src, val)
        return self.dsem[src], val

    def _deps(self, e, reads, writes):
        deps = []
        for b in reads:
            st = self.bufs.get(b)
            if st and st['w']:
                deps.append(st['w'])
        for b in writes:
            st = self.bufs.get(b)
            if st:
                if st['w']:
                    deps.append(st['w'])
                deps.extend(st['r'])
        need = {}
        for (src, val) in deps:
            if src == e and e not in self.SAME_SYNC:
                continue
            if self.waited[e].get(src, 0) >= val:
                continue
            need[src] = max(need.get(src, 0), val)
        for src, val in need.items():
            self.waited[e][src] = val
        return list(need.items())

    def _update(self, tok, reads, writes):
        for b in reads:
            st = self.bufs.setdefault(b, {'w': None, 'r': []})
            st['r'].append(tok)
            if len(st['r']) > 48:
                mx = {}
                for (s, v) in st['r']:
                    mx[s] = max(mx.get(s, 0), v)
                st['r'] = list(mx.items())
        for b in writes:
            self.bufs[b] = {'w': tok, 'r': []}

    def op(self, e, fn, reads=(), writes=()):
        waits = self._deps(e, reads, writes)
        self.cnt[e] += 1
        tok = (e, self.cnt[e])
        sem, _ = self._esem(e, self.cnt[e])
        wl = [self._semval(s, v) for s, v in waits]

        desc = (e, tuple(reads), tuple(writes))

        def emit(eng, fn=fn, wl=wl, sem=sem, desc=desc):
            for (s, v) in wl:
                eng.wait_ge(s, v)
            ins = fn(eng)
            try:
                self.names[ins.ins.name] = desc
            except Exception:
                pass
            ins.then_inc(sem, 1)
        self.ops[e].append(emit)
        self._update(tok, reads, writes)
        return tok

    def dma(self, q, fn, reads=(), writes=()):
        waits = self._deps(q, reads, writes)
        i = self.dnext[q]
        self.dnext[q] = (i + 1) % self.nslots
        key = (q, i)
        prev = self.dcnt[key]
        if prev > 0 and self.waited[q].get(key, 0) < prev:
            waits.append((key, prev))
            self.waited[q][key] = prev
        self.dcnt[key] = prev + 16
        tok = (key, prev + 16)
        sem = self.dsem[key]
        wl = [self._semval(s, v) for s, v in waits]

        def emit(eng, fn=fn, wl=wl, sem=sem):
            for (s, v) in wl:
                eng.wait_ge(s, v)
            fn(eng).then_inc(sem, 16)
        self.ops[q].append(emit)
        self._update(tok, reads, writes)
        return tok

    def barrier(self):
        targets = []
        for e in self.ENG:
            if self.cnt[e] > 0:
                targets.append((e, self.cnt[e]))
        for key, v in self.dcnt.items():
            if v > 0:
                targets.append((key, v))
        for e in self.ENG:
            wl = []
            for (src, val) in targets:
                if src == e and e not in self.SAME_SYNC:
                    continue
                if self.waited[e].get(src, 0) >= val:
                    continue
                self.waited[e][src] = val
                wl.append(self._semval(src, val))

            def emit(eng, wl=wl):
                for (s, v) in wl:
                    eng.wait_ge(s, v)
            self.ops[e].append(emit)
        self.bufs = {}

    def finish(self, block):
        self.barrier()
        ops = self.ops

        @block.tensor
        def _(eng):
            for f in ops['pe']:
                f(eng)

        @block.scalar
        def _(eng):
            for f in ops['act']:
                f(eng)

        @block.vector
        def _(eng):
            for f in ops['dve']:
                f(eng)

        @block.gpsimd
        def _(eng):
            for f in ops['pool']:
                f(eng)

        @block.sync
        def _(eng):
            for f in ops['sp']:
                f(eng)


def V3(ap, h):
    return ap.rearrange("p (h d) -> p h d", h=h)


def build_program(NBO, NBP, NPG, NPOOL, debug=False):
    NG = NPG // 16
    NT = NPG + 1
    NKB = NBP + NBO
    nc = bass.Bass("TRN2", target_bir_lowering=False)

    def din(name, shape, dt=F32):
        return nc.dram_tensor(name, shape, dt, kind="ExternalInput").ap()

    def dout(name, shape, dt=F32):
        return nc.dram_tensor(name, shape, dt, kind="ExternalOutput").ap()

    def dscr(name, shape, dt=F32):
        return nc.dram_tensor(name, shape, dt, kind=("ExternalOutput" if debug else "Internal")).ap()

    x_own = din("x_own", [NBO * 128, D]); x_pre = din("x_pre", [NBP * 128, D]); x_s = din("x_s", [128, D])
    cache_k = din("cache_k", [NPOOL * 128, 512]); cache_v = din("cache_v", [NPOOL * 128, 512])
    state_in = din("state_in", [16, 128, 128])
    ptrep = din("ptrep", [128, 4 * NG], I32)
    flags = din("flags", [128, 2])
    w_g1 = din("w_g1", [D, DFF]); w_u1 = din("w_u1", [D, DFF]); w_d1 = din("w_d1", [DFF, D])
    w_g2 = din("w_g2", [D, DFF]); w_u2 = din("w_u2", [D, DFF]); w_d2 = din("w_d2", [DFF, D])
    w_in = din("w_in", [D, INC]); w_out = din("w_out", [D, D])
    n_f1 = din("n_f1", [1, D]); n_mix = din("n_mix", [1, D]); n_f2 = din("n_f2", [1, D])
    lb_log = din("lb_log", [2, 512])
    g_hg = din("g_hg", [1, 128]); g_q = din("g_q", [1, 64]); g_k = din("g_k", [1, 64]); g_sub = din("g_sub", [1, 128])
    lam_p = din("lam_p", [4, 64])
    c_ident = din("c_ident", [128, 128]); c_maskT = din("c_maskT", [128, 512]); c_up = din("c_up", [128, 128])
    c_wm = din("c_wm", [128, 128]); c_sel = din("c_sel", [128, 4])
    c_eboff = din("c_eboff", [128, 512]); c_ebdiag = din("c_ebdiag", [128, 512])
    c_cb = din("c_cb", [128, 4 * (NKB + 1)])
    c_biasS = din("c_biasS", [128, NG * 16 * 8]); c_selfb = din("c_selfb", [128, 4])
    c_coff = din("c_coff", [128, 1], I32)
    c_rowsel = din("c_rowsel", [128, 4])
    c_csel = din("c_csel", [128, 2])

    y_own = dout("y_own", [NBO * 128, D]); y_s = dout("y_s", [128, D])
    k_own = dout("k_own", [NBO * 128, 512]); v_own = dout("v_own", [NBO * 128, 512])
    k_s = dout("k_s", [128, 512]); v_s = dout("v_s", [128, 512])
    state_p = dout("state_p", [4, 128, 128]); state_s = dout("state_s", [16, 128, 128])
    h_own = dscr("h_own", [NBO * 128, D]); h_pre = dscr("h_pre", [NBP * 128, D]); h_s = dscr("h_s", [128, D])
    h2_own = dscr("h2_own", [NBO * 128, D]); h2_s = dscr("h2_s", [128, D])

    with ExitStack() as es:
        em = Emitter(nc, es)

        def sb(st, name, shape, dt):
            return st.enter_context(nc.sbuf_tensor(name, shape, dt))

        def dma(q, out, in_, r, w):
            em.dma(q, lambda e: e.dma_start(out=out, in_=in_), r, w)

        def act(out, in_, func, r, w, **kw):
            em.op('act', lambda e: e.activation(out=out, in_=in_, func=func, **kw), r, w)

        def tt(eng, out, in0, in1, op, r, w):
            em.op(eng, lambda e: e.tensor_tensor(out=out, in0=in0, in1=in1, op=op), r, w)

        def ts(eng, out, in0, s1, s2, op0, op1, r, w):
            if s2 is None:
                em.op(eng, lambda e: e.tensor_scalar(out=out, in0=in0, scalar1=s1, scalar2=None, op0=op0), r, w)
            else:
                em.op(eng, lambda e: e.tensor_scalar(out=out, in0=in0, scalar1=s1, scalar2=s2, op0=op0, op1=op1), r, w)

        def stt(eng, out, in0, scalar, in1, op0, op1, r, w):
            em.op(eng, lambda e: e.scalar_tensor_tensor(out=out, in0=in0, scalar=scalar, in1=in1, op0=op0, op1=op1), r, w)

        def cp(eng, out, in_, r, w):
            em.op(eng, lambda e: e.tensor_copy(out=out, in_=in_), r, w)

        def mm(out, lhsT, rhs, start, stop, r, w):
            em.op('pe', lambda e: e.matmul(out, lhsT=lhsT, rhs=rhs, start=start, stop=stop), r, w)

        def tr(out, in_, idn, r, w):
            em.op('pe', lambda e: e.transpose(out=out, in_=in_, identity=idn), r, w)

        def memset(eng, ap, val, w):
            em.op(eng, lambda e: e.memset(ap, val), [], w)

        def red(out, in_, r, w):
            em.op('dve', lambda e: e.tensor_reduce(out=out, in_=in_, axis=AX.X, op=ALU.add), r, w)

        def recip(out, in_, r, w):
            em.op('dve', lambda e: e.reciprocal(out=out, in_=in_), r, w)

        PB = [es.enter_context(nc.psum_tensor("pb%d" % i, [128, 512], F32)) for i in range(8)]
        PBh = [p[:].bitcast(BF16) for p in PB]

        G = es
        identf = sb(G, "identf", [128, 128], F32); ident = sb(G, "ident", [128, 128], BF16)
        onesf = sb(G, "onesf", [128, 128], F32)
        flg = sb(G, "flg", [128, 2], F32)
        lam_t = sb(G, "lam_t", [128, 4], F32)
        dma('sp', identf[:], c_ident, [], ['identf'])
        cp('dve', ident[:], identf[:], ['identf'], ['ident'])
        memset('pool', onesf[:], 1.0, ['onesf'])
        dma('sp', flg[:], flags, [], ['flg'])
        with ExitStack() as t0:
            lp = sb(t0, "lp", [128, 4, 64], F32); lpr = sb(t0, "lpr", [128, 2, 64], F32); ls = sb(t0, "ls", [128, 2], F32)
            for i in range(4):
                dma('sp', lp[:, i, :], lam_p[i:i + 1, :].partition_broadcast(128), [], ['lp%d' % i])
            tt('dve', lpr[:, 0, :], lp[:, 0, :], lp[:, 1, :], ALU.mult, ['lp0', 'lp1'], ['lpr0'])
            tt('dve', lpr[:, 1, :], lp[:, 2, :], lp[:, 3, :], ALU.mult, ['lp2', 'lp3'], ['lpr1'])
            red(ls[:, 0:2], lpr[:], ['lpr0', 'lpr1'], ['ls'])
            act(ls[:], ls[:], AF.Exp, ['ls'], ['ls'])
            stt('dve', lam_t[:, 0:1], ls[:, 0:1], LAM_INIT, ls[:, 1:2], ALU.add, ALU.subtract, ['ls'], ['lam_t'])
            ts('dve', lam_t[:, 1:2], lam_t[:, 0:1], -1.0, None, ALU.mult, None, ['lam_t'], ['lam_t'])
            em.barrier()

        def rms_rstd(ss_ap, n, out_ap, key):
            act(out_ap, ss_ap, AF.Sqrt, [key], [key], scale=1.0 / n, bias=epsc[:, 0:1])
            recip(out_ap, out_ap, [key], [key])

        epsc = sb(G, "epsc", [128, 1], F32)
        memset('pool', epsc[:], EPS, ['epsc'])

        def ffn_phase(tag, tiles, wg_d, wu_d, wd_d, nvec_d):
            with ExitStack() as ph:
                Wg = sb(ph, "Wg" + tag, [128, 8, DFF], BF16)
                Wu = sb(ph, "Wu" + tag, [128, 8, DFF], BF16)
                Wd = sb(ph, "Wd" + tag, [128, NFC, D], BF16)
                gbc = sb(ph, "gbc" + tag, [128, D], F32)
                xt = [sb(ph, "xt%d" % i + tag, [128, TB, D], F32) for i in range(2)]
                xs = [sb(ph, "xs%d" % i + tag, [128, D], BF16) for i in range(2)]
                xnT = sb(ph, "xnT" + tag, [128, 8, TB * 128], BF16)
                sg = [sb(ph, "sg%d" % i + tag, [128, TB * 128], F32) for i in range(2)]
                aT = sb(ph, "aT" + tag, [128, NFC, TB * 128], BF16)
                stat = sb(ph, "stat" + tag, [128, 2 * TB], F32)
                dma('sp', gbc[:], nvec_d.partition_broadcast(128), [], ['gbc'])
                for kc in range(8):
                    dma('pool', Wg[:, kc, :], wg_d[kc * 128:(kc + 1) * 128, :], [], ['Wg%d' % kc])
                    dma('pool', Wu[:, kc, :], wu_d[kc * 128:(kc + 1) * 128, :], [], ['Wu%d' % kc])
                for fc in range(NFC):
                    dma('pool', Wd[:, fc, :], wd_d[fc * 128:(fc + 1) * 128, :], [], ['Wd%d' % fc])
                def _hdr(ti):
                    src, dst, nb = tiles[ti]
                    return src, dst, nb, nb * 128, xt[ti % 2], 'xt%d' % (ti % 2)

                def p_dma(ti):
                    src, dst, nb, N, X, xk = _hdr(ti)
                    dma('sp', X[:, 0:nb, :], src.rearrange("(b p) d -> p b d", p=128), [], [xk])

                def p_norm(ti, b):
                    src, dst, nb, N, X, xk = _hdr(ti)
                    if b >= nb:
                        return
                    xsb = xs[b % 2]; xsk = 'xs%d' % (b % 2)
                    memset('pool', stat[:, b:b + 1], 0.0, ['stat%d' % b])
                    act(xsb[:], X[:, b, :], AF.Square, [xk, 'stat%d' % b], [xsk, 'stat%d' % b], accum_out=stat[:, b:b + 1])
                    rms_rstd(stat[:, b:b + 1], D, stat[:, TB + b:TB + b + 1], 'stat%d' % b)
                    stt('dve', xsb[:], X[:, b, :], stat[:, TB + b:TB + b + 1], gbc[:], ALU.mult, ALU.mult,
                        [xk, 'stat%d' % b, 'gbc'], [xsk])

                def p_trans(ti, b):
                    src, dst, nb, N, X, xk = _hdr(ti)
                    if b >= nb:
                        return
                    xsb = xs[b % 2]; xsk = 'xs%d' % (b % 2)
                    for kc in range(8):
                        tr(PBh[0][:, kc * 128:(kc + 1) * 128], xsb[:, kc * 128:(kc + 1) * 128], ident[:], [xsk, 'ident'], ['P0'])
                    if b % 2 == 0:
                        cp('dve', xnT[:, :, b * 128:(b + 1) * 128], V3(PBh[0], 8), ['P0'], ['xnT'])
                    else:
                        act(xnT[:, :, b * 128:(b + 1) * 128], V3(PBh[0], 8), AF.Copy, ['P0'], ['xnT'])

                def gateup(ti):
                    src, dst, nb, N, X, xk = _hdr(ti)
                    for fc in range(NFC):
                        pg = 1 + (fc % 2); pu = 3 + (fc % 2)
                        for kc in range(8):
                            mm(PB[pg][:, 0:N], Wg[:, kc, fc * 128:(fc + 1) * 128], xnT[:, kc, 0:N], kc == 0, kc == 7,
                               ['Wg%d' % kc, 'xnT'], ['P%d' % pg])
                        for kc in range(8):
                            mm(PB[pu][:, 0:N], Wu[:, kc, fc * 128:(fc + 1) * 128], xnT[:, kc, 0:N], kc == 0, kc == 7,
                               ['Wu%d' % kc, 'xnT'], ['P%d' % pu])
                        s = sg[fc % 2]; sk = 'sg%d' % (fc % 2)
                        act(s[:, 0:N], PB[pg][:, 0:N], AF.Silu, ['P%d' % pg], [sk])
                        tt('dve', aT[:, fc, 0:N], s[:, 0:N], PB[pu][:, 0:N], ALU.mult, [sk, 'P%d' % pu], ['aT%d' % fc])

                def down_blk(ti, b):
                    src, dst, nb, N, X, xk = _hdr(ti)
                    if b >= nb:
                        return
                    for nh in range(2):
                        pd = 5 + nh
                        for fc in range(NFC):
                            mm(PB[pd][:, :], aT[:, fc, b * 128:(b + 1) * 128], Wd[:, fc, nh * 512:(nh + 1) * 512], fc == 0, fc == NFC - 1,
                               ['aT%d' % fc, 'Wd%d' % fc], ['P%d' % pd])
                        stt('dve', X[:, b, nh * 512:(nh + 1) * 512], PB[pd][:, :], 0.5, X[:, b, nh * 512:(nh + 1) * 512], ALU.mult, ALU.add,
                            ['P%d' % pd, xk], [xk])

                def down_out(ti):
                    src, dst, nb, N, X, xk = _hdr(ti)
                    dma('act', dst.rearrange("(b p) d -> p b d", p=128), X[:, 0:nb, :], [xk], [])

                p_dma(0)
                for b in range(TB):
                    p_norm(0, b); p_trans(0, b)
                for ti in range(len(tiles)):
                    nxt = ti + 1 if ti + 1 < len(tiles) else None
                    if nxt is not None:
                        p_dma(nxt)
                    gateup(ti)
                    if nxt is not None:
                        p_norm(nxt, 0); p_norm(nxt, 1)
                    down_blk(ti, 0)
                    if nxt is not None:
                        p_trans(nxt, 0); p_trans(nxt, 1)
                        p_norm(nxt, 2); p_norm(nxt, 3)
                    down_blk(ti, 1)
                    if nxt is not None:
                        p_trans(nxt, 2); p_trans(nxt, 3)
                    down_blk(ti, 2); down_blk(ti, 3)
                    down_out(ti)
                em.barrier()

        def tile_list(src, dst, nblk):
            out = []
            b = 0
            while b < nblk:
                nb = min(TB, nblk - b)
                out.append((src[b * 128:(b + nb) * 128, :], dst[b * 128:(b + nb) * 128, :], nb))
                b += nb
            return out

        tilesA = [(x_s, h_s, 1)] + tile_list(x_pre, h_pre, NBP) + tile_list(x_own, h_own, NBO)
        if not SKIP_A:
            ffn_phase("A", tilesA, w_g1, w_u1, w_d1, n_f1)

        with ExitStack() as MB:
          if STAGE >= 2:
            Wout = sb(MB, "Wout", [128, 8, D], BF16)
            for kc in range(8):
                dma('pool', Wout[:, kc, :], w_out[kc * 128:(kc + 1) * 128, :], [], ['Wout%d' % kc])
            omix_s = sb(MB, "omix_s", [128, D], BF16)
            QT_s = sb(MB, "QT_s", [128, 4, 128], BF16)
            KT_s = sb(MB, "KT_s", [128, 4, 128], BF16)
            Vb_s = sb(MB, "Vb_s", [128, 512], BF16)
            OHG = sb(MB, "OHG", [128, NBO, 512], BF16)
            gsub = sb(MB, "gsub", [128, 512], F32)
            csel = sb(MB, "csel", [128, 2], F32)
            dma('sp', csel[:], c_csel, [], ['csel'])
            dma('sp', V3(gsub[:], 4)[:, 0, :], g_sub.partition_broadcast(128), [], ['gsub'])
            ts('dve', V3(gsub[:], 4)[:, 0, :], V3(gsub[:], 4)[:, 0, :], 1.0 - LAM_INIT, None, ALU.mult, None, ['gsub'], ['gsub'])
            for h in range(1, 4):
                cp('dve', V3(gsub[:], 4)[:, h, :], V3(gsub[:], 4)[:, 0, :], ['gsub'], ['gsub'])

            def make_common(st, Wt, wkey):
                C = {}
                gmix = sb(st, "gmix" + wkey, [128, D], F32)
                dma('sp', gmix[:], n_mix.partition_broadcast(128), [], ['gmix'])
                hb = sb(st, "hb" + wkey, [128, D], F32); hs = sb(st, "hs" + wkey, [128, D], BF16)
                hnT = sb(st, "hnT" + wkey, [128, 8, 128], BF16)
                sqt = sb(st, "sqt" + wkey, [128, D], BF16); bst = sb(st, "bst" + wkey, [128, 4], F32)
                nst = sb(st, "nst" + wkey, [128, 16], F32); ntmp = sb(st, "ntmp" + wkey, [128, 512], F32)
                C['hb'] = hb

                def load_norm_block(src_ap):
                    dma('sp', hb[:], src_ap, [], ['hb'])
                    memset('pool', bst[:, 0:1], 0.0, ['bst'])
                    act(sqt[:], hb[:], AF.Square, ['hb', 'bst'], ['sqt', 'bst'], accum_out=bst[:, 0:1])
                    rms_rstd(bst[:, 0:1], D, bst[:, 1:2], 'bst')
                    stt('dve', hs[:], hb[:], bst[:, 1:2], gmix[:], ALU.mult, ALU.mult, ['hb', 'bst', 'gmix'], ['hs'])
                    for kc in range(8):
                        tr(PBh[0][:, kc * 128:(kc + 1) * 128], hs[:, kc * 128:(kc + 1) * 128], ident[:], ['hs', 'ident'], ['P0'])
                    cp('dve', hnT[:], V3(PBh[0], 8), ['P0'], ['hnT'])

                def proj(cg, pbank):
                    for kc in range(8):
                        mm(PB[pbank][:, :], hnT[:, kc, :], Wt[:, kc, cg * 512:(cg + 1) * 512], kc == 0, kc == 7,
                           ['hnT', wkey + '%d' % kc], ['P%d' % pbank])

                def norm_heads(src_ap, nh, dh, gain_ap, out_ap, r, w):
                    act(ntmp[:, 0:nh * dh], src_ap, AF.Square, r, ['nh_tmp'])
                    red(nst[:, 0:nh], ntmp[:, 0:nh * dh].rearrange("p (h d) -> p h d", h=nh), ['nh_tmp'], ['nh_stat'])
                    rms_rstd(nst[:, 0:nh], dh, nst[:, 8:8 + nh], 'nh_stat')
                    s3 = src_ap.rearrange("p (h d) -> p h d", h=nh)
                    g3 = gain_ap.rearrange("p (h d) -> p h d", h=nh)
                    o3 = out_ap.rearrange("p (h d) -> p h d", h=nh)
                    t3 = ntmp[:, 0:nh * dh].rearrange("p (h d) -> p h d", h=nh)
                    tt('dve', t3, s3, nst[:, 8:8 + nh].unsqueeze(2).to_broadcast([128, nh, dh]), ALU.mult, r + ['nh_stat', 'nh_tmp'], ['nh_tmp'])
                    tt('dve', o3, t3, g3, ALU.mult, r + ['nh_tmp'], w)
                C['load'] = load_norm_block; C['proj'] = proj; C['norm_heads'] = norm_heads
                return C

            with ExitStack() as P1:
                Win1 = sb(P1, "Win1", [128, 8, 2048], BF16)
                for kc in range(8):
                    dma('pool', Win1[:, kc, :], w_in[kc * 128:(kc + 1) * 128, 0:2048], [], ['Wa%d' % kc])
                C = make_common(P1, Win1, 'Wa')
                load_norm_block, proj, norm_heads, hb = C['load'], C['proj'], C['norm_heads'], C['hb']
                lbb = sb(P1, "lbb", [128, 512], F32); oml = sb(P1, "oml", [128, 512], F32); ghg = sb(P1, "ghg", [128, 512], F32)
                maskT = sb(P1, "maskT", [128, 512], F32); Up = sb(P1, "Up", [128, 128], F32); Wm = sb(P1, "Wm", [128, 128], F32)
                Sel = sb(P1, "Sel", [128, 4], F32); rowsel = sb(P1, "rowsel", [128, 4], F32)
                for (t, s, k) in ((maskT, c_maskT, 'maskT'), (Up, c_up, 'Up'), (Wm, c_wm, 'Wm'), (Sel, c_sel, 'Sel'), (rowsel, c_rowsel, 'rowsel')):
                    dma('sp', t[:], s, [], [k])
                dma('sp', lbb[:], lb_log[0:1, :].partition_broadcast(128), [], ['lbb'])
                dma('sp', oml[:], lb_log[1:2, :].partition_broadcast(128), [], ['oml'])
                tt('dve', lbb[:], lbb[:], oml[:], ALU.subtract, ['lbb', 'oml'], ['lbb'])
                act(lbb[:], lbb[:], AF.Sigmoid, ['lbb'], ['lbb'])
                ts('dve', oml[:], lbb[:], -1.0, 1.0, ALU.mult, ALU.add, ['lbb'], ['oml'])
                g3_ = V3(ghg[:], 4)
                dma('sp', g3_[:, 0, :], g_hg.partition_broadcast(128), [], ['ghg'])
                for h in range(1, 4):
                    cp('dve', g3_[:, h, :], g3_[:, 0, :], ['ghg'], ['ghg'])
                S = sb(P1, "S", [128, 4, 128], F32)
                memset('pool', S[:], 0.0, ['S0', 'S1', 'S2', 'S3'])
                Sp = [sb(P1, "Sp%d" % c, [128, 4, 128], BF16) for c in range(2)]
                f_t = sb(P1, "f_t", [128, 512], F32); g_t = sb(P1, "g_t", [128, 512], F32); kk = sb(P1, "kk", [128, 512], F32)
                e1 = sb(P1, "e1", [128, 512], F32); e2 = sb(P1, "e2", [128, 512], F32); qs = sb(P1, "qs", [128, 512], F32)
                sx = sb(P1, "sx", [128, 512], F32); xtra = sb(P1, "xtra", [128, 512], F32)
                qtl = sb(P1, "qtl", [128, 512], BF16); ktl = sb(P1, "ktl", [128, 512], BF16)
                khc = [sb(P1, "khat%d" % c, [128, 512], BF16) for c in range(2)]
                vhb = sb(P1, "vhb", [128, 512], BF16)
                qT = sb(P1, "qT", [128, 4, 128], BF16); kT = sb(P1, "kT", [128, 4, 128], BF16)
                qT0 = sb(P1, "qT0", [128, 4, 128], BF16); qT1 = sb(P1, "qT1", [128, 4, 128], BF16)
                memset('pool', qT0[:], 0.0, ['qT0']); memset('pool', qT1[:], 0.0, ['qT1'])
                ecol = sb(P1, "ecol", [128, 4, 4], F32)
                ATm = sb(P1, "ATm", [128, 512], BF16)

                def gate_f():
                    proj(1, 1)
                    act(f_t[:], PB[1][:], AF.Sigmoid, ['P1'], ['f_t'])
                    tt('dve', f_t[:], f_t[:], oml[:], ALU.mult, ['f_t', 'oml'], ['f_t'])
                    tt('dve', f_t[:], f_t[:], lbb[:], ALU.add, ['f_t', 'lbb'], ['f_t'])
                    act(g_t[:], f_t[:], AF.Ln, ['f_t'], ['g_t'])
                    ts('dve', kk[:], f_t[:], -1.0, 1.0, ALU.mult, ALU.add, ['f_t'], ['kk'])

                def hg_out(o_ps_key, o_ps, dst, dkey):
                    proj(3, 1)
                    act(sx[:], PB[1][:], AF.Silu, ['P1'], ['sx'])
                    tt('dve', sx[:], sx[:], ghg[:], ALU.mult, ['sx', 'ghg'], ['sx'])
                    norm_heads(o_ps, 4, 128, sx[:], dst, [o_ps_key, 'sx'], [dkey])

                with ExitStack() as SS:
                    fT = sb(SS, "fT", [128, 4, 128], F32); qTf = sb(SS, "qTf", [128, 4, 128], F32)
                    qTm = [sb(SS, "qTm%d" % j, [128, 4, 128], F32) for j in range(4)]
                    S0t = [sb(SS, "S0t%d" % i, [128, 128], F32) for i in range(2)]
                    Snew = [sb(SS, "Snew%d" % i, [128, 128], F32) for i in range(2)]
                    load_norm_block(h_s)
                    gate_f()
                    proj(0, 2)
                    act(qs[:], PB[2][:], AF.Silu, ['P2'], ['qs'])
                    ts('dve', qs[:], qs[:], 128 ** -0.5, None, ALU.mult, None, ['qs'], ['qs'])
                    proj(2, 3)
                    act(e1[:], PB[3][:], AF.Copy, ['P3'], ['e1'])
                    for h in range(4):
                        tr(PB[4][:, h * 128:(h + 1) * 128], f_t[:, h * 128:(h + 1) * 128], identf[:], ['f_t', 'identf'], ['P4'])
                        tr(PB[5][:, h * 128:(h + 1) * 128], qs[:, h * 128:(h + 1) * 128], identf[:], ['qs', 'identf'], ['P5'])
                    act(fT[:], V3(PB[4][:], 4), AF.Copy, ['P4'], ['fT'])
                    cp('dve', qTf[:], V3(PB[5][:], 4), ['P5'], ['qTf'])
                    for j in range(4):
                        memset('pool', qTm[j][:], 0.0, ['qTm%d' % j])
                        cp('dve', qTm[j][:, :, 32 * j:32 * j + 1], qTf[:, :, 32 * j:32 * j + 1], ['qTf', 'qTm%d' % j], ['qTm%d' % j])
                    kkm = [e2, sx, g_t, xtra]; kkmk = ['e2', 'sx', 'g_t', 'xtra']
                    for j in range(4):
                        ts('dve', kkm[j][:], kk[:], rowsel[:, j:j + 1], None, ALU.mult, None, ['kk', 'rowsel'], [kkmk[j]])
                    for h in range(4):
                        for j in range(4):
                            i2 = (h * 4 + j) % 2
                            dma('sp', S0t[i2][:], state_in[j * 4 + h], [], ['S0t%d' % i2])
                            mm(PB[6][:, 0:128], kkm[j][:, h * 128:(h + 1) * 128], e1[:, h * 128:(h + 1) * 128], True, True,
                               [kkmk[j], 'e1'], ['P6'])
                            stt('dve', Snew[i2][:], S0t[i2][:], fT[:, h, 32 * j:32 * j + 1], PB[6][:, 0:128], ALU.mult, ALU.add,
                                ['S0t%d' % i2, 'fT', 'P6'], ['Snew%d' % i2])
                            dma('act', state_s[j * 4 + h], Snew[i2][:], ['Snew%d' % i2], [])
                            mm(PB[7][:, h * 128:(h + 1) * 128], qTm[j][:, h, :], Snew[i2][:], j == 0, j == 3, ['qTm%d' % j, 'Snew%d' % i2], ['P7'])
                    hg_out('P7', PB[7][:], omix_s[:, 0:512], 'omix_s_hg')
                    em.barrier()

                def hg_block(gb, is_own):
                    li = gb - NBP if is_own else gb
                    load_norm_block((h_own if is_own else h_pre)[li * 128:(li + 1) * 128, :])
                    gate_f()
                    for hh in range(4):
                        if is_own:
                            mm(PB[4][:, hh * 128:(hh + 1) * 128], Up[:], g_t[:, hh * 128:(hh + 1) * 128], True, True, ['Up', 'g_t'], ['P4'])
                        mm(PB[5][:, hh * 128:(hh + 1) * 128], Wm[:], g_t[:, hh * 128:(hh + 1) * 128], True, True, ['Wm', 'g_t'], ['P5'])
                    for h in range(4):
                        mm(PB[6][:, h * 4:(h + 1) * 4], g_t[:, h * 128:(h + 1) * 128], Sel[:], True, True, ['g_t', 'Sel'], ['P6'])
                    act(ecol[:], PB[6][:, 0:16].rearrange("p (h c) -> p h c", h=4), AF.Exp, ['P6'], ['ecol'])
                    act(e2[:], PB[5][:], AF.Exp, ['P5'], ['e2'])
                    for c in range(2):
                        stt('dve', khc[c][:], kk[:], csel[:, c:c + 1], e2[:], ALU.mult, ALU.mult, ['kk', 'e2', 'csel'], ['khat'])
                    proj(2, 3)
                    cp('dve', vhb[:], PB[3][:], ['P3'], ['vhb'])
                    if is_own:
                        act(e1[:], PB[4][:], AF.Exp, ['P4'], ['e1'])
                        act(e2[:], PB[4][:], AF.Exp, ['P4'], ['e2'], scale=-1.0)
                        proj(0, 2)
                        act(qs[:], PB[2][:], AF.Silu, ['P2'], ['qs'])
                        stt('dve', qtl[:], qs[:], 128 ** -0.5, e1[:], ALU.mult, ALU.mult, ['qs', 'e1'], ['qtl'])
                        tt('dve', ktl[:], kk[:], e2[:], ALU.mult, ['kk', 'e2'], ['ktl'])
                        for h in range(4):
                            tr(PBh[0][:, h * 128:(h + 1) * 128], qtl[:, h * 128:(h + 1) * 128], ident[:], ['qtl', 'ident'], ['P0'])
                            tr(PBh[0][:, 512 + h * 128:512 + (h + 1) * 128], ktl[:, h * 128:(h + 1) * 128], ident[:], ['ktl', 'ident'], ['P0'])
                        act(qT[:], V3(PBh[0][:, 0:512], 4), AF.Copy, ['P0'], ['qT'])
                        act(kT[:], V3(PBh[0][:, 512:1024], 4), AF.Copy, ['P0'], ['kT'])
                        cp('dve', qT0[:, :, 0:64], qT[:, :, 0:64], ['qT'], ['qT0'])
                        cp('dve', qT1[:, :, 64:128], qT[:, :, 64:128], ['qT'], ['qT1'])
                        for h in range(4):
                            mm(PB[6][:, h * 128:(h + 1) * 128], kT[:, h, :], qT[:, h, :], True, True, ['kT', 'qT'], ['P6'])
                        tt('dve', ATm[:], PB[6][:], maskT[:], ALU.mult, ['P6', 'maskT'], ['ATm'])
                    SK = ['S0', 'S1', 'S2', 'S3']
                    for c in range(2):
                        pbs = 1 + c
                        for h in range(4):
                            mm(PB[pbs][:, h * 128:(h + 1) * 128], khc[c][:, h * 128:(h + 1) * 128], vhb[:, h * 128:(h + 1) * 128],
                               True, True, ['khat', 'vhb'], ['P%d' % pbs])
                        if is_own:
                            tt('dve', Sp[c][:], S[:], ecol[:, :, 2 * c:2 * c + 1].to_broadcast([128, 4, 128]), ALU.mult,
                               SK + ['ecol'], ['Sp%d_%d' % (c, h) for h in range(4)])
                        tt('dve', S[:], S[:], ecol[:, :, 2 * c + 1:2 * c + 2].to_broadcast([128, 4, 128]), ALU.mult, SK + ['ecol'], SK)
                        tt('dve', S[:], S[:], V3(PB[pbs][:], 4), ALU.add, SK + ['P%d' % pbs], SK)
                    if is_own:
                        for h in range(4):
                            o_h = PB[7][:, h * 128:(h + 1) * 128]
                            mm(o_h, ATm[:, h * 128:(h + 1) * 128], vhb[:, h * 128:(h + 1) * 128], True, False, ['ATm', 'vhb'], ['P7'])
                            mm(o_h, qT0[:, h, :], Sp[0][:, h, :], False, False, ['qT0', 'Sp0_%d' % h], ['P7'])
                            mm(o_h, qT1[:, h, :], Sp[1][:, h, :], False, True, ['qT1', 'Sp1_%d' % h], ['P7'])
                        hg_out('P7', PB[7][:], OHG[:, li, :], 'OHG%d' % li)

                if STAGE >= 3 and not SKIP_P1:
                    for gb in range(NBP):
                        hg_block(gb, False)
                    for h in range(4):
                        ts('dve', S[:, h, :], S[:, h, :], flg[:, 1:2], None, ALU.mult, None, ['S%d' % h, 'flg'], ['S%d' % h])
                    for gb in range(NBP, NKB):
                        hg_block(gb, True)
                    for h in range(4):
                        dma('sp', state_p[h], S[:, h, :], ['S%d' % h], [])
                em.barrier()

            with ExitStack() as P2:
                Win2 = sb(P2, "Win2", [128, 8, 1536], BF16)
                for kc in range(8):
                    dma('pool', Win2[:, kc, :], w_in[kc * 128:(kc + 1) * 128, 2048:3584], [], ['Wb%d' % kc])
                C = make_common(P2, Win2, 'Wb')
                load_norm_block, proj, norm_heads, hb = C['load'], C['proj'], C['norm_heads'], C['hb']
                KT = sb(P2, "KT", [128, 4, NKB * 128], BF16)
                Vb = sb(P2, "Vb", [128, NKB, 4, 132], BF16)
                memset('pool', Vb[:, :, :, 128:132], 1.0, ['Vb_ones'])
                gqb = sb(P2, "gqb", [128, 512], F32); gkb = sb(P2, "gkb", [128, 512], F32)
                EBd = sb(P2, "EBd", [128, 512], F32)
                cbo = sb(P2, "cbo", [128, 4 * (NKB + 1)], F32); cbp = sb(P2, "cbp", [128, 4 * (NKB + 1)], F32)
                for (t, s, k) in ((EBd, c_ebdiag, 'EBd'), (cbo, c_cb, 'cbo')):
                    dma('sp', t[:], s, [], [k])
                ts('dve', cbp[:], cbo[:], flg[:, 0:1], None, ALU.add, None, ['cbo', 'flg'], ['cbp'])
                for (t, s, k) in ((gqb, g_q, 'gqb'), (gkb, g_k, 'gkb')):
                    t3 = t[:].rearrange("p (h d) -> p h d", h=8)
                    dma('sp', t3[:, 0, :], s.partition_broadcast(128), [], [k])
                    for h in range(1, 8):
                        cp('dve', t3[:, h, :], t3[:, 0, :], [k], [k])
                kout = sb(P2, "kout", [128, 512], F32); vout = sb(P2, "vout", [128, 512], F32)
                qn = sb(P2, "qn", [128, 512], BF16); knb = sb(P2, "knb", [128, 512], BF16)
                QT = sb(P2, "QT", [128, 4, 128], BF16)
                QTz = [sb(P2, "QTz%d" % m, [128, 4, 128], BF16) for m in range(2)]
                Eb = [sb(P2, "Eb%d" % i, [128, 256], F32) for i in range(2)]
                Pb = [sb(P2, "Pb%d" % i, [128, 256], BF16) for i in range(2)]
                ofin = sb(P2, "ofin", [128, 512], F32); rz = sb(P2, "rz", [128, 4], F32)
                odab = sb(P2, "odab", [128, 512], BF16); omT = sb(P2, "omT", [128, 8, 128], BF16)

                def da_kv(kdst, vdst, KT_dst, ktkey, Vb_dst, vkey, vb3):
                    proj(1, 2)
                    norm_heads(PB[2][:], 8, 64, gkb[:], kout[:], ['P2', 'gkb'], ['kout'])
                    if kdst is not None:
                        dma('act', kdst, kout[:], ['kout'], [])
                    cp('dve', knb[:], kout[:], ['kout'], ['knb'])
                    for h in range(4):
                        tr(PBh[0][:, h * 128:(h + 1) * 128], knb[:, h * 128:(h + 1) * 128], ident[:], ['knb', 'ident'], ['P0'])
                    act(KT_dst, V3(PBh[0][:, 0:512], 4), AF.Copy, ['P0'], [ktkey])
                    proj(2, 3)
                    act(vout[:], PB[3][:], AF.Copy, ['P3'], ['vout'])
                    if vdst is not None:
                        dma('act', vdst, vout[:], ['vout'], [])
                    cp('dve', Vb_dst, V3(vout[:], 4) if vb3 else vout[:], ['vout', 'Vb_ones'], [vkey])

                def q_T(dst, dkey):
                    proj(0, 2)
                    norm_heads(PB[2][:], 8, 64, gqb[:], qn[:], ['P2', 'gqb'], ['qn'])
                    for h in range(4):
                        tr(PBh[0][:, h * 128:(h + 1) * 128], qn[:, h * 128:(h + 1) * 128], ident[:], ['qn', 'ident'], ['P0'])
                    act(dst, V3(PBh[0][:, 0:512], 4), AF.Copy, ['P0'], [dkey])

                load_norm_block(h_s)
                da_kv(k_s, v_s, KT_s[:], 'KT_s', Vb_s[:], 'Vb_s', False)
                q_T(QT_s[:], 'QT_s')

                def da_block(gb, is_own):
                    li = gb - NBP if is_own else gb
                    load_norm_block((h_own if is_own else h_pre)[li * 128:(li + 1) * 128, :])
                    da_kv(k_own[li * 128:(li + 1) * 128, :] if is_own else None, v_own[li * 128:(li + 1) * 128, :] if is_own else None,
                          KT[:, :, gb * 128:(gb + 1) * 128], 'KT%d' % gb, Vb[:, gb, :, 0:128], 'Vb%d' % gb, True)
                    if not is_own or DA_SUB < 3:
                        return
                    q_T(QT[:], 'QT')
                    if DA_SUB < 4:
                        return
                    for m in range(2):
                        ts('dve', QTz[m][:], QT[:], csel[:, m:m + 1], None, ALU.mult, None, ['QT', 'csel'], ['QTz%d' % m])
                    cnt = 0
                    for h in range(4):
                        po = 3 + 2 * (h % 2)
                        Om = [PB[po][:, 0:132], PB[po + 1][:, 0:132]]
                        pok = ['P%d' % po, 'P%d' % (po + 1)]
                        kb_lo = max(0, gb - DLMAX[h])
                        for kb in range(kb_lo, gb + 1):
                            i2 = cnt % 2; cnt += 1
                            pst = 1 + i2
                            for m in range(2):
                                mm(PB[pst][:, m * 128:(m + 1) * 128], KT[:, h, kb * 128:(kb + 1) * 128], QTz[m][:, h, :],
                                   True, True, ['KT%d' % kb, 'QTz%d' % m], ['P%d' % pst])
                            dl = gb - kb
                            cbt = cbp if kb < NBP else cbo
                            bcol = cbt[:, h * (NKB + 1) + dl:h * (NKB + 1) + dl + 1]
                            if dl == 0:
                                act(Eb[i2][:], PB[pst][:, 0:256], AF.Exp, ['P%d' % pst, 'cbp', 'cbo'], ['Eb%d' % i2], scale=DA_SCALE, bias=bcol)
                                for m in range(2):
                                    tt('dve', Pb[i2][:, m * 128:(m + 1) * 128], Eb[i2][:, m * 128:(m + 1) * 128],
                                       EBd[:, 0:128], ALU.mult, ['Eb%d' % i2, 'EBd'], ['Pb%d' % i2])
                            else:
                                act(Pb[i2][:], PB[pst][:, 0:256], AF.Exp, ['P%d' % pst, 'cbp', 'cbo'], ['Pb%d' % i2], scale=DA_SCALE, bias=bcol)
                            for m in range(2):
                                mm(Om[m], Pb[i2][:, m * 128:(m + 1) * 128], Vb[:, kb, h, 0:132], kb == kb_lo, kb == gb,
                                   ['Pb%d' % i2, 'Vb%d' % kb, 'Vb_ones'], [pok[m]])
                        recip(rz[:, 0:1], Om[0][:, 128:129], [pok[0]], ['rz'])
                        recip(rz[:, 1:2], Om[1][:, 128:129], [pok[1], 'rz'], ['rz'])
                        tt('dve', rz[:, 1:2], rz[:, 1:2], lam_t[:, 1:2], ALU.mult, ['rz', 'lam_t'], ['rz'])
                        ts('dve', ofin[:, h * 128:(h + 1) * 128], Om[0][:, 0:128], rz[:, 0:1], None, ALU.mult, None, [pok[0], 'rz'], ['ofin'])
                        stt('dve', ofin[:, h * 128:(h + 1) * 128], Om[1][:, 0:128], rz[:, 1:2], ofin[:, h * 128:(h + 1) * 128], ALU.mult, ALU.add,
                            [pok[1], 'rz', 'ofin'], ['ofin'])
                    norm_heads(ofin[:], 4, 128, gsub[:], odab[:], ['ofin', 'gsub'], ['odab'])
                    if DA_SUB < 5:
                        return
                    for kc in range(8):
                        src = OHG[:, li, kc * 128:(kc + 1) * 128] if kc < 4 else odab[:, (kc - 4) * 128:(kc - 3) * 128]
                        tr(PBh[0][:, kc * 128:(kc + 1) * 128], src, ident[:], ['OHG%d' % li, 'odab', 'ident'], ['P0'])
                    cp('dve', omT[:], V3(PBh[0], 8), ['P0'], ['omT'])
                    for nh in range(2):
                        pd = 5 + nh
                        for kc in range(8):
                            mm(PB[pd][:, :], omT[:, kc, :], Wout[:, kc, nh * 512:(nh + 1) * 512], kc == 0, kc == 7,
                               ['omT', 'Wout%d' % kc], ['P%d' % pd])
                        tt('dve', hb[:, nh * 512:(nh + 1) * 512], hb[:, nh * 512:(nh + 1) * 512], PB[pd][:, :], ALU.add, ['hb', 'P%d' % pd], ['hb'])
                    dma('act', h2_own[li * 128:(li + 1) * 128, :], hb[:], ['hb'], [])

                if STAGE >= 4 and DA_SUB >= 2:
                    for gb in range(NKB):
                        da_block(gb, gb >= NBP)
                em.barrier()

            with ExitStack() as B2:
                biasS = sb(B2, "biasS", [128, NG * 128], F32); selfb = sb(B2, "selfb", [128, 4], F32)
                dma('sp', biasS[:], c_biasS, [], ['biasS']); dma('sp', selfb[:], c_selfb, [], ['selfb'])
                hb_s = sb(B2, "hb_s", [128, D], F32)
                dma('sp', hb_s[:], h_s, [], ['hb_s'])
                pti = sb(B2, "pti", [128, 4 * NG], I32); cfi = sb(B2, "cfi", [128, 1], I32)
                ptf = sb(B2, "ptf", [128, 4 * NG], F32); cff = sb(B2, "cff", [128, 1], F32); idx = sb(B2, "idx", [128, 4 * NG], I32)
                dma('sp', pti[:], ptrep, [], ['pti']); dma('sp', cfi[:], c_coff, [], ['cfi'])
                cp('dve', ptf[:], pti[:], ['pti'], ['ptf']); cp('dve', cff[:], cfi[:], ['cfi'], ['cff'])
                ts('dve', ptf[:], ptf[:], 128.0, cff[:, 0:1], ALU.mult, ALU.add, ['ptf', 'cff'], ['ptf'])
                cp('dve', idx[:], ptf[:], ['ptf'], ['idx'])
                idxc = []
                for c_ in range(4 * NG):
                    t_ = sb(B2, "idxc%d" % c_, [128, 1], I32)
                    cp('dve', t_[:], idx[:, c_:c_ + 1], ['idx'], ['idxc'])
                    idxc.append(t_)
                Kg = [sb(B2, "Kg%d" % i, [128, 16, 512], BF16) for i in range(2)]
                Vg = [sb(B2, "Vg%d" % i, [128, 16, 512], BF16) for i in range(2)]
                KTt = [sb(B2, "KTt%d" % i, [128, 4, 128], BF16) for i in range(2)]
                qbd = sb(B2, "qbd", [128, 4, 2], BF16)
                E = sb(B2, "E", [128, NT * 8], F32); tmpS = sb(B2, "tmpS", [128, 128], F32)
                A = sb(B2, "A", [128, NT, 4], BF16); Af = sb(B2, "Af", [128, NT], F32)
                zp = sb(B2, "zp", [128, 8], F32); zc = sb(B2, "zc", [128, 16], F32)
                pv = sb(B2, "pv", [4, 512], F32)
                oda = sb(B2, "oda", [128, 512], F32)
                nst2 = sb(B2, "nst2", [128, 16], F32); ntmp2 = sb(B2, "ntmp2", [128, 512], F32)
                omT2 = sb(B2, "omT2", [128, 8, 128], BF16)
                memset('pool', oda[:], 0.0, ['oda'])
                gcount = 0
                for j in range(4 if STAGE >= 5 else 0):
                    for m in range(2):
                        ts('dve', qbd[:, :, m:m + 1], QT_s[:, :, 32 * j:32 * j + 1], csel[:, m:m + 1], None, ALU.mult, None, ['QT_s', 'csel'], ['qbd'])
                    E3 = E[:].rearrange("p (t n) -> p t n", n=8)

                    def v_gather(g):
                        em.dma('pool', (lambda V_, c_: (lambda e: e.indirect_dma_start(
                            out=V_, out_offset=None, in_=cache_v,
                            in_offset=bass.IndirectOffsetOnAxis(ap=idxc[c_][:, :], axis=0))))(Vg[g % 2][:].rearrange("p a b -> p (a b)"), j * NG + g),
                            ['idxc'], ['Vg%d' % (g % 2)])
                    NPRE = min(2, NG)
                    for g in range(NPRE):
                        v_gather(g)
                    for g in range(NG):
                        gi = gcount % 2; gcount += 1
                        col = j * NG + g
                        em.dma('pool', (lambda K_, c_: (lambda e: e.indirect_dma_start(
                            out=K_, out_offset=None, in_=cache_k,
                            in_offset=bass.IndirectOffsetOnAxis(ap=idxc[c_][:, :], axis=0))))(Kg[gi][:].rearrange("p a b -> p (a b)"), col),
                            ['idxc'], ['Kg%d' % gi])
                        for i in range(16):
                            t2 = i % 2
                            for h in range(4):
                                tr(PBh[t2][:, h * 128:(h + 1) * 128], Kg[gi][:, i, h * 128:(h + 1) * 128], ident[:], ['Kg%d' % gi, 'ident'], ['P%d' % t2])
                            if i % 2 == 0:
                                act(KTt[t2][:], V3(PBh[t2][:, 0:512], 4), AF.Copy, ['P%d' % t2], ['KTt%d' % t2])
                            else:
                                cp('dve', KTt[t2][:], V3(PBh[t2][:, 0:512], 4), ['P%d' % t2], ['KTt%d' % t2])
                            for h in range(4):
                                mm(PB[2][:, i * 8 + 2 * h:i * 8 + 2 * h + 2], KTt[t2][:, h, :], qbd[:, h, :], True, True, ['KTt%d' % t2, 'qbd'], ['P2'])
                        stt('dve', tmpS[:], PB[2][:, 0:128], DA_SCALE, biasS[:, g * 128:(g + 1) * 128], ALU.mult, ALU.add, ['P2', 'biasS'], ['tmpS'])
                        act(E[:, g * 128:(g + 1) * 128], tmpS[:], AF.Exp, ['tmpS'], ['E'])
                    for h in range(4):
                        mm(PB[2][:, 2 * h:2 * h + 2], KT_s[:, h, :], qbd[:, h, :], True, True, ['KT_s', 'qbd'], ['P2'])
                    ts('dve', tmpS[:, 0:8], PB[2][:, 0:8], DA_SCALE, selfb[:, j:j + 1], ALU.mult, ALU.add, ['P2', 'selfb'], ['tmpS'])
                    act(E[:, NPG * 8:NPG * 8 + 8], tmpS[:, 0:8], AF.Exp, ['tmpS'], ['E'])
                    red(zp[:], E[:].rearrange("p (t n) -> p n t", n=8), ['E'], ['zp'])
                    mm(PB[3][:, 0:8], onesf[:], zp[:], True, True, ['onesf', 'zp'], ['P3'])
                    recip(zc[:, 0:8], PB[3][:, 0:8], ['P3'], ['zc'])
                    for h in range(4):
                        tt('dve', zc[:, 8 + h:9 + h], zc[:, 2 * h + 1:2 * h + 2], lam_t[:, 1:2], ALU.mult, ['zc', 'lam_t'], ['zc'])
                        ts('dve', Af[:], E3[:, :, 2 * h], zc[:, 2 * h:2 * h + 1], None, ALU.mult, None, ['E', 'zc'], ['Af'])
                        stt('dve', A[:, :, h], E3[:, :, 2 * h + 1], zc[:, 8 + h:9 + h], Af[:], ALU.mult, ALU.add, ['E', 'zc', 'Af'], ['A'])
                    for g in range(NG):
                        gi = g % 2
                        if g >= NPRE:
                            v_gather(g)
                        for i in range(16):
                            mm(PB[4][0:4, :], A[:, g * 16 + i, :], Vg[gi][:, i, :], g == 0 and i == 0, False, ['A', 'Vg%d' % gi], ['P4'])
                    mm(PB[4][0:4, :], A[:, NPG, :], Vb_s[:], False, True, ['A', 'Vb_s'], ['P4'])
                    act(pv[:], PB[4][0:4, :], AF.Copy, ['P4'], ['pv'])
                    for h in range(4):
                        dma('sp', oda[32 * j:32 * j + 1, h * 128:(h + 1) * 128], pv[h:h + 1, h * 128:(h + 1) * 128], ['pv', 'oda'], ['oda'])
                act(ntmp2[:], oda[:], AF.Square, ['oda'], ['nh2'])
                red(nst2[:, 0:4], V3(ntmp2[:], 4), ['nh2'], ['nst2'])
                rms_rstd(nst2[:, 0:4], 128, nst2[:, 8:12], 'nst2')
                for h in range(4):
                    stt('dve', omix_s[:, 512 + h * 128:512 + (h + 1) * 128], oda[:, h * 128:(h + 1) * 128], nst2[:, 8 + h:9 + h], gsub[:, h * 128:(h + 1) * 128],
                        ALU.mult, ALU.mult, ['oda', 'nst2', 'gsub'], ['omix_s_da'])
                for kc in range(8):
                    tr(PBh[0][:, kc * 128:(kc + 1) * 128], omix_s[:, kc * 128:(kc + 1) * 128], ident[:], ['omix_s_da', 'omix_s_hg', 'ident'], ['P0'])
                cp('dve', omT2[:], V3(PBh[0], 8), ['P0'], ['omT2'])
                for nh in range(2):
                    pd = 5 + nh
                    for kc in range(8):
                        mm(PB[pd][:, :], omT2[:, kc, :], Wout[:, kc, nh * 512:(nh + 1) * 512], kc == 0, kc == 7, ['omT2', 'Wout%d' % kc], ['P%d' % pd])
                    tt('dve', hb_s[:, nh * 512:(nh + 1) * 512], hb_s[:, nh * 512:(nh + 1) * 512], PB[pd][:, :], ALU.add, ['hb_s', 'P%d' % pd], ['hb_s'])
                dma('act', h2_s, hb_s[:], ['hb_s'], [])
                em.barrier()

        tilesC = [(h2_s, y_s, 1)] + tile_list(h2_own, y_own, NBO)
        if STAGE >= 6:
            ffn_phase("C", tilesC, w_g2, w_u2, w_d2, n_f2)

        with nc.Block() as block:
            em.finish(block)
        nc._em_names = em.names
    return nc


def make_consts(NBO, NBP, NPG, past_len):
    NKB = NBO + NBP
    NG = NPG // 16
    c = {}
    c["c_ident"] = np.eye(128, dtype=np.float32)
    s = np.arange(128)[:, None]; t = np.arange(128)[None, :]
    same = (s // 64) == (t // 64)
    c["c_maskT"] = np.tile((same & (s <= t)).astype(np.float32), (1, 4))
    mid = (t // 64) * 64 + 31
    c["c_up"] = (same * ((s <= t).astype(np.float32) - (s <= mid).astype(np.float32))).astype(np.float32)
    c["c_wm"] = (same & (s > t)).astype(np.float32)
    sel = np.zeros((128, 4), np.float32)
    sel[0:32, 0] = 1; sel[0:64, 1] = 1; sel[64:96, 2] = 1; sel[64:128, 3] = 1
    c["c_sel"] = sel
    ki = np.arange(128)[:, None].astype(np.float64); qi = np.arange(128)[None, :].astype(np.float64)
    ebo = np.zeros((128, 4, 128), np.float32); ebd = np.zeros((128, 4, 128), np.float32)
    for h in range(4):
        ebo[:, h, :] = 1.0
        ebd[:, h, :] = (qi >= ki)
    c["c_eboff"] = ebo.reshape(128, 512); c["c_ebdiag"] = ebd.reshape(128, 512)
    cb = np.zeros((128, 4, NKB + 1), np.float32)
    for h in range(4):
        cb[:, h, :] = -SLOPES[h] * 128.0 * np.arange(NKB + 1)[None, :] + SLOPES[h] * np.arange(128)[:, None]
    c["c_cb"] = cb.reshape(128, -1)
    p = np.arange(128)[:, None, None, None]; g = np.arange(NG)[None, :, None, None]
    i = np.arange(16)[None, None, :, None]; n = np.arange(8)[None, None, None, :]
    pos = g * 2048 + 16 * p + i
    sl = np.array(SLOPES)[n // 2]
    c["c_biasS"] = (-(sl * (past_len - pos))).astype(np.float32).reshape(128, NG * 128)
    sb_ = np.full((128, 4), NEG, np.float32)
    for j in range(4):
        sb_[32 * j, j] = 0.0
    c["c_selfb"] = sb_
    c["c_rowsel"] = (sb_ == 0.0).astype(np.float32)
    cs = np.zeros((128, 2), np.float32); cs[0:64, 0] = 1.0; cs[64:128, 1] = 1.0
    c["c_csel"] = cs
    c["c_coff"] = ((np.arange(128) % 8) * 16).astype(np.int32).reshape(128, 1)
    return c


def run(inputs, n_cores=8, debug=False, trace=False):
    xp = np.asarray(inputs["x_prompt"]); xs = np.asarray(inputs["x_sample"])
    B, L, _ = xp.shape
    DB = xs.shape[0]
    assert n_cores == 2 * B and DB == 4 * n_cores
    half = L // 2
    NBO = NBP = half // 128
    pt = np.asarray(inputs["page_table"])
    NPG = pt.shape[1]
    ck = np.asarray(inputs["cache_k"]); cv = np.asarray(inputs["cache_v"])
    NPOOL = ck.shape[1]
    past_len = NPG * 128
    NG = NPG // 16
    nc = build_program(NBO, NBP, NPG, NPOOL, debug=debug)
    consts = make_consts(NBO, NBP, NPG, past_len)
    ck2 = np.ascontiguousarray(ck[0].reshape(NPOOL * 128, 512)); cv2 = np.ascontiguousarray(cv[0].reshape(NPOOL * 128, 512))
    shared = {
        "cache_k": ck2, "cache_v": cv2,
        "w_g1": np.asarray(inputs["ffn1_w_gate"])[0], "w_u1": np.asarray(inputs["ffn1_w_up"])[0], "w_d1": np.asarray(inputs["ffn1_w_down"])[0],
        "w_g2": np.asarray(inputs["ffn2_w_gate"])[0], "w_u2": np.asarray(inputs["ffn2_w_up"])[0], "w_d2": np.asarray(inputs["ffn2_w_down"])[0],
        "w_in": np.asarray(inputs["w_in"])[0], "w_out": np.asarray(inputs["w_out"])[0],
        "n_f1": np.asarray(inputs["ffn1_norm"]), "n_mix": np.asarray(inputs["mix_norm"]), "n_f2": np.asarray(inputs["ffn2_norm"]),
        "lb_log": np.asarray(inputs["hg_lb_logits"]),
        "g_hg": np.asarray(inputs["hg_out_norm"]), "g_q": np.asarray(inputs["da_q_norm"]), "g_k": np.asarray(inputs["da_k_norm"]),
        "g_sub": np.asarray(inputs["da_subln"]),
        "lam_p": np.concatenate([np.asarray(inputs[k]) for k in ("da_lambda_q1", "da_lambda_k1", "da_lambda_q2", "da_lambda_k2")], axis=0),
    }
    shared.update(consts)
    st = np.asarray(inputs["state_hgrn"])[0]
    in_maps = []
    for c in range(n_cores):
        b, hf = c // 2, c % 2
        m = dict(shared)
        m["x_own"] = np.ascontiguousarray(xp[b, hf * half:(hf + 1) * half, :])
        m["x_pre"] = np.ascontiguousarray(xp[b, 0:half, :]) if hf == 1 else np.zeros((half, D), np.float32)
        xsb = np.zeros((128, D), np.float32)
        for j in range(4):
            xsb[32 * j] = xs[4 * c + j, 0]
        m["x_s"] = xsb
        m["state_in"] = np.ascontiguousarray(st[4 * c:4 * c + 4].reshape(16, 128, 128))
        pr = np.zeros((128, 4 * NG), np.int32)
        for j in range(4):
            for g in range(NG):
                pr[:, j * NG + g] = np.repeat(pt[4 * c + j, g * 16:(g + 1) * 16], 8)
        m["ptrep"] = pr
        fl = np.zeros((128, 2), np.float32)
        fl[:, 0] = 0.0 if hf == 1 else NEG
        fl[:, 1] = 1.0 if hf == 1 else 0.0
        m["flags"] = fl
        in_maps.append(m)
    res = run_bass_kernel_spmd(nc, in_maps, core_ids=list(range(n_cores)), **({"trace": True} if trace else {}))
    R = res.results
    y_p = np.zeros((B, L, D), np.float32); k_p = np.zeros((1, B, L, 8, 64), np.float32); v_p = np.zeros((1, B, L, 4, 128), np.float32)
    s_p = np.zeros((1, B, 4, 128, 128), np.float32)
    y_s = np.zeros((DB, 1, D), np.float32); k_s = np.zeros((1, DB, 1, 8, 64), np.float32); v_s = np.zeros((1, DB, 1, 4, 128), np.float32)
    s_s = np.zeros((1, DB, 4, 128, 128), np.float32)
    for c in range(n_cores):
        b, hf = c // 2, c % 2
        r = R[c]
        sl = slice(hf * half, (hf + 1) * half)
        y_p[b, sl] = r["y_own"]; k_p[0, b, sl] = r["k_own"].reshape(half, 8, 64); v_p[0, b, sl] = r["v_own"].reshape(half, 4, 128)
        if hf == 1:
            s_p[0, b] = r["state_p"]
        for j in range(4):
            y_s[4 * c + j, 0] = r["y_s"][32 * j]
            k_s[0, 4 * c + j, 0] = r["k_s"][32 * j].reshape(8, 64)
            v_s[0, 4 * c + j, 0] = r["v_s"][32 * j].reshape(4, 128)
        s_s[0, 4 * c:4 * c + 4] = r["state_s"].reshape(4, 4, 128, 128)
    outs = (y_p, y_s, k_p, v_p, s_p, k_s, v_s, s_s)
    if debug:
        return outs, R
    return outs


def kernel(**inputs):
    return run(inputs)
```
